# Optimizing a Trainium2 kernel written in Bass

```python
import jax, jax.numpy as jnp
from jax import lax
import numpy as np

D_MODEL = 1024
BATCH = 4
SEQ = 8192
DEPTH = 1
DEC_BATCH = 32
DEC_SEQ = 1
PAST_LEN = 16384
PAGE_SIZE = 128

N_HEADS = 8
HEAD_DIM = 64
N_KV = 2
GROUP = N_HEADS // N_KV
CMP_LEN = 32
CMP_STRIDE = 16
CMP_SUB = CMP_LEN // CMP_STRIDE
CMP_HIDDEN = 128
SEL_BLOCK = 64
CMP_RATIO = SEL_BLOCK // CMP_STRIDE
CMP_OVER = CMP_SUB - 1
SEL_TOP = 16
WINDOW = 512
Q_BLOCK = 128
FORCE_SCORE = 1e4
D_B = 512
N_GROUPS_B = 4
GROUP_CH = D_B // N_GROUPS_B
CHUNK = 128
D_FF = 2816
ALPHA = (2.0 * DEPTH) ** 0.25
BETA = (8.0 * DEPTH) ** -0.25
LN_EPS = 1e-5
NEG_INF = -1e30
Q_W = N_HEADS * HEAD_DIM
KV_W = N_KV * HEAD_DIM
GATE_W = N_HEADS * 3
IN_W = Q_W + 6 * KV_W + GATE_W + 2 * D_B + 2 * D_MODEL

kernel_name = 'nsa_gmlp_macaron_deepnorm_step'


def layer_norm(x, g, b):
    xf = x.astype(jnp.float32)
    mu = jnp.mean(xf, axis=-1, keepdims=True)
    var = jnp.mean(jnp.square(xf - mu), axis=-1, keepdims=True)
    return ((xf - mu) * lax.rsqrt(var + LN_EPS) * g + b).astype(x.dtype)


def swiglu(x, w_in, w_out):
    gate, up = jnp.split(x @ w_in, 2, axis=-1)
    return (jax.nn.silu(gate) * up) @ w_out


def half_ffn_block(x, w_in, w_out, g, b):
    return layer_norm(ALPHA * x + 0.5 * swiglu(x, w_in, w_out), g, b)


def alibi_slopes():
    h = jnp.arange(1, N_HEADS + 1, dtype=jnp.float32)
    return jnp.exp2(-8.0 * h / N_HEADS).reshape(N_KV, GROUP)


def masked_softmax(s, mask):
    s = jnp.where(mask, s.astype(jnp.float32), NEG_INF)
    return jnp.where(mask, jax.nn.softmax(s, axis=-1), 0.0)


def mixer_inputs(h, w_in):
    B, T = h.shape[0], h.shape[1]
    sizes = (Q_W,) + (KV_W,) * 6 + (GATE_W, D_B, D_B, D_MODEL, D_MODEL)
    cuts = np.cumsum(sizes)[:-1].tolist()
    q, kc, vc, ks, vs, kw, vw, g, u, v, ga, gb = jnp.split(h @ w_in, cuts, axis=-1)
    kv = lambda a: a.reshape(B, T, N_KV, HEAD_DIM)
    q = q.reshape(B, T, N_KV, GROUP, HEAD_DIM) * (HEAD_DIM ** -0.5)
    gates = jax.nn.sigmoid(g.reshape(B, T, N_KV, GROUP, 3))
    return (q, gates, kv(kc), kv(vc), kv(ks), kv(vs), kv(kw), kv(vw),
            jax.nn.gelu(u), jax.nn.gelu(v), ga, gb)


def compress(k, pe, w1, w2):
    B, T = k.shape[0], k.shape[1]
    ns = T // CMP_STRIDE
    nc = ns - CMP_SUB + 1
    sub = k.reshape(B, ns, CMP_STRIDE, N_KV, HEAD_DIM).transpose(0, 1, 3, 2, 4)
    sub = sub.reshape(B, ns, N_KV, CMP_STRIDE * HEAD_DIM)
    w1p = w1.reshape(CMP_SUB, CMP_STRIDE * HEAD_DIM, CMP_HIDDEN)
    hid = sub[:, :nc] @ w1p[0]
    for i in range(1, CMP_SUB):
        hid = hid + sub[:, i:i + nc] @ w1p[i]
    hid = hid + pe.reshape(-1) @ w1
    return jax.nn.gelu(hid) @ w2


def cmp_end_positions(nc):
    return jnp.arange(nc, dtype=jnp.int32) * CMP_STRIDE + (CMP_LEN - 1)


def sel_blocks(k):
    B, T = k.shape[0], k.shape[1]
    return k.reshape(B, T // SEL_BLOCK, SEL_BLOCK, N_KV, HEAD_DIM).transpose(0, 3, 1, 2, 4)


def block_importance(imp, nsel):
    nc = imp.shape[-1]
    total = CMP_RATIO * nsel + CMP_OVER
    pad = [(0, 0)] * (imp.ndim - 1) + [(CMP_OVER, total - CMP_OVER - nc)]
    P = jnp.pad(imp, pad)
    out = P[..., 0:CMP_RATIO * nsel:CMP_RATIO]
    for o in range(1, CMP_RATIO + CMP_OVER):
        out = out + P[..., o:o + CMP_RATIO * nsel:CMP_RATIO]
    return out


def nsa_block(q, gates, qpos, kc, vc, c_end, kslb, vslb, kw, vw, wpos, slopes):
    B, Q = q.shape[0], q.shape[1]
    nsel = kslb.shape[2]
    dist_c = qpos[:, None] - c_end[None, :]
    s_c = jnp.einsum('bqgrd,bcgd->bqgrc', q, kc) - slopes[:, :, None] * dist_c[:, None, None, :]
    p_c = masked_softmax(s_c, (dist_c >= 0)[:, None, None, :])
    o_c = jnp.einsum('bqgrc,bcgd->bqgrd', p_c, vc)
    p_slc = block_importance(jnp.sum(p_c, axis=3), nsel)
    j = jnp.arange(nsel, dtype=jnp.int32)[None, :]
    cur = (qpos // SEL_BLOCK)[:, None]
    forced = (j == 0) | (j == cur) | (j == cur - 1)
    valid = (j * SEL_BLOCK) <= qpos[:, None]
    score = jnp.where(valid[None, :, None, :],
                      jnp.where(forced[None, :, None, :], FORCE_SCORE, p_slc), NEG_INF)
    _, idx = lax.top_k(score, min(SEL_TOP, nsel))
    bi = jnp.arange(B)[:, None, None, None]
    gi = jnp.arange(N_KV)[None, None, :, None]
    ks = kslb[bi, gi, idx]
    vs = vslb[bi, gi, idx]
    spos = idx[..., None] * SEL_BLOCK + jnp.arange(SEL_BLOCK, dtype=jnp.int32)
    dist_s = qpos[None, :, None, None, None] - spos
    s_s = jnp.einsum('bqgrd,bqgkjd->bqgrkj', q, ks) - slopes[None, None, :, :, None, None] * dist_s[:, :, :, None]
    mask_s = jnp.broadcast_to((dist_s >= 0)[:, :, :, None], s_s.shape)
    n_keys = idx.shape[-1] * SEL_BLOCK
    p_s = masked_softmax(s_s.reshape(B, Q, N_KV, GROUP, n_keys), mask_s.reshape(B, Q, N_KV, GROUP, n_keys))
    o_s = jnp.einsum('bqgrn,bqgnd->bqgrd', p_s, vs.reshape(B, Q, N_KV, n_keys, HEAD_DIM))
    dist_w = qpos[:, None] - wpos[None, :]
    mask_w = (dist_w >= 0) & (dist_w < WINDOW) & (wpos >= 0)[None, :]
    s_w = jnp.einsum('bqgrd,bwgd->bqgrw', q, kw) - slopes[:, :, None] * dist_w[:, None, None, :]
    p_w = masked_softmax(s_w, mask_w[:, None, None, :])
    o_w = jnp.einsum('bqgrw,bwgd->bqgrd', p_w, vw)
    o = gates[..., 0:1] * o_c + gates[..., 1:2] * o_s + gates[..., 2:3] * o_w
    return o.astype(q.dtype)


def nsa_prompt(q, gates, kcmp, vcmp, ksel, vsel, kwin, vwin, pe_k, w1_k, w2_k, pe_v, w1_v, w2_v, slopes):
    B, T = q.shape[0], q.shape[1]
    kc = compress(kcmp, pe_k, w1_k, w2_k)
    vc = compress(vcmp, pe_v, w1_v, w2_v)
    c_end = cmp_end_positions(kc.shape[1])
    kslb, vslb = sel_blocks(ksel), sel_blocks(vsel)
    pad = ((0, 0), (WINDOW, 0), (0, 0), (0, 0))
    kw_pad, vw_pad = jnp.pad(kwin, pad), jnp.pad(vwin, pad)
    nqb = T // Q_BLOCK
    qs = q.reshape(B, nqb, Q_BLOCK, N_KV, GROUP, HEAD_DIM).swapaxes(0, 1)
    gs = gates.reshape(B, nqb, Q_BLOCK, N_KV, GROUP, 3).swapaxes(0, 1)
    starts = jnp.arange(nqb, dtype=jnp.int32) * Q_BLOCK

    def one_block(args):
        qb, gb, s0 = args
        qpos = s0 + jnp.arange(Q_BLOCK, dtype=jnp.int32)
        kw = lax.dynamic_slice_in_dim(kw_pad, s0, WINDOW + Q_BLOCK, axis=1)
        vw = lax.dynamic_slice_in_dim(vw_pad, s0, WINDOW + Q_BLOCK, axis=1)
        wpos = s0 - WINDOW + jnp.arange(WINDOW + Q_BLOCK, dtype=jnp.int32)
        return nsa_block(qb, gb, qpos, kc, vc, c_end, kslb, vslb, kw, vw, wpos, slopes)

    o = lax.map(one_block, (qs, gs, starts))
    return o.swapaxes(0, 1).reshape(B, T, Q_W)


def paged_rows(pool, page_table):
    rows = pool[page_table]
    return rows.reshape(rows.shape[0], -1, N_KV, HEAD_DIM)


def nsa_sample(q, gates, kcmp, vcmp, ksel, vsel, kwin, vwin,
               cache_cmp_k, cache_cmp_v, cache_sel_k, cache_sel_v, cache_win_k, cache_win_v, page_table,
               pe_k, w1_k, w2_k, pe_v, w1_v, w2_v, slopes):
    B, n_new = q.shape[0], q.shape[1]
    past = page_table.shape[1] * PAGE_SIZE
    t_pad = -(-(past + n_new) // SEL_BLOCK) * SEL_BLOCK

    def full_rows(pool, new):
        rows = jnp.concatenate([paged_rows(pool, page_table), new], axis=1)
        return jnp.pad(rows, ((0, 0), (0, t_pad - rows.shape[1]), (0, 0), (0, 0)))

    kc = compress(full_rows(cache_cmp_k, kcmp), pe_k, w1_k, w2_k)
    vc = compress(full_rows(cache_cmp_v, vcmp), pe_v, w1_v, w2_v)
    c_end = cmp_end_positions(kc.shape[1])
    kslb = sel_blocks(full_rows(cache_sel_k, ksel))
    vslb = sel_blocks(full_rows(cache_sel_v, vsel))
    wbuf = cache_win_k.shape[1]
    kw = jnp.concatenate([cache_win_k, kwin], axis=1)
    vw = jnp.concatenate([cache_win_v, vwin], axis=1)
    wpos = past - wbuf + jnp.arange(wbuf + n_new, dtype=jnp.int32)
    qpos = past + jnp.arange(n_new, dtype=jnp.int32)
    o = nsa_block(q, gates, qpos, kc, vc, c_end, kslb, vslb, kw, vw, wpos, slopes)
    return o.reshape(B, n_new, Q_W), kw[:, -wbuf:], vw[:, -wbuf:]


def gmlp_mix(u, v, ln_g, ln_b, w_s, b_s):
    B, T = u.shape[0], u.shape[1]
    L = min(T, CHUNK)
    vn = layer_norm(v, ln_g, ln_b)
    vc = vn.reshape(B, T // L, L, N_GROUPS_B, GROUP_CH)
    w = jnp.tril(w_s[:, :L, :L])
    s = jnp.einsum('hst,bnthc->bnshc', w, vc) + b_s[:, :L].T[None, None, :, :, None]
    return u * s.reshape(B, T, D_B), vn


def merge_branches(o_a, z, ga, gb, w_branch_a, w_branch_b, w_out):
    return (jax.nn.sigmoid(ga) * (o_a @ w_branch_a) + jax.nn.sigmoid(gb) * (z @ w_branch_b)) @ w_out


def setup_inputs(seed: int = 0) -> dict:
    key = jax.random.key(seed)
    ks = jax.random.split(key, 40)
    nrm = lambda k, shape, scale: jax.random.normal(k, shape, jnp.float32) * scale
    n_pages = PAST_LEN // PAGE_SIZE
    n_used = DEC_BATCH * n_pages
    n_phys = n_used + (n_used + 3) // 4
    wbuf = min(WINDOW, PAST_LEN)
    pool = (n_phys, PAGE_SIZE, N_KV, HEAD_DIM)
    perm = jax.random.permutation(ks[9], n_phys).astype(jnp.int32)
    page_table = perm[:n_used].reshape(DEC_BATCH, n_pages)
    gain = lambda k, n: 1.0 + nrm(k, (n,), 0.02)
    return {
        'x_prompt': nrm(ks[0], (BATCH, SEQ, D_MODEL), 1.0),
        'x_sample': nrm(ks[1], (DEC_BATCH, DEC_SEQ, D_MODEL), 1.0),
        'cache_cmp_k': nrm(ks[2], pool, 1.0),
        'cache_cmp_v': nrm(ks[3], pool, 1.0),
        'cache_sel_k': nrm(ks[4], pool, 1.0),
        'cache_sel_v': nrm(ks[5], pool, 1.0),
        'cache_win_k': nrm(ks[6], (DEC_BATCH, wbuf, N_KV, HEAD_DIM), 1.0),
        'cache_win_v': nrm(ks[7], (DEC_BATCH, wbuf, N_KV, HEAD_DIM), 1.0),
        'page_table': page_table,
        'ffn1_w_in': nrm(ks[10], (D_MODEL, 2 * D_FF), D_MODEL ** -0.5),
        'ffn1_w_out': nrm(ks[11], (D_FF, D_MODEL), BETA * D_FF ** -0.5),
        'ln1_g': gain(ks[12], D_MODEL),
        'ln1_b': nrm(ks[13], (D_MODEL,), 0.02),
        'w_in': nrm(ks[14], (D_MODEL, IN_W), D_MODEL ** -0.5),
        'cmp_pe_k': nrm(ks[15], (CMP_LEN, HEAD_DIM), 0.02),
        'cmp_w1_k': nrm(ks[16], (CMP_LEN * HEAD_DIM, CMP_HIDDEN), (CMP_LEN * HEAD_DIM) ** -0.5),
        'cmp_w2_k': nrm(ks[17], (CMP_HIDDEN, HEAD_DIM), CMP_HIDDEN ** -0.5),
        'cmp_pe_v': nrm(ks[18], (CMP_LEN, HEAD_DIM), 0.02),
        'cmp_w1_v': nrm(ks[19], (CMP_LEN * HEAD_DIM, CMP_HIDDEN), (CMP_LEN * HEAD_DIM) ** -0.5),
        'cmp_w2_v': nrm(ks[20], (CMP_HIDDEN, HEAD_DIM), CMP_HIDDEN ** -0.5),
        'gmlp_ln_g': gain(ks[21], D_B),
        'gmlp_ln_b': nrm(ks[22], (D_B,), 0.02),
        'spatial_w': nrm(ks[23], (N_GROUPS_B, CHUNK, CHUNK), CHUNK ** -0.5),
        'spatial_b': 1.0 + nrm(ks[24], (N_GROUPS_B, CHUNK), 0.02),
        'w_branch_a': nrm(ks[25], (Q_W, D_MODEL), Q_W ** -0.5),
        'w_branch_b': nrm(ks[26], (D_B, D_MODEL), D_B ** -0.5),
        'w_out': nrm(ks[27], (D_MODEL, D_MODEL), BETA * D_MODEL ** -0.5),
        'ln2_g': gain(ks[28], D_MODEL),
        'ln2_b': nrm(ks[29], (D_MODEL,), 0.02),
        'ffn2_w_in': nrm(ks[30], (D_MODEL, 2 * D_FF), D_MODEL ** -0.5),
        'ffn2_w_out': nrm(ks[31], (D_FF, D_MODEL), BETA * D_FF ** -0.5),
        'ln3_g': gain(ks[32], D_MODEL),
        'ln3_b': nrm(ks[33], (D_MODEL,), 0.02),
    }


def reference(x_prompt, x_sample, cache_cmp_k, cache_cmp_v, cache_sel_k, cache_sel_v, cache_win_k, cache_win_v,
              page_table, ffn1_w_in, ffn1_w_out, ln1_g, ln1_b, w_in, cmp_pe_k, cmp_w1_k, cmp_w2_k,
              cmp_pe_v, cmp_w1_v, cmp_w2_v, gmlp_ln_g, gmlp_ln_b, spatial_w, spatial_b,
              w_branch_a, w_branch_b, w_out, ln2_g, ln2_b, ffn2_w_in, ffn2_w_out, ln3_g, ln3_b):
    slopes = alibi_slopes()
    for _ in range(DEPTH):
        h = half_ffn_block(x_prompt, ffn1_w_in, ffn1_w_out, ln1_g, ln1_b)
        (q, gates, kc, vc, ksl, vsl, kw, vw, u, v, ga, gb) = mixer_inputs(h, w_in)
        o_a = nsa_prompt(q, gates, kc, vc, ksl, vsl, kw, vw,
                         cmp_pe_k, cmp_w1_k, cmp_w2_k, cmp_pe_v, cmp_w1_v, cmp_w2_v, slopes)
        z, _ = gmlp_mix(u, v, gmlp_ln_g, gmlp_ln_b, spatial_w, spatial_b)
        h = layer_norm(ALPHA * h + merge_branches(o_a, z, ga, gb, w_branch_a, w_branch_b, w_out), ln2_g, ln2_b)
        y_prompt = half_ffn_block(h, ffn2_w_in, ffn2_w_out, ln3_g, ln3_b)
        wkeep = min(WINDOW, kw.shape[1])
        p_cmp_k, p_cmp_v, p_sel_k, p_sel_v = kc, vc, ksl, vsl
        p_win_k, p_win_v = kw[:, -wkeep:], vw[:, -wkeep:]
        hs = half_ffn_block(x_sample, ffn1_w_in, ffn1_w_out, ln1_g, ln1_b)
        (qs, gates_s, kcs, vcs, ksls, vsls, kws, vws, us, vs_, gas, gbs) = mixer_inputs(hs, w_in)
        o_as, s_win_k, s_win_v = nsa_sample(qs, gates_s, kcs, vcs, ksls, vsls, kws, vws,
                                             cache_cmp_k, cache_cmp_v, cache_sel_k, cache_sel_v,
                                             cache_win_k, cache_win_v, page_table,
                                             cmp_pe_k, cmp_w1_k, cmp_w2_k, cmp_pe_v, cmp_w1_v, cmp_w2_v, slopes)
        zs, s_chunk_v = gmlp_mix(us, vs_, gmlp_ln_g, gmlp_ln_b, spatial_w, spatial_b)
        hs = layer_norm(ALPHA * hs + merge_branches(o_as, zs, gas, gbs, w_branch_a, w_branch_b, w_out), ln2_g, ln2_b)
        y_sample = half_ffn_block(hs, ffn2_w_in, ffn2_w_out, ln3_g, ln3_b)
    return (y_prompt, y_sample, p_cmp_k, p_cmp_v, p_sel_k, p_sel_v, p_win_k, p_win_v,
            kcs, vcs, ksls, vsls, s_win_k, s_win_v, s_chunk_v)
```

```python
import contextlib
import numpy as np
import concourse.bass as bass
import concourse.mybir as mybir
from concourse.bass_utils import run_bass_kernel_spmd

F32 = mybir.dt.float32
FP8 = mybir.dt.float8e4
BF16 = mybir.dt.bfloat16
I32 = mybir.dt.int32
U32 = mybir.dt.uint32
AF = mybir.ActivationFunctionType
ALU = mybir.AluOpType
AX = mybir.AxisListType


class Buf:
    __slots__ = ("name", "ap", "w", "r", "excl")

    def __init__(self, name, ap=None, excl=False):
        self.name = name
        self.ap = ap
        self.excl = excl
        self.w = {}
        self.r = {}


class Sched:
    import os as _o
    EPOCH = int(_o.environ.get("KEPOCH", "16000"))
    NDMA = 10

    def __init__(self, nc, stack):
        self.nc = nc
        self.st = stack
        self.engs = {"pe": nc.tensor, "act": nc.scalar, "dve": nc.vector, "pool": nc.gpsimd, "sp": nc.sync}
        self.prog = {k: [] for k in self.engs}
        self.cnt = {k: 0 for k in self.engs}
        self.seen = {k: {} for k in self.engs}
        self.esem = {}
        self.dsem = {k: [] for k in self.engs}
        self.dnext = {k: 0 for k in self.engs}
        self.nsem = 0

    def sb(self, name, shape, dt):
        t = self.st.enter_context(self.nc.sbuf_tensor(name, list(shape), dt))
        return Buf(name, t.ap() if hasattr(t, "ap") and callable(getattr(t, "ap")) else t)

    def ps(self, name, shape, dt):
        t = self.st.enter_context(self.nc.psum_tensor(name, list(shape), dt))
        return Buf(name, t.ap() if hasattr(t, "ap") and callable(getattr(t, "ap")) else t, excl=True)

    def _sem(self, name):
        self.nsem += 1
        return self.st.enter_context(self.nc.semaphore(name))

    def _eng_sem(self, e, epoch):
        key = (e, epoch)
        if key not in self.esem:
            self.esem[key] = self._sem("s_%s_%d" % (e, epoch))
        return self.esem[key]

    def _waits(self, e, deps):
        out = []
        for ev in deps:
            if ev is None:
                continue
            if ev[0] == "eng":
                _, src, k = ev
                if src == e and e in ("pe", "sp"):
                    continue
                if self.seen[e].get(src, -1) >= k:
                    continue
                self.seen[e][src] = k
                epoch, idx = divmod(k, self.EPOCH)
                out.append((self._eng_sem(src, epoch), idx + 1))
            else:
                _, sem, val, sid = ev
                if self.seen[e].get(sid, -1) >= val:
                    continue
                self.seen[e][sid] = val
                out.append((sem, val))
        return out

    def _deps(self, reads, writes):
        deps = []
        for b in reads:
            deps.extend(b.w.values())
            if b.excl:
                deps.extend(b.r.values())
        for b in writes:
            deps.extend(b.w.values())
            deps.extend(b.r.values())
        return deps

    def _mark(self, ev, key, reads, writes):
        for b in reads:
            b.r[key] = ev
        for b in writes:
            if ev[0] == "dma" and all(v[0] == "dma" for v in b.w.values()):
                b.w[ev[3]] = ev
            else:
                b.w = {"w": ev}
            b.r = {}

    def op(self, e, fn, reads=(), writes=()):
        waits = self._waits(e, self._deps(reads, writes))
        k = self.cnt[e]
        self.cnt[e] += 1
        epoch, _ = divmod(k, self.EPOCH)
        sem = self._eng_sem(e, epoch)

        def thunk(eng, fn=fn, sem=sem, waits=waits):
            for s, v in waits:
                eng.wait_ge(s, v)
            fn(eng).then_inc(sem, 1)

        self.prog[e].append(thunk)
        ev = ("eng", e, k)
        self._mark(ev, e, reads, writes)
        return ev

    def dma(self, q, out, in_, reads=(), writes=(), fn=None, **kw):
        pool = self.dsem[q]
        if len(pool) < self.NDMA:
            pool.append([self._sem("d_%s_%d" % (q, len(pool))), 0, "d_%s_%d" % (q, len(pool))])
        i = self.dnext[q] % self.NDMA
        self.dnext[q] += 1
        ent = pool[i]
        sem, prev, sid = ent
        deps = self._deps(reads, writes)
        if prev > 0:
            deps.append(("dma", sem, prev, sid))
        waits = self._waits(q, deps)
        tgt = prev + 16
        ent[1] = tgt

        def thunk(eng, waits=waits, sem=sem, out=out, in_=in_, kw=kw, fn=fn):
            for s, v in waits:
                eng.wait_ge(s, v)
            if fn is not None:
                fn(eng).then_inc(sem, 16)
            else:
                eng.dma_start(out=out, in_=in_, **kw).then_inc(sem, 16)

        self.prog[q].append(thunk)
        ev = ("dma", sem, tgt, sid)
        self._mark(ev, "dma_%s_%d" % (q, self.dnext[q]), reads, writes)
        return ev

    def barrier(self):
        evs = []
        for src in ("pe", "act", "dve", "pool"):
            if self.cnt[src] > 0:
                evs.append(("eng", src, self.cnt[src] - 1))
        for q, pool in self.dsem.items():
            for sem, val, sid in pool:
                if val > 0:
                    evs.append(("dma", sem, val, sid))
        for e in ("pe", "act", "dve", "pool", "sp"):
            deps = [ev for ev in evs if not (ev[0] == "eng" and ev[1] == e)]
            waits = self._waits(e, deps)

            def thunk(eng, waits=waits):
                for s, v in waits:
                    eng.wait_ge(s, v)

            self.prog[e].append(thunk)

    def finish(self, evs):
        waits = self._waits("sp", list(evs))

        def thunk(eng, waits=waits):
            for s, v in waits:
                eng.wait_ge(s, v)

        self.prog["sp"].append(thunk)

    def emit(self):
        with self.nc.Block() as block:
            for name, reg in (("pe", "tensor"), ("act", "scalar"), ("dve", "vector"), ("pool", "gpsimd"), ("sp", "sync")):
                prog = self.prog[name]

                def body(eng, prog=prog):
                    for th in prog:
                        th(eng)

                getattr(block, reg)(body)


D = 1024
DFF = 2816
NJ = DFF // 128
INW = 4376
ALPHA = 2.0 ** 0.25
LN_EPS = 1e-5
C_Q, C_KV, C_G, C_U, C_V, C_GA, C_GB = 0, 512, 1280, 1304, 1816, 2328, 3352
SLOPES = [2.0 ** (-(h + 1)) for h in range(8)]
NEGBIG = -30000.0


def host_consts(NTL, first_valid_tile, tmin=0):
    import ml_dtypes
    bf = ml_dtypes.bfloat16
    NPOS = NTL * 128
    NCB = NPOS // 16 - 1
    NCT = (NCB + 127) // 128
    c = {}
    c["c_identf"] = np.eye(128, dtype=np.float32)
    ka = np.zeros((5, NPOS), np.float32)
    pos = np.arange(NPOS)
    ka[0] = pos % 128
    ka[1] = 1.0
    ka[2] = pos // 128
    ka[3] = 1.0
    ka[4] = np.where(pos < first_valid_tile * 128, -1.0e6, 0.0)
    c["c_kaug"] = ka.astype(bf)
    kc = np.zeros((5, NCT * 128), np.float32)
    ci = np.arange(NCT * 128)
    kc[0] = 16 * (ci % 128)
    kc[1] = 1.0
    kc[2] = 16 * (ci // 128)
    kc[3] = 1.0
    kc[4] = 31.0 + np.where(16 * ci < first_valid_tile * 128, -1.0e6, 0.0)
    c["c_kcaug"] = kc.astype(bf)
    qa = np.zeros((8, 5, NPOS), np.float32)
    for h in range(8):
        s = SLOPES[h]
        qa[h, 0] = s
        qa[h, 1] = -s * (pos % 128)
        qa[h, 2] = s * 128
        qa[h, 3] = -s * 128 * (pos // 128)
        qa[h, 4] = s
    c["c_qaug"] = qa.astype(bf)
    npt = max(NPOS, tmin)
    post = np.arange(npt)
    T = np.zeros((128, npt), np.float32)
    T[(post // 64) % 128, post] = 1.0
    c["c_T"] = T.astype(bf)
    W = np.zeros((128, NCT, 128), np.float32)
    for ct in range(NCT):
        for ir in range(128):
            i = ct * 128 + ir
            for j in range(128):
                if 4 * j - 1 <= i <= 4 * j + 3:
                    W[ir, ct, j] = 1.0
    c["c_wimp"] = W.astype(bf)
    Gb = np.zeros((128, 256), np.float32)
    Gm = np.zeros((128, 256), np.float32)
    for q in range(128):
        a = 1 if q >= 64 else 0
        for x in range(256):
            d = x - 128
            if d > a:
                Gb[q, x] = -1.0e30
            elif d >= a - 1:
                Gb[q, x] = 1.0e4
            else:
                Gm[q, x] = 1.0
    c["c_gb"] = Gb
    c["c_gm"] = Gm
    F0 = np.zeros((128, 128), np.float32)
    jf = first_valid_tile * 2
    F0[:, :jf] = -1.0e30
    F0[:, jf] = 1.0e4
    c["c_f0"] = F0
    sp = np.zeros((128, 2, 64), np.float32)
    for m in range(64):
        sp[2 * m, 0, m] = 1.0
        sp[2 * m + 1, 1, m] = 1.0
    c["c_selpair"] = sp.astype(bf)
    c["c_ones"] = np.ones((1, 128), np.float32).astype(bf)
    return c


W_NAMES = ["ffn1_w_in", "ffn1_w_out", "ln1_g", "ln1_b", "w_in", "cmp_pe_k", "cmp_w1_k", "cmp_w2_k",
           "cmp_pe_v", "cmp_w1_v", "cmp_w2_v", "gmlp_ln_g", "gmlp_ln_b", "spatial_w", "spatial_b",
           "w_branch_a", "w_branch_b", "w_out", "ln2_g", "ln2_b", "ffn2_w_in", "ffn2_w_out", "ln3_g", "ln3_b"]
W_SHAPES = {"ffn1_w_in": [1024, 5632], "ffn1_w_out": [2816, 1024], "ln1_g": [1024], "ln1_b": [1024],
            "w_in": [1024, 4376], "cmp_pe_k": [32, 64], "cmp_w1_k": [2048, 128], "cmp_w2_k": [128, 64],
            "cmp_pe_v": [32, 64], "cmp_w1_v": [2048, 128], "cmp_w2_v": [128, 64], "gmlp_ln_g": [512],
            "gmlp_ln_b": [512], "spatial_w": [4, 128, 128], "spatial_b": [4, 128], "w_branch_a": [512, 1024],
            "w_branch_b": [512, 1024], "w_out": [1024, 1024], "ln2_g": [1024], "ln2_b": [1024],
            "ffn2_w_in": [1024, 5632], "ffn2_w_out": [2816, 1024], "ln3_g": [1024], "ln3_b": [1024]}


def build(NBC, NBO, stage=99, NPG=0, NPHYS=0):
    NTC, NTO = 4 * NBC, 4 * NBO
    NTL = NTC + NTO
    NPOS = NTL * 128
    NCB = NPOS // 16 - 1
    NCT = (NCB + 127) // 128
    nc = bass.Bass("TRN2", target_bir_lowering=False)

    def din(name, shape, dt=F32):
        return nc.dram_tensor(name, list(shape), dt, kind="ExternalInput").ap()

    def dout(name, shape, dt=F32):
        return nc.dram_tensor(name, list(shape), dt, kind="ExternalOutput").ap()

    def dint(name, shape, dt=BF16):
        return nc.dram_tensor(name, list(shape), dt, kind="Internal").ap()

    x_ctx = din("x_ctx", [max(NTC, 1) * 128, D])
    x_own = din("x_own", [NTO * 128, D])
    wd = {n: din(n, W_SHAPES[n]) for n in W_NAMES}
    NPT = max(NPOS, 8192) if NPG else NPOS
    cshapes = {"c_identf": ([128, 128], F32), "c_kaug": ([5, NPOS], BF16), "c_kcaug": ([5, NCT * 128], BF16),
               "c_qaug": ([8, 5, NPOS], BF16), "c_T": ([128, NPT], BF16), "c_wimp": ([128, NCT, 128], BF16),
               "c_gb": ([128, 256], F32), "c_gm": ([128, 256], F32), "c_f0": ([128, 128], F32),
               "c_selpair": ([128, 2, 64], BF16), "c_ones": ([1, 128], BF16)}
    cd = {n: din(n, s, dt) for n, (s, dt) in cshapes.items()}
    import os as _os0
    DBG = bool(_os0.environ.get("KDBG"))
    if DBG:
        dbg_oa = dout("dbg_oa", [NTO * 128, 512])
        dbg_zT = dout("dbg_zT", [512, NTO * 128])
        dbg_cf = dout("dbg_cf", [NTO * 128, 2, 12])
        dbg_ps = dout("dbg_ps", [NTO * 128, 2, 128])
    if NPG:
        x_smp = din("x_smp", [128, D])
        pools = [din("pool%d" % i, [NPHYS * 128, 128]) for i in range(4)]
        cwin = [din("cwin_k", [4, 512, 128]), din("cwin_v", [4, 512, 128])]
        ptab = din("ptab", [4, NPG], I32)
        NCS = NPG * 8
        NCTS = (NCS + 127) // 128
        NJS = 2 * NPG + 1
        sconst = {"c_piota": ([128, 1], F32), "c_skaug": ([5, (NPG + 1) * 128], BF16), "c_skcaug": ([5, NCTS * 128], BF16),
                  "c_swaug": ([5, 512], BF16), "c_sqaug": ([8, 5, 128], BF16), "c_w0": ([128, 33], BF16),
                  "c_smb": ([128, 264], F32), "c_sbb": ([128, 264], F32)}
        scd = {n: din(n, sh, dt) for n, (sh, dt) in sconst.items()}
        y_s = dout("y_s", [128, D])
        kv_s = dout("kv_s", [128, 768])
        swin = [dout("swin_k", [4, 512, 128]), dout("swin_v", [4, 512, 128])]
        chunkv = dout("chunkv", [128, 512])
    y_own = dout("y_own", [NTO * 128, D])
    kv_own = dout("kv_own", [NTO * 128, 768])

    s_f_in = [dint("s_f1in", [NJ, 128, 2, 8, 128]), dint("s_f2in", [NJ, 128, 2, 8, 128])]
    s_f_out = [dint("s_f1out", [DFF, D]), dint("s_f2out", [DFF, D])]
    WIN_UNITS = [(0, 256), (256, 256), (512, 256), (768, 256), (1024, 256), (1280, 24), (1304, 256), (1560, 256),
                 (1816, 256), (2072, 256)] + [(2328 + 256 * i, 256) for i in range(4)] + [(3352 + 256 * i, 256) for i in range(4)]
    U_Q, U_KV, U_G, U_U, U_V, U_GA, U_GB = 0, 2, 5, 6, 8, 10, 14
    s_winu = dint("s_winu", [len(WIN_UNITS), 128, 8, 256])
    s_wa = dint("s_wa", [8, 128, 4, 128])
    s_wb = dint("s_wb", [8, 128, 4, 128])
    s_wo = dint("s_wo", [D, D])
    s_w1 = [dint("s_w1k", [2048, 128]), dint("s_w1v", [2048, 128])]
    s_w2 = [dint("s_w2k", [128, 64]), dint("s_w2v", [128, 64])]
    s_pe = [dint("s_pek", [2048, 1]), dint("s_pev", [2048, 1])]
    s_sw = dint("s_sw", [4, 128, 128])
    s_sb = dint("s_sb", [4, 128])

    out_events = []
    with contextlib.ExitStack() as st:
        S = Sched(nc, st)
        sb, ps = S.sb, S.ps

        identf = sb("identf", [128, 128], F32)
        identb = sb("identb", [128, 128], BF16)
        lnp = sb("lnp", [128, 6, D], F32)
        glnp = sb("glnp", [128, 2, 512], F32)
        wsT = sb("wsT", [128, 4, 128], BF16)
        bsrow = sb("bsrow", [1, 4, 128], BF16)
        onesrow = sb("onesrow", [1, 128], BF16)
        Tsel = sb("Tsel", [128, NPT], FP8)
        wimp = sb("wimp", [128, NCT, 128], BF16)
        gbt = sb("gbt", [128, 256], F32)
        gmt = sb("gmt", [128, 256], F32)
        f0t = sb("f0t", [128, 128], F32)
        selpair = sb("selpair", [128, 2, 64], BF16)
        w2c = [sb("w2k", [128, 64], BF16), sb("w2v", [128, 64], BF16)]
        pec = [sb("pek", [128, 16], BF16), sb("pev", [128, 16], BF16)]
        pebias = sb("pebias", [128, 2], F32)
        Ksel = sb("Ksel", [69, 2, NPOS], BF16)
        Vsel = sb("Vsel", [128, NTL, 2, 66], BF16)
        Kwin = sb("Kwin", [69, 2, 1024], BF16)
        Vwin = sb("Vwin", [128, 8, 2, 66], BF16)
        Kc = sb("Kc", [69, 2, NCT * 128], BF16)
        Vc = sb("Vc", [128, NCT, 2, 66], BF16)
        XT2 = sb("XT2", [128, 4, 264], BF16)
        xres = [sb("xres%d" % i, [128, D], F32) for i in range(4)]
        xT = sb("xT", [128, 8, 512], BF16)
        chunks = [sb("chunk%d" % i, [128, 512], BF16) for i in range(NJ)]
        qT = sb("qT", [69, 8, 512], BF16)
        NR = 4
        wring = [sb("wring%d" % i, [128, 2048], BF16) for i in range(NR)]
        silu_t = [sb("silu%d" % i, [128, 512], BF16) for i in range(2)]
        pTb = [sb("pT%d" % i, [128, 512], BF16) for i in range(3)]
        f32t = [sb("f32t%d" % i, [128, 792], F32) for i in range(2)]
        stats = sb("stats", [128, 2, 6], F32)
        mv = sb("mv", [128, 2], F32)
        rstd = sb("rstd", [128, 1], F32)
        banks = [ps("bank%d" % i, [128, 512], F32) for i in range(8)]
        gts = sb("gts", [128, 4, 24], F32)
        vnb = sb("vnb", [128, 4, 512], BF16)
        oab = sb("oab", [128, 512], BF16)
        psc = sb("psc", [128, 128], F32)
        psc2 = sb("psc2", [128, 128], F32)
        m8a = sb("m8a", [128, 8], F32)
        m8b = sb("m8b", [128, 8], F32)
        nsb = sb("nsb", [128, 128], BF16)
        nsT = sb("nsT", [128, 128], BF16)
        rden = sb("rden", [128, 3, 4], F32)
        coef = sb("coef", [128, 3, 4], F32)
        oa32 = [sb("oa32_%d" % i, [128, 4, 64], F32) for i in range(2)]

        rr = {"w": 0, "silu": 0, "pT": 0, "f32": 0, "S": 0, "oa": 0}

        def nxt(key, lst):
            i = rr[key] % len(lst)
            rr[key] += 1
            return lst[i]

        def wload(src_ap, reads, shape):
            slot = nxt("w", wring)
            n = 1
            for s_ in shape[1:]:
                n *= s_
            view = slot.ap[:, 0:n]
            if len(shape) == 3:
                view = view.rearrange("p (a b) -> p a b", a=shape[1])
            elif len(shape) == 4:
                view = view.rearrange("p (a b c) -> p a b c", a=shape[1], b=shape[2])
            S.dma("sp", view, src_ap, reads=reads, writes=[slot])
            return slot, view

        def mm(ob, oap, lb, lap, rb, rap, start, stop, extra=(), skip=False):
            S.op("pe", lambda e: e.matmul(oap, lhsT=lap, rhs=rap, start=start, stop=stop, skip_group_check=skip),
                 reads=[lb, rb, *extra], writes=[ob])

        def tr(ob, oap, ib, iap, idb):
            S.op("pe", lambda e: e.transpose(out=oap, in_=iap, identity=idb.ap[:iap.shape[0], :iap.shape[0]]),
                 reads=[ib, idb], writes=[ob])

        def act(ob, oap, ib, iap, func, extra_r=(), **kw):
            S.op("act", lambda e: e.activation(out=oap, in_=iap, func=func, **kw), reads=[ib, *extra_r], writes=[ob])

        def tt(eng, ob, oap, ab, aap, bb, bap, op):
            S.op(eng, lambda e: e.tensor_tensor(out=oap, in0=aap, in1=bap, op=op), reads=[ab, bb], writes=[ob])

        def ts(eng, ob, oap, ib, iap, s1, s2, op0, op1=None, extra_r=()):
            if op1 is None:
                S.op(eng, lambda e: e.tensor_scalar(out=oap, in0=iap, scalar1=s1, scalar2=None, op0=op0),
                     reads=[ib, *extra_r], writes=[ob])
            else:
                S.op(eng, lambda e: e.tensor_scalar(out=oap, in0=iap, scalar1=s1, scalar2=s2, op0=op0, op1=op1),
                     reads=[ib, *extra_r], writes=[ob])

        def stt(ob, oap, ab, aap, scalar, bb, bap, op0, op1, extra_r=()):
            S.op("dve", lambda e: e.scalar_tensor_tensor(out=oap, in0=aap, scalar=scalar, in1=bap, op0=op0, op1=op1),
                 reads=[ab, bb, *extra_r], writes=[ob])

        def cp(eng, ob, oap, ib, iap):
            if eng == "act":
                S.op("act", lambda e: e.copy(out=oap, in_=iap), reads=[ib], writes=[ob])
            else:
                S.op(eng, lambda e: e.tensor_copy(out=oap, in_=iap), reads=[ib], writes=[ob])

        S.dma("sp", identf.ap[:], cd["c_identf"], writes=[identf])
        cp("dve", identb, identb.ap[:], identf, identf.ap[:])
        for i, n in enumerate(["ln1_g", "ln1_b", "ln2_g", "ln2_b", "ln3_g", "ln3_b"]):
            S.dma("sp", lnp.ap[:, i, :], wd[n].partition_broadcast(128), writes=[lnp])
        for i, n in enumerate(["gmlp_ln_g", "gmlp_ln_b"]):
            S.dma("sp", glnp.ap[:, i, :], wd[n].partition_broadcast(128), writes=[glnp])
        for i0 in range(0, NPT, 512):
            tb_ = nxt("pT", pTb)
            S.dma("sp", tb_.ap[:], cd["c_T"][:, i0:i0 + 512], writes=[tb_])
            S.op("dve", lambda e, tb_=tb_, i0=i0: e.tensor_copy(out=Tsel.ap[:, i0:i0 + 512], in_=tb_.ap[:], saturate=False),
                 reads=[tb_], writes=[Tsel])
        S.dma("sp", wimp.ap[:], cd["c_wimp"], writes=[wimp])
        S.dma("sp", gbt.ap[:], cd["c_gb"], writes=[gbt])
        S.dma("sp", gmt.ap[:], cd["c_gm"], writes=[gmt])
        S.dma("sp", f0t.ap[:], cd["c_f0"], writes=[f0t])
        S.dma("sp", selpair.ap[:], cd["c_selpair"], writes=[selpair])
        S.dma("sp", onesrow.ap[:], cd["c_ones"], writes=[onesrow])
        S.op("dve", lambda e: e.memset(Kc.ap[:], 0.0), writes=[Kc])
        for g in range(2):
            S.dma("sp", Ksel.ap[64:69, g, :], cd["c_kaug"], writes=[Ksel])
            S.dma("sp", Kc.ap[64:69, g, :], cd["c_kcaug"], writes=[Kc])
        S.op("dve", lambda e: e.memset(Vsel.ap[:], 1.0), writes=[Vsel])
        S.op("dve", lambda e: e.memset(Vwin.ap[:], 1.0), writes=[Vwin])
        S.op("dve", lambda e: e.memset(Vc.ap[:], 1.0), writes=[Vc])
        S.op("dve", lambda e: e.memset(XT2.ap[:], 0.0), writes=[XT2])

        sc = {}

        def cast(key, out_ap, in_ap):
            b = sc.setdefault(key, Buf(key))
            S.dma("pool", out_ap, in_ap, writes=[b])

        def cast_ffn(f):
            wi = wd["ffn%d_w_in" % (f + 1)]
            wo = wd["ffn%d_w_out" % (f + 1)]
            for j in range(NJ):
                for gu in range(2):
                    c0 = gu * DFF + j * 128
                    cast(("fin", f, j), s_f_in[f][j, :, gu, :, :], wi[:, c0:c0 + 128].rearrange("(kc p) c -> p kc c", p=128))
            for j in range(NJ):
                cast(("fout", f, j), s_f_out[f][j * 128:(j + 1) * 128, :], wo[j * 128:(j + 1) * 128, :])

        cast_ffn(0)
        for u, (c0, ncol) in enumerate(WIN_UNITS):
            cast(("win", u), s_winu[u, :, :, 0:ncol], wd["w_in"][:, c0:c0 + ncol].rearrange("(kc p) c -> p kc c", p=128))
        for kv, (n1, n2, npe) in enumerate((("cmp_w1_k", "cmp_w2_k", "cmp_pe_k"), ("cmp_w1_v", "cmp_w2_v", "cmp_pe_v"))):
            cast("cmpw", s_w1[kv][:, :], wd[n1][:, :])
            cast("cmpw", s_w2[kv][:, :], wd[n2][:, :])
            cast("cmpw", s_pe[kv].rearrange("(a b) o -> a (b o)", b=64), wd[npe][:, :])
        cast("sw", s_sw.rearrange("h a b -> (h a) b"), wd["spatial_w"].rearrange("h a b -> (h a) b"))
        cast("sw", s_sb[:, :], wd["spatial_b"][:, :])
        for c in range(8):
            cast(("wa", c), s_wa[c], wd["w_branch_a"][:, c * 128:(c + 1) * 128].rearrange("(kc p) n -> p kc n", p=128))
            cast(("wb", c), s_wb[c], wd["w_branch_b"][:, c * 128:(c + 1) * 128].rearrange("(kc p) n -> p kc n", p=128))
        for r0 in range(0, 1024, 128):
            cast(("wo", r0 // 128), s_wo[r0:r0 + 128, :], wd["w_out"][r0:r0 + 128, :])
        cast_ffn(1)

        for kv in range(2):
            S.dma("sp", w2c[kv].ap[:], s_w2[kv][:, :], reads=[sc["cmpw"]], writes=[w2c[kv]])
            pl = nxt("pT", pTb)
            S.dma("sp", pl.ap[0:16, 0:128], s_pe[kv].rearrange("(c p) o -> c (p o)", p=128), reads=[sc["cmpw"]], writes=[pl])
            tr(banks[7], banks[7].ap.bitcast(BF16)[:, kv * 16:(kv + 1) * 16], pl, pl.ap[0:16, 0:128], identb)
            cp("dve", pec[kv], pec[kv].ap[:], banks[7], banks[7].ap.bitcast(BF16)[:, kv * 16:(kv + 1) * 16])
        S.dma("sp", bsrow.ap[:], s_sb.rearrange("(o h) t -> o h t", o=1), reads=[sc["sw"]], writes=[bsrow])
        swl = nxt("pT", pTb)
        S.dma("sp", swl.ap[:].rearrange("p (h t) -> p h t", h=4), s_sw.rearrange("h a b -> a h b"), reads=[sc["sw"]], writes=[swl])
        trb = banks[7]
        trb_bf = trb.ap.bitcast(BF16)
        for h in range(4):
            tr(trb, trb_bf[:, h * 128:(h + 1) * 128], swl, swl.ap[:, h * 128:(h + 1) * 128], identb)
        cp("dve", wsT, wsT.ap[:].rearrange("p h t -> p (h t)"), trb, trb_bf[:, 0:512])
        S.op("pool", lambda e: e.affine_select(out=wsT.ap[:], in_=wsT.ap[:], pattern=[[0, 4], [1, 128]],
                                               compare_op=ALU.is_ge, fill=0.0, base=0, channel_multiplier=-1),
             reads=[wsT], writes=[wsT])
        pb = banks[5]
        for kv in range(2):
            w1s, w1v_ = wload(s_w1[kv].rearrange("(c p) h -> p c h", p=128), [sc["cmpw"]], [128, 16, 128])
            for c in range(16):
                mm(pb, pb.ap[:, kv:kv + 1], w1s, w1v_[:, c, :], pec[kv], pec[kv].ap[:, c:c + 1], c == 0, c == 15)
        cp("dve", pebias, pebias.ap[:], pb, pb.ap[:, 0:2])

        def transpose_tile(src, t):
            for half in range(2):
                bk = banks[(2 * t + half) % 4]
                for k in range(4):
                    kc = half * 4 + k
                    tr(bk, bk.ap[:, k * 128:(k + 1) * 128], src, src.ap[:, kc * 128:(kc + 1) * 128], identf)
                eng = "act" if half == 0 else "dve"
                cp(eng, xT, xT.ap[:, half * 4:half * 4 + 4, t * 128:(t + 1) * 128],
                   bk, bk.ap[:].rearrange("p (k n) -> p k n", k=4))

        def layer_norm_tile(xb, gi, eps):
            for h in range(2):
                S.op("dve", lambda e, h=h: e.bn_stats(out=stats.ap[:, h, :], in_=xb.ap[:, h * 512:(h + 1) * 512]),
                     reads=[xb], writes=[stats])
            S.op("dve", lambda e: e.bn_aggr(out=mv.ap[:], in_=stats.ap[:].rearrange("p a b -> p (a b)")),
                 reads=[stats], writes=[mv])
            act(rstd, rstd.ap[:], mv, mv.ap[:, 1:2], AF.Sqrt, bias=eps, scale=1.0)
            S.op("dve", lambda e: e.reciprocal(out=rstd.ap[:], in_=rstd.ap[:]), reads=[rstd], writes=[rstd])
            ts("dve", xb, xb.ap[:], xb, xb.ap[:], mv.ap[:, 0:1], rstd.ap[:, 0:1], ALU.subtract, ALU.mult, extra_r=[mv, rstd])
            tt("dve", xb, xb.ap[:], xb, xb.ap[:], lnp, lnp.ap[:, gi, :], ALU.mult)
            tt("dve", xb, xb.ap[:], xb, xb.ap[:], lnp, lnp.ap[:, gi + 1, :], ALU.add)

        def ffn_block(f, nt, gi):
            ntok = nt * 128
            for j in range(NJ):
                slot, w = wload(s_f_in[f][j], [sc[("fin", f, j)]], [128, 2, 8, 128])
                gb_, ub_ = banks[j % 2], banks[2 + j % 2]
                for gu, bk in ((0, gb_), (1, ub_)):
                    for kc in range(8):
                        mm(bk, bk.ap[:, 0:ntok], slot, w[:, gu, kc, :], xT, xT.ap[:, kc, 0:ntok], kc == 0, kc == 7)
                sl = nxt("silu", silu_t)
                act(sl, sl.ap[:, 0:ntok], gb_, gb_.ap[:, 0:ntok], AF.Silu)
                tt("dve", chunks[j], chunks[j].ap[:, 0:ntok], sl, sl.ap[:, 0:ntok], ub_, ub_.ap[:, 0:ntok], ALU.mult)
            for n in range(2):
                for j in range(NJ):
                    slot, w = wload(s_f_out[f][j * 128:(j + 1) * 128, n * 512:(n + 1) * 512], [sc[("fout", f, j)]], [128, 512])
                    for t in range(nt):
                        bk = banks[4 + t]
                        mm(bk, bk.ap[:], chunks[j], chunks[j].ap[:, t * 128:(t + 1) * 128], slot, w, j == 0, j == NJ - 1)
                for t in range(nt):
                    bk = banks[4 + t]
                    stt(xres[t], xres[t].ap[:, n * 512:(n + 1) * 512], xres[t], xres[t].ap[:, n * 512:(n + 1) * 512],
                        2.0 * ALPHA, bk, bk.ap[:], ALU.mult, ALU.add)
            for t in range(nt):
                layer_norm_tile(xres[t], gi, 4.0 * LN_EPS)

        def kv_project(nt, lt0, is_own, own_row0, need_win):
            wl = []
            for u in ([U_KV, U_KV + 1] + ([U_KV + 2] if need_win else [])):
                wl.append(wload(s_winu[u], [sc[("win", u)]], [128, 8, 256]))
            for t in range(nt):
                lt = lt0 + t
                bkA, bkB = banks[4 + (t % 2)], banks[6]
                for ui, (slot, w) in enumerate(wl):
                    bk, o0 = (bkA, ui * 256) if ui < 2 else (bkB, 0)
                    for kc in range(8):
                        mm(bk, bk.ap[:, o0:o0 + 256], xT, xT.ap[:, kc, t * 128:(t + 1) * 128], slot, w[:, kc, :], kc == 0, kc == 7)
                if is_own and (KSUB & 8):
                    stg = nxt("f32", f32t)
                    cp("act", stg, stg.ap[:, 0:512], bkA, bkA.ap[:])
                    if need_win:
                        cp("act", stg, stg.ap[:, 512:768], bkB, bkB.ap[:, 0:256])
                    r0 = own_row0 + t * 128
                    ncol = 768 if need_win else 512
                    out_events.append(S.dma("act", kv_own[r0:r0 + 128, 0:ncol], stg.ap[:, 0:ncol], reads=[stg]))
                kvb = nxt("pT", pTb)
                cp("dve", kvb, kvb.ap[:], bkA, bkA.ap[:])
                if KSUB & 4:
                    cp("act", Vsel, Vsel.ap[:, lt, :, 0:64], bkA, bkA.ap[:, 384:512].rearrange("p (g d) -> p g d", g=2))
                if KSUB & 2:
                    for g in range(2):
                        tr(trb, trb_bf[0:64, g * 128:(g + 1) * 128], kvb, kvb.ap[:, 256 + g * 64:320 + g * 64], identb)
                if need_win:
                    kwb = nxt("pT", pTb)
                    cp("dve", kwb, kwb.ap[:, 0:256], bkB, bkB.ap[:, 0:256])
                    if KSUB & 2:
                        for g in range(2):
                            tr(trb, trb_bf[0:64, 256 + g * 128:384 + g * 128], kwb, kwb.ap[:, g * 64:(g + 1) * 64], identb)
                    ws = lt % 8
                    if KSUB & 4:
                        cp("act", Vwin, Vwin.ap[:, ws, :, 0:64], bkB, bkB.ap[:, 128:256].rearrange("p (g d) -> p g d", g=2))
                if KSUB & 2:
                    cp("act", Ksel, Ksel.ap[0:64, :, lt * 128:(lt + 1) * 128], trb, trb_bf[0:64, 0:256].rearrange("p (g n) -> p g n", g=2))
                if need_win:
                    ws = lt % 8
                    if KSUB & 2:
                        cp("dve", Kwin, Kwin.ap[0:64, :, ws * 128:(ws + 1) * 128], trb, trb_bf[0:64, 256:512].rearrange("p (g n) -> p g n", g=2))
                    if KSUB & 16:
                        for g in range(2):
                            S.dma("sp", Kwin.ap[64:69, g, ws * 128:(ws + 1) * 128], cd["c_kaug"][:, lt * 128:(lt + 1) * 128], writes=[Kwin])
                xb_ = banks[(2 * t) % 4]
                if not (KSUB & 1):
                    continue
                for kvg in range(4):
                    for par in range(2):
                        mm(xb_, xb_.ap[par * 64:(par + 1) * 64, kvg * 64:(kvg + 1) * 64], kvb, kvb.ap[:, kvg * 64:(kvg + 1) * 64],
                           selpair, selpair.ap[:, par, :], True, True)
                cp("dve", XT2, XT2.ap[:, :, 8 + t * 64:8 + (t + 1) * 64], xb_, xb_.ap[:, 0:256].rearrange("p (a m) -> p a m", a=4))

        def compress_block(nt, lt0, Kc=Kc, Vc=Vc):
            nb = nt * 8
            i_lo = lt0 * 8 - 1
            skip = 1 if i_lo < 0 else 0
            n = nb - skip
            i0 = i_lo + skip
            for kv in range(2):
                hb = banks[kv]
                w1s, w1v_ = wload(s_w1[kv].rearrange("(c p) h -> p c h", p=128), [sc["cmpw"]], [128, 16, 128])
                for g in range(2):
                    a = kv * 2 + g
                    for c in range(16):
                        rhs = XT2.ap[:, a, c + 8 * skip: c + 8 * skip + 8 * (n - 1) + 1: 8]
                        mm(hb, hb.ap[:, g * 64:g * 64 + n], w1s, w1v_[:, c, :], XT2, rhs, c == 0, c == 15)
                gl = nxt("silu", silu_t)
                for g in range(2):
                    act(gl, gl.ap[:, g * 64:g * 64 + n], hb, hb.ap[:, g * 64:g * 64 + n], AF.Gelu_apprx_tanh,
                        extra_r=[pebias], bias=pebias.ap[:, kv:kv + 1], scale=1.0)
                ob = banks[2 + kv]
                if kv == 0:
                    for g in range(2):
                        mm(ob, ob.ap[0:64, g * 64:g * 64 + n], w2c[0], w2c[0].ap[:], gl, gl.ap[:, g * 64:g * 64 + n], True, True)
                    for g in range(2):
                        cp("act", Kc, Kc.ap[0:64, g, i0:i0 + n], ob, ob.ap[0:64, g * 64:g * 64 + n])
                else:
                    for g in range(2):
                        mm(ob, ob.ap[0:n, g * 64:(g + 1) * 64], gl, gl.ap[:, g * 64:g * 64 + n], w2c[1], w2c[1].ap[:], True, True)
                    stg = nxt("pT", pTb)
                    cp("act", stg, stg.ap[0:n, 0:128], ob, ob.ap[0:n, 0:128])
                    i = i0
                    while i < i0 + n:
                        ct, p0 = divmod(i, 128)
                        m = min(i0 + n - i, 128 - p0)
                        S.dma("sp", Vc.ap[p0:p0 + m, ct, :, 0:64], stg.ap[i - i0:i - i0 + m, 0:128].rearrange("p (g d) -> p g d", g=2),
                              reads=[stg], writes=[Vc])
                        i += m
            cp("dve", XT2, XT2.ap[:, :, 0:8], XT2, XT2.ap[:, :, nt * 64:nt * 64 + 8])

        uTc, oaT, zT, mT = chunks[0:4], chunks[4:8], chunks[8:12], chunks[12:20]

        def ln_small(src, sap, n, gap, bap, dst, dap, eps):
            S.op("dve", lambda e: e.bn_stats(out=stats.ap[:, 0, :], in_=sap), reads=[src], writes=[stats])
            S.op("dve", lambda e: e.bn_aggr(out=mv.ap[:], in_=stats.ap[:, 0, :]), reads=[stats], writes=[mv])
            act(rstd, rstd.ap[:], mv, mv.ap[:, 1:2], AF.Sqrt, bias=eps, scale=1.0)
            S.op("dve", lambda e: e.reciprocal(out=rstd.ap[:], in_=rstd.ap[:]), reads=[rstd], writes=[rstd])
            ts("dve", src, sap, src, sap, mv.ap[:, 0:1], rstd.ap[:, 0:1], ALU.subtract, ALU.mult, extra_r=[mv, rstd])
            tt("dve", src, sap, src, sap, glnp, gap, ALU.mult)
            tt("dve", dst, dap, src, sap, glnp, bap, ALU.add)

        def sbank():
            return nxt("S", banks[0:2])

        def pool_mask(P, base, cm, qstep, op):
            S.op("pool", lambda e: e.affine_select(out=P.ap[:].rearrange("p (r q) -> p r q", r=4),
                                                   in_=P.ap[:].rearrange("p (r q) -> p r q", r=4),
                                                   pattern=[[0, 4], [qstep, 128]], compare_op=op, fill=0.0,
                                                   base=base, channel_multiplier=cm), reads=[P], writes=[P])

        def pv(Ob, first, P, Vb, vap):
            for r in range(4):
                mm(Ob, Ob.ap[:, r * 66:r * 66 + 65], P, P.ap[:, r * 128:(r + 1) * 128], Vb, vap,
                   first and r == 0, False, skip=True)

        def dens(Ob, br):
            dv = Ob.ap[:, 0:264].rearrange("p (r e) -> p r e", r=4)[:, :, 64]
            ts("dve", rden, rden.ap[:, br, :], Ob, dv, 1e-30, None, ALU.max)
            S.op("dve", lambda e: e.reciprocal(out=rden.ap[:, br, :], in_=rden.ap[:, br, :]), reads=[rden], writes=[rden])

        def attention_tile(t, lt):
            q0 = lt * 128
            imax = min((q0 + 96) // 16, NCB - 1)
            nct = imax // 128 + 1
            Oc, Ow, Os, psl = banks[2], banks[3], banks[4], banks[5]
            for g in range(2):
                qrhs = qT.ap[:, 4 * g:4 * g + 4, t * 128:(t + 1) * 128]
                for ct in range(nct):
                    Sb = sbank()
                    mm(Sb, Sb.ap[:].rearrange("p (r q) -> p r q", r=4), Kc, Kc.ap[:, g, ct * 128:(ct + 1) * 128], qT, qrhs, True, True)
                    P = nxt("pT", pTb)
                    bnd = q0 - 2048 * ct - 2032 - 31 < 0
                    if bnd:
                        ts("dve", Sb, Sb.ap[:], Sb, Sb.ap[:], 30.0, None, ALU.min)
                    act(P, P.ap[:], Sb, Sb.ap[:], AF.Exp)
                    if bnd:
                        pool_mask(P, q0 - 2048 * ct - 31, -16, 1, ALU.is_ge)
                    pv(Oc, ct == 0, P, Vc, Vc.ap[:, ct, g, 0:65])
                    for r in range(4):
                        mm(psl, psl.ap[:, r * 128:(r + 1) * 128], P, P.ap[:, r * 128:(r + 1) * 128], wimp, wimp.ap[:, ct, :],
                           ct == 0 and r == 0, False, skip=True)
                dens(Oc, 0)
                ts("dve", psc, psc.ap[:], psl, psl.ap[:, 0:128], rden.ap[:, 0, 0:1], None, ALU.mult, extra_r=[rden])
                for r in range(1, 4):
                    stt(psc, psc.ap[:], psl, psl.ap[:, r * 128:(r + 1) * 128], rden.ap[:, 0, r:r + 1], psc, psc.ap[:],
                        ALU.mult, ALU.add, extra_r=[rden])
                o = 128 - 2 * lt
                tt("dve", psc, psc.ap[:], psc, psc.ap[:], gmt, gmt.ap[:, o:o + 128], ALU.mult)
                tt("dve", psc, psc.ap[:], psc, psc.ap[:], gbt, gbt.ap[:, o:o + 128], ALU.add)
                tt("dve", psc, psc.ap[:], psc, psc.ap[:], f0t, f0t.ap[:], ALU.add)
                S.op("dve", lambda e: e.max(out=m8a.ap[:], in_=psc.ap[:]), reads=[psc], writes=[m8a])
                S.op("dve", lambda e: e.match_replace(out=psc2.ap[:], in_to_replace=m8a.ap[:], in_values=psc.ap[:], imm_value=-3.0e38),
                     reads=[psc, m8a], writes=[psc2])
                S.op("dve", lambda e: e.max(out=m8b.ap[:], in_=psc2.ap[:]), reads=[psc2], writes=[m8b])
                ts("dve", nsb, nsb.ap[:], psc, psc.ap[:], m8b.ap[:, 7:8], NEGBIG, ALU.is_lt, ALU.mult, extra_r=[m8b])
                tr(trb, trb_bf[:, 0:128], nsb, nsb.ap[:], identb)
                cp("act", nsT, nsT.ap[:], trb, trb_bf[:, 0:128])
                kts = [kt for kt in range(lt - 4, lt + 1) if kt >= 0]
                for idx, kt in enumerate(kts):
                    ws = kt % 8
                    Sb = sbank()
                    mm(Sb, Sb.ap[:].rearrange("p (r q) -> p r q", r=4), Kwin, Kwin.ap[:, g, ws * 128:(ws + 1) * 128], qT, qrhs, True, True)
                    P = nxt("pT", pTb)
                    act(P, P.ap[:], Sb, Sb.ap[:], AF.Exp)
                    if kt == lt:
                        pool_mask(P, 0, -1, 1, ALU.is_ge)
                    if kt == lt - 4:
                        pool_mask(P, 0, 1, -1, ALU.is_gt)
                    pv(Ow, idx == 0, P, Vwin, Vwin.ap[:, ws, g, 0:65])
                dens(Ow, 2)
                for kt in range(0, lt + 1):
                    Sb = sbank()
                    mm(Sb, Sb.ap[:].rearrange("p (r q) -> p r q", r=4), Ksel, Ksel.ap[:, g, kt * 128:(kt + 1) * 128], qT, qrhs, True, False)
                    for r in range(4):
                        mm(Sb, Sb.ap[:, r * 128:(r + 1) * 128], Tsel, Tsel.ap[:, kt * 128:(kt + 1) * 128], nsT, nsT.ap[:], False, r == 3)
                    P = nxt("pT", pTb)
                    act(P, P.ap[:], Sb, Sb.ap[:], AF.Exp)
                    if kt == lt:
                        pool_mask(P, 0, -1, 1, ALU.is_ge)
                    pv(Os, kt == 0, P, Vsel, Vsel.ap[:, kt, g, 0:65])
                dens(Os, 1)
                gv = gts.ap[:, t, g * 12:(g + 1) * 12].rearrange("p (r b) -> p r b", b=3)
                for br in range(3):
                    tt("dve", coef, coef.ap[:, br, :], rden, rden.ap[:, br, :], gts, gv[:, :, br], ALU.mult)
                if DBG:
                    r0 = (lt - NTC) * 128
                    out_events.append(S.dma("sp", dbg_cf[r0:r0 + 128, g, :], rden.ap[:].rearrange("p a b -> p (a b)"), reads=[rden]))
                    out_events.append(S.dma("sp", dbg_ps[r0:r0 + 128, g, :], psc.ap[:], reads=[psc]))
                acc = nxt("oa", oa32)
                tmp = nxt("oa", oa32)
                for br, Ob in ((0, Oc), (1, Os), (2, Ow)):
                    ov = Ob.ap[:, 0:264].rearrange("p (r e) -> p r e", r=4)[:, :, 0:64]
                    cb = coef.ap[:, br, :].unsqueeze(2).broadcast_to([128, 4, 64])
                    if br == 0:
                        tt("dve", acc, acc.ap[:], Ob, ov, coef, cb, ALU.mult)
                    else:
                        tt("dve", tmp, tmp.ap[:], Ob, ov, coef, cb, ALU.mult)
                        if br == 1:
                            tt("dve", acc, acc.ap[:], acc, acc.ap[:], tmp, tmp.ap[:], ALU.add)
                        else:
                            tt("dve", oab, oab.ap[:, g * 256:(g + 1) * 256].rearrange("p (r d) -> p r d", r=4),
                               acc, acc.ap[:], tmp, tmp.ap[:], ALU.add)
            if DBG:
                d32 = nxt("f32", f32t)
                cp("dve", d32, d32.ap[:, 0:512], oab, oab.ap[:])
                r0 = (lt - NTC) * 128
                out_events.append(S.dma("sp", dbg_oa[r0:r0 + 128, :], d32.ap[:, 0:512], reads=[d32]))
            for kc in range(4):
                tr(trb, trb_bf[:, 256 + kc * 128:384 + kc * 128], oab, oab.ap[:, kc * 128:(kc + 1) * 128], identb)
            for kc in range(4):
                cp("act" if kc % 2 else "dve", oaT[kc], oaT[kc].ap[:, t * 128:(t + 1) * 128], trb, trb_bf[:, 256 + kc * 128:384 + kc * 128])

        def mixer_block(nt, lt0):
            ntok = nt * 128
            for hq in range(2):
                slot, w = wload(s_winu[U_Q + hq], [sc[("win", U_Q + hq)]], [128, 8, 256])
                for hh in range(4):
                    h = 4 * hq + hh
                    bk = banks[hh % 4]
                    for kc in range(8):
                        mm(bk, bk.ap[0:64, 0:ntok], slot, w[:, kc, hh * 64:(hh + 1) * 64], xT, xT.ap[:, kc, 0:ntok], kc == 0, kc == 7)
                    S.op("act", lambda e, bk=bk, h=h: e.mul(qT.ap[0:64, h, 0:ntok], bk.ap[0:64, 0:ntok], 0.125), reads=[bk], writes=[qT])
            for h in range(8):
                S.dma("sp", qT.ap[64:69, h, 0:ntok], cd["c_qaug"][h, :, lt0 * 128:lt0 * 128 + ntok], writes=[qT])
            slot, w = wload(s_winu[U_G, :, :, 0:24], [sc[("win", U_G)]], [128, 8, 24])
            for t in range(nt):
                bk = banks[4 + t % 2]
                for kc in range(8):
                    mm(bk, bk.ap[:, 0:24], xT, xT.ap[:, kc, t * 128:(t + 1) * 128], slot, w[:, kc, :], kc == 0, kc == 7)
                act(gts, gts.ap[:, t, :], bk, bk.ap[:, 0:24], AF.Sigmoid)
            for uu in range(2):
                slot, w = wload(s_winu[U_U + uu], [sc[("win", U_U + uu)]], [128, 8, 256])
                for cc in range(2):
                    c = uu * 2 + cc
                    bk = banks[c % 4]
                    for kc in range(8):
                        mm(bk, bk.ap[:, 0:ntok], slot, w[:, kc, cc * 128:(cc + 1) * 128], xT, xT.ap[:, kc, 0:ntok], kc == 0, kc == 7)
                    act(uTc[c], uTc[c].ap[:, 0:ntok], bk, bk.ap[:, 0:ntok], AF.Gelu_apprx_tanh)
            wv = [wload(s_winu[U_V + i], [sc[("win", U_V + i)]], [128, 8, 256]) for i in range(2)]
            for t in range(nt):
                bk = banks[4 + t % 2]
                for i in range(2):
                    for kc in range(8):
                        mm(bk, bk.ap[:, i * 256:(i + 1) * 256], xT, xT.ap[:, kc, t * 128:(t + 1) * 128], wv[i][0], wv[i][1][:, kc, :], kc == 0, kc == 7)
                vf = nxt("f32", f32t)
                act(vf, vf.ap[:, 0:512], bk, bk.ap[:], AF.Gelu_apprx_tanh)
                ln_small(vf, vf.ap[:, 0:512], 512, glnp.ap[:, 0, :], glnp.ap[:, 1, :], vnb, vnb.ap[:, t, :], LN_EPS)
                gb_ = banks[6]
                for hg in range(4):
                    mm(gb_, gb_.ap[:, hg * 128:(hg + 1) * 128], vnb, vnb.ap[:, t, hg * 128:(hg + 1) * 128], wsT, wsT.ap[:, hg, :], True, False)
                    mm(gb_, gb_.ap[:, hg * 128:(hg + 1) * 128], onesrow, onesrow.ap[0:1, :], bsrow, bsrow.ap[0:1, hg, :], False, True)
                for hg in range(4):
                    tt("dve", zT[hg], zT[hg].ap[:, t * 128:(t + 1) * 128], gb_, gb_.ap[:, hg * 128:(hg + 1) * 128],
                       uTc[hg], uTc[hg].ap[:, t * 128:(t + 1) * 128], ALU.mult)
            if DBG:
                for hg in range(4):
                    d32 = nxt("f32", f32t)
                    cp("dve", d32, d32.ap[:, 0:ntok], zT[hg], zT[hg].ap[:, 0:ntok])
                    c0 = (lt0 - NTC) * 128
                    out_events.append(S.dma("sp", dbg_zT[hg * 128:(hg + 1) * 128, c0:c0 + ntok], d32.ap[:, 0:ntok], reads=[d32]))
            for t in range(nt):
                attention_tile(t, lt0 + t)
            merge_tail(nt)

        def merge_tail(nt):
            ntok = nt * 128
            for c in range(8):
                cc = c % 2
                wga = wload(s_winu[U_GA + c // 2, :, :, cc * 128:(cc + 1) * 128], [sc[("win", U_GA + c // 2)]], [128, 8, 128])
                wgb = wload(s_winu[U_GB + c // 2, :, :, cc * 128:(cc + 1) * 128], [sc[("win", U_GB + c // 2)]], [128, 8, 128])
                wa_ = wload(s_wa[c], [sc[("wa", c)]], [128, 4, 128])
                wb_ = wload(s_wb[c], [sc[("wb", c)]], [128, 4, 128])
                bA, bG, bB, bH = banks[0], banks[1], banks[2], banks[3]
                for kc in range(8):
                    mm(bG, bG.ap[:, 0:ntok], wga[0], wga[1][:, kc, :], xT, xT.ap[:, kc, 0:ntok], kc == 0, kc == 7)
                for kc in range(4):
                    mm(bA, bA.ap[:, 0:ntok], wa_[0], wa_[1][:, kc, :], oaT[kc], oaT[kc].ap[:, 0:ntok], kc == 0, kc == 3)
                for kc in range(8):
                    mm(bH, bH.ap[:, 0:ntok], wgb[0], wgb[1][:, kc, :], xT, xT.ap[:, kc, 0:ntok], kc == 0, kc == 7)
                for kc in range(4):
                    mm(bB, bB.ap[:, 0:ntok], wb_[0], wb_[1][:, kc, :], zT[kc], zT[kc].ap[:, 0:ntok], kc == 0, kc == 3)
                sa = nxt("silu", silu_t)
                act(sa, sa.ap[:, 0:ntok], bG, bG.ap[:, 0:ntok], AF.Sigmoid)
                m1 = nxt("f32", f32t)
                tt("dve", m1, m1.ap[:, 0:ntok], bA, bA.ap[:, 0:ntok], sa, sa.ap[:, 0:ntok], ALU.mult)
                sb_ = nxt("silu", silu_t)
                act(sb_, sb_.ap[:, 0:ntok], bH, bH.ap[:, 0:ntok], AF.Sigmoid)
                tt("dve", sb_, sb_.ap[:, 0:ntok], bB, bB.ap[:, 0:ntok], sb_, sb_.ap[:, 0:ntok], ALU.mult)
                tt("dve", mT[c], mT[c].ap[:, 0:ntok], m1, m1.ap[:, 0:ntok], sb_, sb_.ap[:, 0:ntok], ALU.add)
            for n in range(2):
                for kc in range(8):
                    slot, w = wload(s_wo[kc * 128:(kc + 1) * 128, n * 512:(n + 1) * 512], [sc[("wo", kc)]], [128, 512])
                    for t in range(nt):
                        bk = banks[4 + t]
                        mm(bk, bk.ap[:], mT[kc], mT[kc].ap[:, t * 128:(t + 1) * 128], slot, w, kc == 0, kc == 7)
                for t in range(nt):
                    bk = banks[4 + t]
                    stt(xres[t], xres[t].ap[:, n * 512:(n + 1) * 512], xres[t], xres[t].ap[:, n * 512:(n + 1) * 512],
                        ALPHA, bk, bk.ap[:], ALU.mult, ALU.add)
            for t in range(nt):
                layer_norm_tile(xres[t], 2, LN_EPS)

        def sample_phase():
            from concourse.bass import IndirectOffsetOnAxis
            past = NPG * 128
            S.barrier()
            ALIAS = NTL >= 64 or bool(_os0.environ.get("KALIAS"))
            pools_ = {"v": [Vsel.ap[:].rearrange("p a b c -> p (a b c)"), 0, NTL * 2 * 66],
                      "n": [vnb.ap[:].rearrange("p a b -> p (a b)"), 0, 4 * 512],
                      "k": [Ksel.ap[:, 0, :], 0, NPOS], "k1": [Ksel.ap[:, 1, :], 0, NPOS]}

            def carve(name, shape, dt, where):
                if not ALIAS:
                    return sb(name, shape, dt)
                esz = 2 if dt == BF16 else 4
                n = 1
                for s_ in shape[1:]:
                    n *= s_
                nb16 = (n * esz + 3) // 4 * 2
                for key in where:
                    flat, off, cap = pools_[key]
                    if off + nb16 <= cap:
                        pools_[key][1] = off + nb16
                        v = flat[0:shape[0], off:off + nb16]
                        if dt != BF16:
                            v = v.bitcast(dt)
                        v = v[:, 0:n]
                        if len(shape) == 3:
                            v = v.rearrange("p (a b) -> p a b", a=shape[1])
                        elif len(shape) == 4:
                            v = v.rearrange("p (a b c) -> p a b c", a=shape[1], b=shape[2])
                        return Buf(name, v)
                return sb(name, shape, dt)

            qS = carve("qS", [69, 8, 128], BF16, ["k"])
            gS = sb("gS", [128, 1, 24], F32)
            Kcs = carve("Kcs", [69, 2, NCTS * 128], BF16, ["k"])
            Vcs = carve("Vcs", [128, NCTS, 2, 66], BF16, ["v", "n"])
            pg = [carve("pg%d" % i, [128, 2, 128], F32, ["v", "n"]) for i in range(2)]
            pgb = [carve("pgb%d" % i, [128, 256], BF16, ["v", "n"]) for i in range(2)]
            ptb = sb("ptb", [128, NPG], I32)
            idx = sb("idx", [128, NPG], I32)
            piota = sb("piota", [128, 1], F32)
            Kr = [carve("Kr%d" % i, [69, 2, 128], BF16, ["k"]) for i in range(3)]
            Vr = [carve("Vr%d" % i, [128, 2, 66], BF16, ["v", "n"]) for i in range(3)]
            wk32 = [carve("wk32", [128, 4, 128], F32, ["v", "n"]), carve("wv32", [128, 4, 128], F32, ["v", "n"])]
            pscs = carve("pscs", [128, 264], F32, ["v", "n"])
            pscs2 = carve("pscs2", [128, 264], F32, ["v", "n"])
            nsbs = carve("nsbs", [128, 384], BF16, ["v", "n"])
            nsTs = carve("nsTs", [128, 3, 128], BF16, ["v", "n"])
            smb = carve("smb", [128, 264], F32, ["v", "n"])
            sbb = carve("sbb", [128, 264], F32, ["v", "n"])
            w0 = sb("w0", [128, 33], BF16)
            oabS = oab
            rr.update({"pg": 0, "Kr": 0, "Vr": 0})
            S.dma("sp", piota.ap[:], scd["c_piota"], writes=[piota])
            S.dma("sp", smb.ap[:], scd["c_smb"], writes=[smb])
            S.dma("sp", sbb.ap[:], scd["c_sbb"], writes=[sbb])
            S.dma("sp", w0.ap[:], scd["c_w0"], writes=[w0])
            S.op("dve", lambda e: e.memset(Kcs.ap[:], 0.0), writes=[Kcs])
            S.op("dve", lambda e: e.memset(Vcs.ap[:], 1.0), writes=[Vcs])
            S.op("dve", lambda e: e.memset(qS.ap[:], 0.0), writes=[qS])
            S.op("dve", lambda e: e.memset(gS.ap[:], 0.0), writes=[gS])
            S.op("dve", lambda e: e.memset(oabS.ap[:], 0.0), writes=[oabS])
            S.op("dve", lambda e: e.memset(nsbs.ap[:], NEGBIG), writes=[nsbs])
            for v_ in Vr:
                S.op("dve", lambda e, v_=v_: e.memset(v_.ap[:], 1.0), writes=[v_])
            for g in range(2):
                S.dma("sp", Kcs.ap[64:69, g, :], scd["c_skcaug"], writes=[Kcs])
            for h in range(8):
                S.dma("sp", qS.ap[64:69, h, :], scd["c_sqaug"][h], writes=[qS])
            wb4 = sb("wb4", [128, 8], F32)
            for hg in range(4):
                S.dma("sp", wb4.ap[:, hg:hg + 1], wd["spatial_w"][hg, 0, 0:1].partition_broadcast(128), writes=[wb4])
                S.dma("sp", wb4.ap[:, 4 + hg:5 + hg], wd["spatial_b"][hg, 0:1].partition_broadcast(128), writes=[wb4])

            xs = xres[0]
            S.dma("sp", xs.ap[:], x_smp, writes=[xs])
            transpose_tile(xs, 0)
            ffn_block(0, 1, 0)
            transpose_tile(xs, 0)
            wl = [wload(s_winu[u], [sc[("win", u)]], [128, 8, 256]) for u in (U_KV, U_KV + 1, U_KV + 2)]
            bkA, bkB = banks[4], banks[5]
            for ui, (slot, w) in enumerate(wl):
                bk, o0 = (bkA, ui * 256) if ui < 2 else (bkB, 0)
                for kc in range(8):
                    mm(bk, bk.ap[:, o0:o0 + 256], xT, xT.ap[:, kc, 0:128], slot, w[:, kc, :], kc == 0, kc == 7)
            kvst = f32t[0]
            cp("act", kvst, kvst.ap[:, 0:512], bkA, bkA.ap[:])
            cp("act", kvst, kvst.ap[:, 512:768], bkB, bkB.ap[:, 0:256])
            out_events.append(S.dma("act", kv_s[:, :], kvst.ap[:, 0:768], reads=[kvst]))
            for hq in range(2):
                slot, w = wload(s_winu[U_Q + hq], [sc[("win", U_Q + hq)]], [128, 8, 256])
                for hh in range(4):
                    h = 4 * hq + hh
                    bk = banks[hh % 4]
                    for kc in range(8):
                        mm(bk, bk.ap[0:64, 0:128], slot, w[:, kc, hh * 64:(hh + 1) * 64], xT, xT.ap[:, kc, 0:128], kc == 0, kc == 7)
                    S.op("act", lambda e, bk=bk, h=h: e.mul(qT.ap[0:64, h, 0:128], bk.ap[0:64, 0:128], 0.125), reads=[bk], writes=[qT])
            slot, w = wload(s_winu[U_G, :, :, 0:24], [sc[("win", U_G)]], [128, 8, 24])
            bk = banks[6]
            for kc in range(8):
                mm(bk, bk.ap[:, 0:24], xT, xT.ap[:, kc, 0:128], slot, w[:, kc, :], kc == 0, kc == 7)
            act(gts, gts.ap[:, 0, :], bk, bk.ap[:, 0:24], AF.Sigmoid)
            uv = []
            for which, U0 in ((0, U_U), (1, U_V)):
                wv = [wload(s_winu[U0 + i], [sc[("win", U0 + i)]], [128, 8, 256]) for i in range(2)]
                bk = banks[4 + which]
                for i in range(2):
                    for kc in range(8):
                        mm(bk, bk.ap[:, i * 256:(i + 1) * 256], xT, xT.ap[:, kc, 0:128], wv[i][0], wv[i][1][:, kc, :], kc == 0, kc == 7)
                dst = f32t[1] if which == 0 else kvst
                if which == 1:
                    pass
                uv.append(dst)
                if which == 0:
                    act(dst, dst.ap[:, 0:512], bk, bk.ap[:], AF.Gelu_apprx_tanh)
            vf = wk32[0]
            vfa = vf.ap[:].rearrange("p a b -> p (a b)")
            act(vf, vfa, banks[5], banks[5].ap[:], AF.Gelu_apprx_tanh)
            vn32 = wk32[1]
            vna = vn32.ap[:].rearrange("p a b -> p (a b)")
            ln_small(vf, vfa, 512, glnp.ap[:, 0, :], glnp.ap[:, 1, :], vn32, vna, LN_EPS)
            out_events.append(S.dma("act", chunkv[:, :], vna, reads=[vn32]))
            for hg in range(4):
                ts("dve", vn32, vna[:, hg * 128:(hg + 1) * 128], vn32, vna[:, hg * 128:(hg + 1) * 128],
                   wb4.ap[:, hg:hg + 1], wb4.ap[:, 4 + hg:5 + hg], ALU.mult, ALU.add, extra_r=[wb4])
            zs = nxt("pT", pTb)
            tt("dve", zs, zs.ap[:], vn32, vna, f32t[1], f32t[1].ap[:, 0:512], ALU.mult)
            for kc in range(4):
                tr(trb, trb_bf[:, kc * 128:(kc + 1) * 128], zs, zs.ap[:, kc * 128:(kc + 1) * 128], identb)
            for kc in range(4):
                cp("act" if kc % 2 else "dve", zT[kc], zT[kc].ap[:, 0:128], trb, trb_bf[:, kc * 128:(kc + 1) * 128])

            for i in range(4):
                S.dma("sp", ptb.ap[:], ptab[i].partition_broadcast(128), writes=[ptb])
                ts("dve", idx, idx.ap[:], ptb, ptb.ap[:], 128.0, piota.ap[:, 0:1], ALU.mult, ALU.add, extra_r=[piota])

                def gather(dst, dap, pool_i, s_):
                    return S.dma("pool", None, None, reads=[idx], writes=[dst],
                                 fn=lambda e: e.indirect_dma_start(out=dap, out_offset=None, in_=pools[pool_i],
                                                                   in_offset=IndirectOffsetOnAxis(ap=idx.ap[:, s_:s_ + 1], axis=0)))

                for kvi in range(2):
                    wt = wk32[kvi]
                    S.dma("sp", wt.ap[:, 0:3, :], cwin[kvi][i, 1:385, :].rearrange("(t p) c -> p t c", p=128), writes=[wt])
                    S.dma("sp", wt.ap[0:127, 3, :], cwin[kvi][i, 385:512, :], writes=[wt])
                    c0 = 512 + kvi * 128
                    S.dma("sp", wt.ap[127:128, 3, :], kvst.ap[i:i + 1, c0:c0 + 128], reads=[kvst], writes=[wt])
                    out_events.append(S.dma("act", swin[kvi][i].rearrange("(t p) c -> p t c", p=128), wt.ap[:], reads=[wt]))
                cp("act", qS, qS.ap[0:64, :, 0:1], qT, qT.ap[0:64, :, i:i + 1])
                S.dma("sp", gS.ap[0:1, 0, :], gts.ap[i:i + 1, 0, :], reads=[gts], writes=[gS])
                S.op("dve", lambda e: e.memset(XT2.ap[:], 0.0), writes=[XT2])
                for s0 in range(0, NPG, 4):
                    npg = min(4, NPG - s0)
                    for t in range(npg):
                        p_ = nxt("pg", pg)
                        gather(p_, p_.ap[:, 0, :], 0, s0 + t)
                        gather(p_, p_.ap[:, 1, :], 1, s0 + t)
                        pb_ = nxt("pg", pgb)
                        cp("dve", pb_, pb_.ap[:], p_, p_.ap[:].rearrange("p a b -> p (a b)"))
                        xb_ = banks[(2 * t) % 4]
                        for kvg in range(4):
                            for par in range(2):
                                mm(xb_, xb_.ap[par * 64:(par + 1) * 64, kvg * 64:(kvg + 1) * 64], pb_, pb_.ap[:, kvg * 64:(kvg + 1) * 64],
                                   selpair, selpair.ap[:, par, :], True, True)
                        cp("dve", XT2, XT2.ap[:, :, 8 + t * 64:8 + (t + 1) * 64], xb_, xb_.ap[:, 0:256].rearrange("p (a m) -> p a m", a=4))
                    compress_block(npg, s0, Kcs, Vcs)
                sample_attention(i, qS, gS, Kcs, Vcs, Kr, Vr, wk32, pg, pgb, pscs, pscs2, nsbs, nsTs, smb, sbb, w0, oabS, gather, kvst)
            for kc in range(4):
                tr(trb, trb_bf[:, kc * 128:(kc + 1) * 128], oabS, oabS.ap[:, kc * 128:(kc + 1) * 128], identb)
            for kc in range(4):
                cp("act" if kc % 2 else "dve", oaT[kc], oaT[kc].ap[:, 0:128], trb, trb_bf[:, kc * 128:(kc + 1) * 128])
            merge_tail(1)
            transpose_tile(xs, 0)
            ffn_block(1, 1, 4)
            out_events.append(S.dma("act", y_s[:, :], xs.ap[:], reads=[xs]))

        def sample_attention(i, qS, gS, Kcs, Vcs, Kr, Vr, wk32, pg, pgb, pscs, pscs2, nsbs, nsTs, smb, sbb, w0, oabS, gather, kvst):
            past = NPG * 128
            q0 = past
            Oc, Ow, Os = banks[2], banks[3], banks[4]
            pslb = [banks[5], banks[6], banks[3], banks[4]]
            kaug = scd["c_skaug"]
            for g in range(2):
                qrhs = qS.ap[:, 4 * g:4 * g + 4, :]
                for ct in range(NCTS):
                    Sb = sbank()
                    mm(Sb, Sb.ap[:].rearrange("p (r q) -> p r q", r=4), Kcs, Kcs.ap[:, g, ct * 128:(ct + 1) * 128], qS, qrhs, True, True)
                    P = nxt("pT", pTb)
                    bnd = q0 - 2048 * ct - 2032 - 31 < 0
                    if bnd:
                        ts("dve", Sb, Sb.ap[:], Sb, Sb.ap[:], 30.0, None, ALU.min)
                    act(P, P.ap[:], Sb, Sb.ap[:], AF.Exp)
                    if bnd:
                        pool_mask(P, q0 - 2048 * ct - 31, -16, 1, ALU.is_ge)
                    pv(Oc, ct == 0, P, Vcs, Vcs.ap[:, ct, g, 0:65])
                    for r in range(4):
                        mm(pslb[r], pslb[r].ap[:, 32 * ct:32 * ct + 33], P, P.ap[:, r * 128:(r + 1) * 128], w0, w0.ap[:], ct == 0, False, skip=True)
                dens(Oc, 0)
                NJ_ = NJS
                ts("dve", pscs, pscs.ap[:, 0:NJ_], pslb[0], pslb[0].ap[:, 0:NJ_], rden.ap[:, 0, 0:1], None, ALU.mult, extra_r=[rden])
                for r in range(1, 4):
                    stt(pscs, pscs.ap[:, 0:NJ_], pslb[r], pslb[r].ap[:, 0:NJ_], rden.ap[:, 0, r:r + 1], pscs, pscs.ap[:, 0:NJ_],
                        ALU.mult, ALU.add, extra_r=[rden])
                tt("dve", pscs, pscs.ap[:, 0:NJ_], pscs, pscs.ap[:, 0:NJ_], smb, smb.ap[:, 0:NJ_], ALU.mult)
                tt("dve", pscs, pscs.ap[:, 0:NJ_], pscs, pscs.ap[:, 0:NJ_], sbb, sbb.ap[:, 0:NJ_], ALU.add)
                S.op("dve", lambda e: e.max(out=m8a.ap[:], in_=pscs.ap[:, 0:NJ_]), reads=[pscs], writes=[m8a])
                S.op("dve", lambda e: e.match_replace(out=pscs2.ap[:, 0:NJ_], in_to_replace=m8a.ap[:], in_values=pscs.ap[:, 0:NJ_], imm_value=-3.0e38),
                     reads=[pscs, m8a], writes=[pscs2])
                S.op("dve", lambda e: e.max(out=m8b.ap[:], in_=pscs2.ap[:, 0:NJ_]), reads=[pscs2], writes=[m8b])
                ts("dve", nsbs, nsbs.ap[:, 0:NJ_], pscs, pscs.ap[:, 0:NJ_], m8b.ap[:, 7:8], NEGBIG, ALU.is_lt, ALU.mult, extra_r=[m8b])
                for ch in range(3):
                    tr(trb, trb_bf[:, ch * 128:(ch + 1) * 128], nsbs, nsbs.ap[:, ch * 128:(ch + 1) * 128], identb)
                cp("act", nsTs, nsTs.ap[:].rearrange("p a b -> p (a b)"), trb, trb_bf[:, 0:384])
                for wtile in range(4):
                    kb = nxt("pg", pgb)
                    cp("dve", kb, kb.ap[:, 0:128], wk32[0], wk32[0].ap[:, wtile, :])
                    tr(trb, trb_bf[0:64, 512:640], kb, kb.ap[:, g * 64:(g + 1) * 64], identb)
                    K_ = nxt("Kr", Kr)
                    cp("act", K_, K_.ap[0:64, g, :], trb, trb_bf[0:64, 512:640])
                    S.dma("sp", K_.ap[64:69, g, :], scd["c_swaug"][:, wtile * 128:(wtile + 1) * 128], writes=[K_])
                    V_ = nxt("Vr", Vr)
                    cp("act", V_, V_.ap[:, g, 0:64], wk32[1], wk32[1].ap[:, wtile, g * 64:(g + 1) * 64])
                    Sb = sbank()
                    mm(Sb, Sb.ap[:].rearrange("p (r q) -> p r q", r=4), K_, K_.ap[:, g, :], qS, qrhs, True, True)
                    P = nxt("pT", pTb)
                    act(P, P.ap[:], Sb, Sb.ap[:], AF.Exp)
                    pv(Ow, wtile == 0, P, V_, V_.ap[:, g, 0:65])
                dens(Ow, 2)
                for kt in range(NPG + 1):
                    K_ = nxt("Kr", Kr)
                    V_ = nxt("Vr", Vr)
                    if kt < NPG:
                        p_ = nxt("pg", pg)
                        gather(p_, p_.ap[:, 0, :], 2, kt)
                        gather(p_, p_.ap[:, 1, :], 3, kt)
                        kb = nxt("pg", pgb)
                        cp("dve", kb, kb.ap[:, 0:64], p_, p_.ap[:, 0, g * 64:(g + 1) * 64])
                        cp("act", V_, V_.ap[:, g, 0:64], p_, p_.ap[:, 1, g * 64:(g + 1) * 64])
                    else:
                        kb = nxt("pg", pgb)
                        S.op("dve", lambda e, kb=kb: e.memset(kb.ap[:, 0:64], 0.0), writes=[kb])
                        S.op("dve", lambda e, V_=V_: e.memset(V_.ap[:, g, 0:64], 0.0), writes=[V_])
                        c0 = 256 + g * 64
                        S.dma("sp", kb.ap[0:1, 0:64], kvst.ap[i:i + 1, c0:c0 + 64], reads=[kvst], writes=[kb]) if False else None
                        nb16 = nxt("pT", pTb)
                        cp("dve", nb16, nb16.ap[:, 0:256], kvst, kvst.ap[:, 256:512])
                        S.dma("sp", kb.ap[0:1, 0:64], nb16.ap[i:i + 1, g * 64:(g + 1) * 64], reads=[nb16], writes=[kb])
                        S.dma("sp", V_.ap[0:1, g, 0:64], nb16.ap[i:i + 1, 128 + g * 64:192 + g * 64], reads=[nb16], writes=[V_])
                    tr(trb, trb_bf[0:64, 512:640], kb, kb.ap[:, 0:64], identb)
                    cp("act", K_, K_.ap[0:64, g, :], trb, trb_bf[0:64, 512:640])
                    S.dma("sp", K_.ap[64:69, g, :], kaug[:, kt * 128:(kt + 1) * 128], writes=[K_])
                    Sb = sbank()
                    mm(Sb, Sb.ap[:].rearrange("p (r q) -> p r q", r=4), K_, K_.ap[:, g, :], qS, qrhs, True, False)
                    ch, tc = (2 * kt) // 128, (kt % 64) * 128
                    for r in range(4):
                        mm(Sb, Sb.ap[:, r * 128:(r + 1) * 128], Tsel, Tsel.ap[:, tc:tc + 128], nsTs, nsTs.ap[:, ch, :], False, r == 3)
                    P = nxt("pT", pTb)
                    act(P, P.ap[:], Sb, Sb.ap[:], AF.Exp)
                    if kt == NPG:
                        pool_mask(P, 0, -1, 1, ALU.is_ge)
                    pv(Os, kt == 0, P, V_, V_.ap[:, g, 0:65])
                dens(Os, 1)
                gv = gS.ap[:, 0, g * 12:(g + 1) * 12].rearrange("p (r b) -> p r b", b=3)
                for br in range(3):
                    tt("dve", coef, coef.ap[:, br, :], rden, rden.ap[:, br, :], gS, gv[:, :, br], ALU.mult)
                acc = nxt("oa", oa32)
                tmp = nxt("oa", oa32)
                orow = nxt("pT", pTb)
                for br, Ob in ((0, Oc), (1, Os), (2, Ow)):
                    ov = Ob.ap[:, 0:264].rearrange("p (r e) -> p r e", r=4)[:, :, 0:64]
                    cb = coef.ap[:, br, :].unsqueeze(2).broadcast_to([128, 4, 64])
                    if br == 0:
                        tt("dve", acc, acc.ap[:], Ob, ov, coef, cb, ALU.mult)
                    else:
                        tt("dve", tmp, tmp.ap[:], Ob, ov, coef, cb, ALU.mult)
                        if br == 1:
                            tt("dve", acc, acc.ap[:], acc, acc.ap[:], tmp, tmp.ap[:], ALU.add)
                        else:
                            tt("dve", orow, orow.ap[:, 0:256].rearrange("p (r d) -> p r d", r=4), acc, acc.ap[:], tmp, tmp.ap[:], ALU.add)
                S.dma("sp", oabS.ap[i:i + 1, g * 256:(g + 1) * 256], orow.ap[0:1, 0:256], reads=[orow], writes=[oabS])

        import os as _os
        CUT = int(_os.environ.get("KCUT", "99"))
        KSUB = int(_os.environ.get("KSUB", "255"))

        def load_block(src, row0, nt):
            for t in range(nt):
                S.dma("sp", xres[t].ap[:], src[row0 + t * 128:row0 + (t + 1) * 128, :], writes=[xres[t]])
                transpose_tile(xres[t], t)

        for b in range(NBC if CUT >= 1 else 0):
            load_block(x_ctx, b * 512, 4)
            if CUT >= 2:
                ffn_block(0, 4, 0)
            for t in range(4):
                transpose_tile(xres[t], t)
            if CUT >= 3:
                kv_project(4, b * 4, False, 0, b == NBC - 1)
            if CUT >= 4:
                compress_block(4, b * 4)

        for b in range(NBO):
            load_block(x_own, b * 512, 4)
            if CUT >= 2:
                ffn_block(0, 4, 0)
            for t in range(4):
                transpose_tile(xres[t], t)
            if CUT >= 3:
                kv_project(4, NTC + b * 4, True, b * 512, True)
            if CUT >= 4:
                compress_block(4, NTC + b * 4)
            if stage >= 2:
                mixer_block(4, NTC + b * 4)
            if stage >= 3:
                for t in range(4):
                    transpose_tile(xres[t], t)
                ffn_block(1, 4, 4)
            for t in range(4):
                out_events.append(S.dma("act", y_own[b * 512 + t * 128:b * 512 + (t + 1) * 128, :], xres[t].ap[:], reads=[xres[t]]))

        if NPG:
            sample_phase()
        S.finish(out_events)
        S.emit()
    return nc


def sample_consts(NPG):
    import ml_dtypes
    bf = ml_dtypes.bfloat16
    past = NPG * 128
    c = {}
    c["c_piota"] = np.arange(128, dtype=np.float32).reshape(128, 1)
    npos = (NPG + 1) * 128
    pos = np.arange(npos)
    ka = np.zeros((5, npos), np.float32)
    ka[0] = pos % 128
    ka[1] = 1.0
    ka[2] = pos // 128
    ka[3] = 1.0
    c["c_skaug"] = ka.astype(bf)
    NCS = NPG * 8
    NCTS = (NCS + 127) // 128
    ci = np.arange(NCTS * 128)
    kc = np.zeros((5, NCTS * 128), np.float32)
    kc[0] = 16 * (ci % 128)
    kc[1] = 1.0
    kc[2] = 16 * (ci // 128)
    kc[3] = 1.0
    kc[4] = 31.0
    c["c_skcaug"] = kc.astype(bf)
    w = np.arange(512)
    kw = np.zeros((5, 512), np.float32)
    kw[0] = (w % 128) + 1
    kw[1] = 1.0
    kw[2] = (past - 512) // 128 + w // 128
    kw[3] = 1.0
    c["c_swaug"] = kw.astype(bf)
    qa = np.zeros((8, 5, 128), np.float32)
    qr = np.arange(128)
    for h in range(8):
        s = SLOPES[h]
        qa[h, 0] = s
        qa[h, 1] = -s * qr
        qa[h, 2] = s * 128
        qa[h, 3] = -s * 128 * NPG
        qa[h, 4] = s
    c["c_sqaug"] = qa.astype(bf)
    W0 = np.zeros((128, 33), np.float32)
    for ir in range(128):
        for jj in range(33):
            if 4 * jj - 1 <= ir <= 4 * jj + 3:
                W0[ir, jj] = 1.0
    c["c_w0"] = W0.astype(bf)
    NJS = 2 * NPG + 1
    smb = np.zeros((128, 264), np.float32)
    sbb = np.zeros((128, 264), np.float32)
    smb[:, :NJS] = 1.0
    for j in (0, NJS - 1, NJS - 2):
        smb[:, j] = 0.0
        sbb[:, j] = 1.0e4
    c["c_smb"] = smb
    c["c_sbb"] = sbb
    return c


_CONST_CACHE = {}


def _consts(ntl, fv):
    key = (ntl, fv)
    if key not in _CONST_CACHE:
        _CONST_CACHE[key] = host_consts(ntl, fv, 8192)
    return _CONST_CACHE[key]


def kernel(**inp):
    xp = np.asarray(inp["x_prompt"], dtype=np.float32)
    xs = np.asarray(inp["x_sample"], dtype=np.float32)
    B, T, _ = xp.shape
    DB = xs.shape[0]
    half_t = T // 2
    nb = half_t // 512
    ptab = np.asarray(inp["page_table"]).astype(np.int32)
    npg = ptab.shape[1]
    pool_names = ("cache_cmp_k", "cache_cmp_v", "cache_sel_k", "cache_sel_v")
    nphys = np.asarray(inp[pool_names[0]]).shape[0]
    pools = [np.ascontiguousarray(np.asarray(inp[n], dtype=np.float32)).reshape(nphys * 128, 128) for n in pool_names]
    cwk = np.asarray(inp["cache_win_k"], dtype=np.float32).reshape(DB, 512, 128)
    cwv = np.asarray(inp["cache_win_v"], dtype=np.float32).reshape(DB, 512, 128)
    nc = build(nb, nb, 3, NPG=npg, NPHYS=nphys)
    weights = {n: np.ascontiguousarray(np.asarray(inp[n], dtype=np.float32)) for n in W_NAMES}
    sconst = sample_consts(npg)
    in_maps = []
    for c in range(8):
        b, half = divmod(c, 2)
        m = dict(weights)
        m.update(_consts(2 * nb * 4, nb * 4 if half == 0 else 0))
        m.update(sconst)
        m["x_ctx"] = np.ascontiguousarray(xp[b, 0:half_t])
        m["x_own"] = np.ascontiguousarray(xp[b, half * half_t:(half + 1) * half_t])
        xsm = np.zeros((128, D), np.float32)
        xsm[0:4] = xs[4 * c:4 * c + 4, 0]
        m["x_smp"] = xsm
        for i in range(4):
            m["pool%d" % i] = pools[i]
        m["cwin_k"] = np.ascontiguousarray(cwk[4 * c:4 * c + 4])
        m["cwin_v"] = np.ascontiguousarray(cwv[4 * c:4 * c + 4])
        m["ptab"] = np.ascontiguousarray(ptab[4 * c:4 * c + 4])
        in_maps.append(m)
    res = run_bass_kernel_spmd(nc, in_maps, core_ids=list(range(8))).results
    y_prompt = np.stack([np.concatenate([res[2 * b]["y_own"], res[2 * b + 1]["y_own"]], axis=0) for b in range(B)])
    kv = np.stack([np.concatenate([res[2 * b]["kv_own"], res[2 * b + 1]["kv_own"]], axis=0) for b in range(B)])
    parts = [np.ascontiguousarray(kv[:, :, i * 128:(i + 1) * 128]).reshape(B, T, 2, 64) for i in range(6)]
    p_cmp_k, p_cmp_v, p_sel_k, p_sel_v, kw, vw = parts
    p_win_k = np.ascontiguousarray(kw[:, -512:])
    p_win_v = np.ascontiguousarray(vw[:, -512:])
    y_sample = np.concatenate([res[c]["y_s"][0:4] for c in range(8)], axis=0).reshape(DB, 1, D)
    kvs = np.concatenate([res[c]["kv_s"][0:4] for c in range(8)], axis=0)
    sp = [np.ascontiguousarray(kvs[:, i * 128:(i + 1) * 128]).reshape(DB, 1, 2, 64) for i in range(4)]
    s_win_k = np.concatenate([res[c]["swin_k"] for c in range(8)], axis=0).reshape(DB, 512, 2, 64)
    s_win_v = np.concatenate([res[c]["swin_v"] for c in range(8)], axis=0).reshape(DB, 512, 2, 64)
    s_chunk_v = np.concatenate([res[c]["chunkv"][0:4] for c in range(8)], axis=0).reshape(DB, 1, 512)
    f = lambda a: np.ascontiguousarray(a, dtype=np.float32)
    return (f(y_prompt), f(y_sample), p_cmp_k, p_cmp_v, p_sel_k, p_sel_v, p_win_k, p_win_v,
            sp[0], sp[1], sp[2], sp[3], f(s_win_k), f(s_win_v), f(s_chunk_v))
```

```python
import contextlib
import numpy as np
import concourse.bass as bass
import concourse.mybir as mybir
from concourse.bass_utils import run_bass_kernel_spmd

F32 = mybir.dt.float32
FP8 = mybir.dt.float8e4
BF16 = mybir.dt.bfloat16
I32 = mybir.dt.int32
U32 = mybir.dt.uint32
AF = mybir.ActivationFunctionType
ALU = mybir.AluOpType
AX = mybir.AxisListType


class Buf:
    __slots__ = ("name", "ap", "w", "r", "excl")

    def __init__(self, name, ap=None, excl=False):
        self.name = name
        self.ap = ap
        self.excl = excl
        self.w = {}
        self.r = {}


class Sched:
    import os as _o
    EPOCH = int(_o.environ.get("KEPOCH", "16000"))
    NDMA = 14

    def __init__(self, nc, stack):
        self.nc = nc
        self.st = stack
        self.engs = {"pe": nc.tensor, "act": nc.scalar, "dve": nc.vector, "pool": nc.gpsimd, "sp": nc.sync}
        self.prog = {k: [] for k in self.engs}
        self.cnt = {k: 0 for k in self.engs}
        self.seen = {k: {} for k in self.engs}
        self.esem = {}
        self.dsem = {k: [] for k in self.engs}
        self.dnext = {k: 0 for k in self.engs}
        self.nsem = 0

    def sb(self, name, shape, dt):
        t = self.st.enter_context(self.nc.sbuf_tensor(name, list(shape), dt))
        return Buf(name, t.ap() if hasattr(t, "ap") and callable(getattr(t, "ap")) else t)

    def ps(self, name, shape, dt):
        t = self.st.enter_context(self.nc.psum_tensor(name, list(shape), dt))
        return Buf(name, t.ap() if hasattr(t, "ap") and callable(getattr(t, "ap")) else t, excl=True)

    def _sem(self, name):
        self.nsem += 1
        return self.st.enter_context(self.nc.semaphore(name))

    def _eng_sem(self, e, epoch):
        key = (e, epoch)
        if key not in self.esem:
            self.esem[key] = self._sem("s_%s_%d" % (e, epoch))
        return self.esem[key]

    def _waits(self, e, deps):
        out = []
        for ev in deps:
            if ev is None:
                continue
            if ev[0] == "eng":
                _, src, k = ev
                if src == e and e in ("pe", "sp"):
                    continue
                if self.seen[e].get(src, -1) >= k:
                    continue
                self.seen[e][src] = k
                epoch, idx = divmod(k, self.EPOCH)
                out.append((self._eng_sem(src, epoch), idx + 1))
            else:
                _, sem, val, sid = ev
                if self.seen[e].get(sid, -1) >= val:
                    continue
                self.seen[e][sid] = val
                out.append((sem, val))
        return out

    def _deps(self, reads, writes):
        deps = []
        for b in reads:
            deps.extend(b.w.values())
            if b.excl:
                deps.extend(b.r.values())
        for b in writes:
            deps.extend(b.w.values())
            deps.extend(b.r.values())
        return deps

    def _mark(self, ev, key, reads, writes):
        for b in reads:
            b.r[key] = ev
        for b in writes:
            if ev[0] == "dma" and all(v[0] == "dma" for v in b.w.values()):
                b.w[ev[3]] = ev
            else:
                b.w = {"w": ev}
            b.r = {}

    def op(self, e, fn, reads=(), writes=()):
        waits = self._waits(e, self._deps(reads, writes))
        k = self.cnt[e]
        self.cnt[e] += 1
        epoch, _ = divmod(k, self.EPOCH)
        sem = self._eng_sem(e, epoch)

        def thunk(eng, fn=fn, sem=sem, waits=waits):
            for s, v in waits:
                eng.wait_ge(s, v)
            fn(eng).then_inc(sem, 1)

        self.prog[e].append(thunk)
        ev = ("eng", e, k)
        self._mark(ev, e, reads, writes)
        return ev

    def dma(self, q, out, in_, reads=(), writes=(), fn=None, **kw):
        pool = self.dsem[q]
        if len(pool) < self.NDMA:
            pool.append([self._sem("d_%s_%d" % (q, len(pool))), 0, "d_%s_%d" % (q, len(pool))])
        i = self.dnext[q] % self.NDMA
        self.dnext[q] += 1
        ent = pool[i]
        sem, prev, sid = ent
        deps = self._deps(reads, writes)
        if prev > 0:
            deps.append(("dma", sem, prev, sid))
        waits = self._waits(q, deps)
        tgt = prev + 16
        ent[1] = tgt

        def thunk(eng, waits=waits, sem=sem, out=out, in_=in_, kw=kw, fn=fn):
            for s, v in waits:
                eng.wait_ge(s, v)
            if fn is not None:
                fn(eng).then_inc(sem, 16)
            else:
                eng.dma_start(out=out, in_=in_, **kw).then_inc(sem, 16)

        self.prog[q].append(thunk)
        ev = ("dma", sem, tgt, sid)
        self._mark(ev, "dma_%s_%d" % (q, self.dnext[q]), reads, writes)
        return ev

    def barrier(self):
        evs = []
        for src in ("pe", "act", "dve", "pool"):
            if self.cnt[src] > 0:
                evs.append(("eng", src, self.cnt[src] - 1))
        for q, pool in self.dsem.items():
            for sem, val, sid in pool:
                if val > 0:
                    evs.append(("dma", sem, val, sid))
        for e in ("pe", "act", "dve", "pool", "sp"):
            deps = [ev for ev in evs if not (ev[0] == "eng" and ev[1] == e)]
            waits = self._waits(e, deps)

            def thunk(eng, waits=waits):
                for s, v in waits:
                    eng.wait_ge(s, v)

            self.prog[e].append(thunk)

    def finish(self, evs):
        waits = self._waits("sp", list(evs))

        def thunk(eng, waits=waits):
            for s, v in waits:
                eng.wait_ge(s, v)

        self.prog["sp"].append(thunk)

    def emit(self):
        with self.nc.Block() as block:
            for name, reg in (("pe", "tensor"), ("act", "scalar"), ("dve", "vector"), ("pool", "gpsimd"), ("sp", "sync")):
                prog = self.prog[name]

                def body(eng, prog=prog):
                    for th in prog:
                        th(eng)

                getattr(block, reg)(body)


D = 1024
DFF = 2816
NJ = DFF // 128
INW = 4376
ALPHA = 2.0 ** 0.25
LN_EPS = 1e-5
C_Q, C_KV, C_G, C_U, C_V, C_GA, C_GB = 0, 512, 1280, 1304, 1816, 2328, 3352
SLOPES = [2.0 ** (-(h + 1)) for h in range(8)]
NEGBIG = -30000.0


def host_consts(NTL, first_valid_tile, tmin=0):
    import ml_dtypes
    bf = ml_dtypes.bfloat16
    NPOS = NTL * 128
    NCB = NPOS // 16 - 1
    NCT = (NCB + 127) // 128
    c = {}
    c["c_identf"] = np.eye(128, dtype=np.float32)
    ka = np.zeros((5, NPOS), np.float32)
    pos = np.arange(NPOS)
    ka[0] = pos % 128
    ka[1] = 1.0
    ka[2] = pos // 128
    ka[3] = 1.0
    ka[4] = np.where(pos < first_valid_tile * 128, -1.0e6, 0.0)
    c["c_kaug"] = ka.astype(bf)
    kc = np.zeros((5, NCT * 128), np.float32)
    ci = np.arange(NCT * 128)
    kc[0] = 16 * (ci % 128)
    kc[1] = 1.0
    kc[2] = 16 * (ci // 128)
    kc[3] = 1.0
    kc[4] = 31.0 + np.where(16 * ci < first_valid_tile * 128, -1.0e6, 0.0)
    c["c_kcaug"] = kc.astype(bf)
    qa = np.zeros((8, 5, NPOS), np.float32)
    for h in range(8):
        s = SLOPES[h]
        qa[h, 0] = s
        qa[h, 1] = -s * (pos % 128)
        qa[h, 2] = s * 128
        qa[h, 3] = -s * 128 * (pos // 128)
        qa[h, 4] = s
    c["c_qaug"] = qa.astype(bf)
    npt = max(NPOS, tmin)
    post = np.arange(npt)
    T = np.zeros((128, npt), np.float32)
    T[(post // 64) % 128, post] = 1.0
    c["c_T"] = T.astype(bf)
    W = np.zeros((128, NCT, 128), np.float32)
    for ct in range(NCT):
        for ir in range(128):
            i = ct * 128 + ir
            for j in range(128):
                if 4 * j - 1 <= i <= 4 * j + 3:
                    W[ir, ct, j] = 1.0
    c["c_wimp"] = W.astype(bf)
    Gb = np.zeros((128, 256), np.float32)
    Gm = np.zeros((128, 256), np.float32)
    for q in range(128):
        a = 1 if q >= 64 else 0
        for x in range(256):
            d = x - 128
            if d > a:
                Gb[q, x] = -1.0e30
            elif d >= a - 1:
                Gb[q, x] = 1.0e4
            else:
                Gm[q, x] = 1.0
    c["c_gb"] = Gb
    c["c_gm"] = Gm
    F0 = np.zeros((128, 128), np.float32)
    jf = first_valid_tile * 2
    F0[:, :jf] = -1.0e30
    F0[:, jf] = 1.0e4
    c["c_f0"] = F0
    sp = np.zeros((128, 2, 64), np.float32)
    for m in range(64):
        sp[2 * m, 0, m] = 1.0
        sp[2 * m + 1, 1, m] = 1.0
    c["c_selpair"] = sp.astype(bf)
    c["c_ones"] = np.ones((1, 128), np.float32).astype(bf)
    return c


W_NAMES = ["ffn1_w_in", "ffn1_w_out", "ln1_g", "ln1_b", "w_in", "cmp_pe_k", "cmp_w1_k", "cmp_w2_k",
           "cmp_pe_v", "cmp_w1_v", "cmp_w2_v", "gmlp_ln_g", "gmlp_ln_b", "spatial_w", "spatial_b",
           "w_branch_a", "w_branch_b", "w_out", "ln2_g", "ln2_b", "ffn2_w_in", "ffn2_w_out", "ln3_g", "ln3_b"]
W_SHAPES = {"ffn1_w_in": [1024, 5632], "ffn1_w_out": [2816, 1024], "ln1_g": [1024], "ln1_b": [1024],
            "w_in": [1024, 4376], "cmp_pe_k": [32, 64], "cmp_w1_k": [2048, 128], "cmp_w2_k": [128, 64],
            "cmp_pe_v": [32, 64], "cmp_w1_v": [2048, 128], "cmp_w2_v": [128, 64], "gmlp_ln_g": [512],
            "gmlp_ln_b": [512], "spatial_w": [4, 128, 128], "spatial_b": [4, 128], "w_branch_a": [512, 1024],
            "w_branch_b": [512, 1024], "w_out": [1024, 1024], "ln2_g": [1024], "ln2_b": [1024],
            "ffn2_w_in": [1024, 5632], "ffn2_w_out": [2816, 1024], "ln3_g": [1024], "ln3_b": [1024]}


def build(NBC, NBO, stage=99, NPG=0, NPHYS=0):
    NTC, NTO = 4 * NBC, 4 * NBO
    NTL = NTC + NTO
    NPOS = NTL * 128
    NCB = NPOS // 16 - 1
    NCT = (NCB + 127) // 128
    nc = bass.Bass("TRN2", target_bir_lowering=False)

    def din(name, shape, dt=F32):
        return nc.dram_tensor(name, list(shape), dt, kind="ExternalInput").ap()

    def dout(name, shape, dt=F32):
        return nc.dram_tensor(name, list(shape), dt, kind="ExternalOutput").ap()

    def dint(name, shape, dt=BF16):
        return nc.dram_tensor(name, list(shape), dt, kind="Internal").ap()

    x_ctx = din("x_ctx", [max(NTC, 1) * 128, D])
    x_own = din("x_own", [NTO * 128, D])
    wd = {n: din(n, W_SHAPES[n]) for n in W_NAMES}
    NPT = max(NPOS, 8192) if NPG else NPOS
    cshapes = {"c_identf": ([128, 128], F32), "c_kaug": ([5, NPOS], BF16), "c_kcaug": ([5, NCT * 128], BF16),
               "c_qaug": ([8, 5, NPOS], BF16), "c_T": ([128, NPT], BF16), "c_wimp": ([128, NCT, 128], BF16),
               "c_gb": ([128, 256], F32), "c_gm": ([128, 256], F32), "c_f0": ([128, 128], F32),
               "c_selpair": ([128, 2, 64], BF16), "c_ones": ([1, 128], BF16)}
    cd = {n: din(n, s, dt) for n, (s, dt) in cshapes.items()}
    import os as _os0
    DBG = bool(_os0.environ.get("KDBG"))
    if DBG:
        dbg_oa = dout("dbg_oa", [NTO * 128, 512])
        dbg_zT = dout("dbg_zT", [512, NTO * 128])
        dbg_cf = dout("dbg_cf", [NTO * 128, 2, 12])
        dbg_ps = dout("dbg_ps", [NTO * 128, 2, 128])
    if NPG:
        x_smp = din("x_smp", [128, D])
        pools = [din("pool%d" % i, [NPHYS * 128, 256]) for i in range(2)]
        cwin = [din("cwin_k", [4, 512, 128]), din("cwin_v", [4, 512, 128])]
        ptab = din("ptab", [4, NPG], I32)
        NCS = NPG * 8
        NCTS = (NCS + 127) // 128
        NJS = 2 * NPG + 1
        sconst = {"c_piota": ([128, 1], F32), "c_skaug": ([5, (NPG + 1) * 128], BF16), "c_skcaug": ([5, NCTS * 128], BF16),
                  "c_swaug": ([5, 512], BF16), "c_sqaug": ([8, 5, 128], BF16), "c_w0": ([128, 33], BF16),
                  "c_smb": ([128, 264], F32), "c_sbb": ([128, 264], F32), "c_augt": ([5, 3, 128], BF16)}
        scd = {n: din(n, sh, dt) for n, (sh, dt) in sconst.items()}
        y_s = dout("y_s", [128, D])
        kv_s = dout("kv_s", [128, 768])
        swin = [dout("swin_k", [4, 512, 128]), dout("swin_v", [4, 512, 128])]
        chunkv = dout("chunkv", [128, 512])
    y_own = dout("y_own", [NTO * 128, D])
    kv_own = dout("kv_own", [NTO * 128, 768])

    s_f_in = [dint("s_f1in", [NJ, 128, 2, 8, 128]), dint("s_f2in", [NJ, 128, 2, 8, 128])]
    s_f_out = [dint("s_f1out", [DFF, D]), dint("s_f2out", [DFF, D])]
    WIN_UNITS = [(0, 256), (256, 256), (512, 256), (768, 256), (1024, 256), (1280, 24), (1304, 256), (1560, 256),
                 (1816, 256), (2072, 256)] + [(2328 + 256 * i, 256) for i in range(4)] + [(3352 + 256 * i, 256) for i in range(4)]
    U_Q, U_KV, U_G, U_U, U_V, U_GA, U_GB = 0, 2, 5, 6, 8, 10, 14
    s_winu = dint("s_winu", [len(WIN_UNITS), 128, 8, 256])
    s_wa = dint("s_wa", [8, 128, 4, 128])
    s_wb = dint("s_wb", [8, 128, 4, 128])
    s_wo = dint("s_wo", [D, D])
    s_w1 = [dint("s_w1k", [2048, 128]), dint("s_w1v", [2048, 128])]
    s_w2 = [dint("s_w2k", [128, 64]), dint("s_w2v", [128, 64])]
    s_pe = [dint("s_pek", [2048, 1]), dint("s_pev", [2048, 1])]
    s_sw = dint("s_sw", [4, 128, 128])
    s_sb = dint("s_sb", [4, 128])

    out_events = []
    with contextlib.ExitStack() as st:
        S = Sched(nc, st)
        sb, ps = S.sb, S.ps

        identf = sb("identf", [128, 128], F32)
        identb = sb("identb", [128, 128], BF16)
        lnp = sb("lnp", [128, 6, D], F32)
        glnp = sb("glnp", [128, 2, 512], F32)
        wsT = sb("wsT", [128, 4, 128], BF16)
        bsrow = sb("bsrow", [1, 4, 128], BF16)
        onesrow = sb("onesrow", [1, 128], BF16)
        Tsel = sb("Tsel", [128, NPT], FP8)
        wimp = sb("wimp", [128, NCT, 128], BF16)
        gbt = sb("gbt", [128, 256], F32)
        gmt = sb("gmt", [128, 256], F32)
        f0t = sb("f0t", [128, 128], F32)
        selpair = sb("selpair", [128, 2, 64], BF16)
        w2c = [sb("w2k", [128, 64], BF16), sb("w2v", [128, 64], BF16)]
        pec = [sb("pek", [128, 16], BF16), sb("pev", [128, 16], BF16)]
        pebias = sb("pebias", [128, 2], F32)
        Ksel = sb("Ksel", [69, 2, NPOS], BF16)
        Vsel = sb("Vsel", [128, NTL, 2, 66], BF16)
        Kwin = sb("Kwin", [69, 2, 1024], BF16)
        Vwin = sb("Vwin", [128, 8, 2, 66], BF16)
        Kc = sb("Kc", [69, 2, NCT * 128], BF16)
        Vc = sb("Vc", [128, NCT, 2, 66], BF16)
        XT2 = sb("XT2", [128, 4, 264], BF16)
        xres = [sb("xres%d" % i, [128, D], F32) for i in range(4)]
        xT = sb("xT", [128, 8, 512], BF16)
        chunks = [sb("chunk%d" % i, [128, 512], BF16) for i in range(NJ)]
        qT = sb("qT", [69, 8, 512], BF16)
        NR = 4
        wring = [sb("wring%d" % i, [128, 2048], BF16) for i in range(NR)]
        silu_t = [sb("silu%d" % i, [128, 512], BF16) for i in range(2)]
        pTb = [sb("pT%d" % i, [128, 512], BF16) for i in range(3)]
        f32t = [sb("f32t%d" % i, [128, 792], F32) for i in range(2)]
        stats = sb("stats", [128, 2, 6], F32)
        mv = sb("mv", [128, 2], F32)
        rstd = sb("rstd", [128, 1], F32)
        banks = [ps("bank%d" % i, [128, 512], F32) for i in range(8)]
        gts = sb("gts", [128, 4, 24], F32)
        vnb = sb("vnb", [128, 4, 512], BF16)
        oab = sb("oab", [128, 512], BF16)
        psc = sb("psc", [128, 128], F32)
        psc2 = sb("psc2", [128, 128], F32)
        m8a = sb("m8a", [128, 8], F32)
        m8b = sb("m8b", [128, 8], F32)
        nsb = sb("nsb", [128, 128], BF16)
        nsT = sb("nsT", [128, 128], BF16)
        rden = sb("rden", [128, 3, 4], F32)
        coef = sb("coef", [128, 3, 4], F32)
        oa32 = [sb("oa32_%d" % i, [128, 4, 64], F32) for i in range(2)]

        rr = {"w": 0, "silu": 0, "pT": 0, "f32": 0, "S": 0, "oa": 0}

        def nxt(key, lst):
            i = rr[key] % len(lst)
            rr[key] += 1
            return lst[i]

        def wload(src_ap, reads, shape):
            slot = nxt("w", wring)
            n = 1
            for s_ in shape[1:]:
                n *= s_
            view = slot.ap[:, 0:n]
            if len(shape) == 3:
                view = view.rearrange("p (a b) -> p a b", a=shape[1])
            elif len(shape) == 4:
                view = view.rearrange("p (a b c) -> p a b c", a=shape[1], b=shape[2])
            S.dma("sp", view, src_ap, reads=reads, writes=[slot])
            return slot, view

        def mm(ob, oap, lb, lap, rb, rap, start, stop, extra=(), skip=False):
            S.op("pe", lambda e: e.matmul(oap, lhsT=lap, rhs=rap, start=start, stop=stop, skip_group_check=skip),
                 reads=[lb, rb, *extra], writes=[ob])

        def tr(ob, oap, ib, iap, idb):
            S.op("pe", lambda e: e.transpose(out=oap, in_=iap, identity=idb.ap[:iap.shape[0], :iap.shape[0]]),
                 reads=[ib, idb], writes=[ob])

        def act(ob, oap, ib, iap, func, extra_r=(), **kw):
            S.op("act", lambda e: e.activation(out=oap, in_=iap, func=func, **kw), reads=[ib, *extra_r], writes=[ob])

        def tt(eng, ob, oap, ab, aap, bb, bap, op):
            S.op(eng, lambda e: e.tensor_tensor(out=oap, in0=aap, in1=bap, op=op), reads=[ab, bb], writes=[ob])

        def ts(eng, ob, oap, ib, iap, s1, s2, op0, op1=None, extra_r=()):
            if op1 is None:
                S.op(eng, lambda e: e.tensor_scalar(out=oap, in0=iap, scalar1=s1, scalar2=None, op0=op0),
                     reads=[ib, *extra_r], writes=[ob])
            else:
                S.op(eng, lambda e: e.tensor_scalar(out=oap, in0=iap, scalar1=s1, scalar2=s2, op0=op0, op1=op1),
                     reads=[ib, *extra_r], writes=[ob])

        def stt(ob, oap, ab, aap, scalar, bb, bap, op0, op1, extra_r=()):
            S.op("dve", lambda e: e.scalar_tensor_tensor(out=oap, in0=aap, scalar=scalar, in1=bap, op0=op0, op1=op1),
                 reads=[ab, bb, *extra_r], writes=[ob])

        def cp(eng, ob, oap, ib, iap):
            if eng == "act":
                S.op("act", lambda e: e.copy(out=oap, in_=iap), reads=[ib], writes=[ob])
            else:
                S.op(eng, lambda e: e.tensor_copy(out=oap, in_=iap), reads=[ib], writes=[ob])

        S.dma("sp", identf.ap[:], cd["c_identf"], writes=[identf])
        cp("dve", identb, identb.ap[:], identf, identf.ap[:])
        for i, n in enumerate(["ln1_g", "ln1_b", "ln2_g", "ln2_b", "ln3_g", "ln3_b"]):
            S.dma("sp", lnp.ap[:, i, :], wd[n].partition_broadcast(128), writes=[lnp])
        for i, n in enumerate(["gmlp_ln_g", "gmlp_ln_b"]):
            S.dma("sp", glnp.ap[:, i, :], wd[n].partition_broadcast(128), writes=[glnp])
        for i0 in range(0, NPT, 512):
            tb_ = nxt("pT", pTb)
            S.dma("sp", tb_.ap[:], cd["c_T"][:, i0:i0 + 512], writes=[tb_])
            S.op("dve", lambda e, tb_=tb_, i0=i0: e.tensor_copy(out=Tsel.ap[:, i0:i0 + 512], in_=tb_.ap[:], saturate=False),
                 reads=[tb_], writes=[Tsel])
        S.dma("sp", wimp.ap[:], cd["c_wimp"], writes=[wimp])
        S.dma("sp", gbt.ap[:], cd["c_gb"], writes=[gbt])
        S.dma("sp", gmt.ap[:], cd["c_gm"], writes=[gmt])
        S.dma("sp", f0t.ap[:], cd["c_f0"], writes=[f0t])
        S.dma("sp", selpair.ap[:], cd["c_selpair"], writes=[selpair])
        S.dma("sp", onesrow.ap[:], cd["c_ones"], writes=[onesrow])
        S.op("dve", lambda e: e.memset(Kc.ap[:], 0.0), writes=[Kc])
        for g in range(2):
            S.dma("sp", Ksel.ap[64:69, g, :], cd["c_kaug"], writes=[Ksel])
            S.dma("sp", Kc.ap[64:69, g, :], cd["c_kcaug"], writes=[Kc])
        S.op("dve", lambda e: e.memset(Vsel.ap[:], 1.0), writes=[Vsel])
        S.op("dve", lambda e: e.memset(Vwin.ap[:], 1.0), writes=[Vwin])
        S.op("dve", lambda e: e.memset(Vc.ap[:], 1.0), writes=[Vc])
        S.op("dve", lambda e: e.memset(XT2.ap[:], 0.0), writes=[XT2])

        sc = {}

        def cast(key, out_ap, in_ap):
            b = sc.setdefault(key, Buf(key))
            S.dma("pool", out_ap, in_ap, writes=[b])

        def cast_ffn(f):
            wi = wd["ffn%d_w_in" % (f + 1)]
            wo = wd["ffn%d_w_out" % (f + 1)]
            for j in range(NJ):
                for gu in range(2):
                    c0 = gu * DFF + j * 128
                    cast(("fin", f, j), s_f_in[f][j, :, gu, :, :], wi[:, c0:c0 + 128].rearrange("(kc p) c -> p kc c", p=128))
            for j in range(NJ):
                cast(("fout", f, j), s_f_out[f][j * 128:(j + 1) * 128, :], wo[j * 128:(j + 1) * 128, :])

        cast_ffn(0)
        for u, (c0, ncol) in enumerate(WIN_UNITS):
            cast(("win", u), s_winu[u, :, :, 0:ncol], wd["w_in"][:, c0:c0 + ncol].rearrange("(kc p) c -> p kc c", p=128))
        for kv, (n1, n2, npe) in enumerate((("cmp_w1_k", "cmp_w2_k", "cmp_pe_k"), ("cmp_w1_v", "cmp_w2_v", "cmp_pe_v"))):
            cast("cmpw", s_w1[kv][:, :], wd[n1][:, :])
            cast("cmpw", s_w2[kv][:, :], wd[n2][:, :])
            cast("cmpw", s_pe[kv].rearrange("(a b) o -> a (b o)", b=64), wd[npe][:, :])
        cast("sw", s_sw.rearrange("h a b -> (h a) b"), wd["spatial_w"].rearrange("h a b -> (h a) b"))
        cast("sw", s_sb[:, :], wd["spatial_b"][:, :])
        for c in range(8):
            cast(("wa", c), s_wa[c], wd["w_branch_a"][:, c * 128:(c + 1) * 128].rearrange("(kc p) n -> p kc n", p=128))
            cast(("wb", c), s_wb[c], wd["w_branch_b"][:, c * 128:(c + 1) * 128].rearrange("(kc p) n -> p kc n", p=128))
        for r0 in range(0, 1024, 128):
            cast(("wo", r0 // 128), s_wo[r0:r0 + 128, :], wd["w_out"][r0:r0 + 128, :])
        cast_ffn(1)

        for kv in range(2):
            S.dma("sp", w2c[kv].ap[:], s_w2[kv][:, :], reads=[sc["cmpw"]], writes=[w2c[kv]])
            pl = nxt("pT", pTb)
            S.dma("sp", pl.ap[0:16, 0:128], s_pe[kv].rearrange("(c p) o -> c (p o)", p=128), reads=[sc["cmpw"]], writes=[pl])
            tr(banks[7], banks[7].ap.bitcast(BF16)[:, kv * 16:(kv + 1) * 16], pl, pl.ap[0:16, 0:128], identb)
            cp("dve", pec[kv], pec[kv].ap[:], banks[7], banks[7].ap.bitcast(BF16)[:, kv * 16:(kv + 1) * 16])
        S.dma("sp", bsrow.ap[:], s_sb.rearrange("(o h) t -> o h t", o=1), reads=[sc["sw"]], writes=[bsrow])
        swl = nxt("pT", pTb)
        S.dma("sp", swl.ap[:].rearrange("p (h t) -> p h t", h=4), s_sw.rearrange("h a b -> a h b"), reads=[sc["sw"]], writes=[swl])
        trb = banks[7]
        trb_bf = trb.ap.bitcast(BF16)
        for h in range(4):
            tr(trb, trb_bf[:, h * 128:(h + 1) * 128], swl, swl.ap[:, h * 128:(h + 1) * 128], identb)
        cp("dve", wsT, wsT.ap[:].rearrange("p h t -> p (h t)"), trb, trb_bf[:, 0:512])
        S.op("pool", lambda e: e.affine_select(out=wsT.ap[:], in_=wsT.ap[:], pattern=[[0, 4], [1, 128]],
                                               compare_op=ALU.is_ge, fill=0.0, base=0, channel_multiplier=-1),
             reads=[wsT], writes=[wsT])
        pb = banks[5]
        for kv in range(2):
            w1s, w1v_ = wload(s_w1[kv].rearrange("(c p) h -> p c h", p=128), [sc["cmpw"]], [128, 16, 128])
            for c in range(16):
                mm(pb, pb.ap[:, kv:kv + 1], w1s, w1v_[:, c, :], pec[kv], pec[kv].ap[:, c:c + 1], c == 0, c == 15)
        cp("dve", pebias, pebias.ap[:], pb, pb.ap[:, 0:2])

        def transpose_tile(src, t):
            for half in range(2):
                bk = banks[(2 * t + half) % 4]
                for k in range(4):
                    kc = half * 4 + k
                    tr(bk, bk.ap[:, k * 128:(k + 1) * 128], src, src.ap[:, kc * 128:(kc + 1) * 128], identf)
                eng = "act" if half == 0 else "dve"
                cp(eng, xT, xT.ap[:, half * 4:half * 4 + 4, t * 128:(t + 1) * 128],
                   bk, bk.ap[:].rearrange("p (k n) -> p k n", k=4))

        def layer_norm_tile(xb, gi, eps):
            for h in range(2):
                S.op("dve", lambda e, h=h: e.bn_stats(out=stats.ap[:, h, :], in_=xb.ap[:, h * 512:(h + 1) * 512]),
                     reads=[xb], writes=[stats])
            S.op("dve", lambda e: e.bn_aggr(out=mv.ap[:], in_=stats.ap[:].rearrange("p a b -> p (a b)")),
                 reads=[stats], writes=[mv])
            act(rstd, rstd.ap[:], mv, mv.ap[:, 1:2], AF.Sqrt, bias=eps, scale=1.0)
            S.op("dve", lambda e: e.reciprocal(out=rstd.ap[:], in_=rstd.ap[:]), reads=[rstd], writes=[rstd])
            ts("dve", xb, xb.ap[:], xb, xb.ap[:], mv.ap[:, 0:1], rstd.ap[:, 0:1], ALU.subtract, ALU.mult, extra_r=[mv, rstd])
            tt("dve", xb, xb.ap[:], xb, xb.ap[:], lnp, lnp.ap[:, gi, :], ALU.mult)
            tt("dve", xb, xb.ap[:], xb, xb.ap[:], lnp, lnp.ap[:, gi + 1, :], ALU.add)

        def ffn_block(f, nt, gi):
            ntok = nt * 128
            for j in range(NJ):
                slot, w = wload(s_f_in[f][j], [sc[("fin", f, j)]], [128, 2, 8, 128])
                gb_, ub_ = banks[j % 2], banks[2 + j % 2]
                for gu, bk in ((0, gb_), (1, ub_)):
                    for kc in range(8):
                        mm(bk, bk.ap[:, 0:ntok], slot, w[:, gu, kc, :], xT, xT.ap[:, kc, 0:ntok], kc == 0, kc == 7)
                sl = nxt("silu", silu_t)
                act(sl, sl.ap[:, 0:ntok], gb_, gb_.ap[:, 0:ntok], AF.Silu)
                tt("dve", chunks[j], chunks[j].ap[:, 0:ntok], sl, sl.ap[:, 0:ntok], ub_, ub_.ap[:, 0:ntok], ALU.mult)
            for n in range(2):
                for j in range(NJ):
                    slot, w = wload(s_f_out[f][j * 128:(j + 1) * 128, n * 512:(n + 1) * 512], [sc[("fout", f, j)]], [128, 512])
                    for t in range(nt):
                        bk = banks[4 + t]
                        mm(bk, bk.ap[:], chunks[j], chunks[j].ap[:, t * 128:(t + 1) * 128], slot, w, j == 0, j == NJ - 1)
                for t in range(nt):
                    bk = banks[4 + t]
                    stt(xres[t], xres[t].ap[:, n * 512:(n + 1) * 512], xres[t], xres[t].ap[:, n * 512:(n + 1) * 512],
                        2.0 * ALPHA, bk, bk.ap[:], ALU.mult, ALU.add)
            for t in range(nt):
                layer_norm_tile(xres[t], gi, 4.0 * LN_EPS)

        def kv_project(nt, lt0, is_own, own_row0, need_win):
            wl = []
            for u in ([U_KV, U_KV + 1] + ([U_KV + 2] if need_win else [])):
                wl.append(wload(s_winu[u], [sc[("win", u)]], [128, 8, 256]))
            for t in range(nt):
                lt = lt0 + t
                bkA, bkB = banks[4 + (t % 2)], banks[6]
                for ui, (slot, w) in enumerate(wl):
                    bk, o0 = (bkA, ui * 256) if ui < 2 else (bkB, 0)
                    for kc in range(8):
                        mm(bk, bk.ap[:, o0:o0 + 256], xT, xT.ap[:, kc, t * 128:(t + 1) * 128], slot, w[:, kc, :], kc == 0, kc == 7)
                if is_own and (KSUB & 8):
                    stg = nxt("f32", f32t)
                    cp("act", stg, stg.ap[:, 0:512], bkA, bkA.ap[:])
                    if need_win:
                        cp("act", stg, stg.ap[:, 512:768], bkB, bkB.ap[:, 0:256])
                    r0 = own_row0 + t * 128
                    ncol = 768 if need_win else 512
                    out_events.append(S.dma("act", kv_own[r0:r0 + 128, 0:ncol], stg.ap[:, 0:ncol], reads=[stg]))
                kvb = nxt("pT", pTb)
                cp("dve", kvb, kvb.ap[:], bkA, bkA.ap[:])
                if KSUB & 4:
                    cp("act", Vsel, Vsel.ap[:, lt, :, 0:64], bkA, bkA.ap[:, 384:512].rearrange("p (g d) -> p g d", g=2))
                if KSUB & 2:
                    for g in range(2):
                        tr(trb, trb_bf[0:64, g * 128:(g + 1) * 128], kvb, kvb.ap[:, 256 + g * 64:320 + g * 64], identb)
                if need_win:
                    kwb = nxt("pT", pTb)
                    cp("dve", kwb, kwb.ap[:, 0:256], bkB, bkB.ap[:, 0:256])
                    if KSUB & 2:
                        for g in range(2):
                            tr(trb, trb_bf[0:64, 256 + g * 128:384 + g * 128], kwb, kwb.ap[:, g * 64:(g + 1) * 64], identb)
                    ws = lt % 8
                    if KSUB & 4:
                        cp("act", Vwin, Vwin.ap[:, ws, :, 0:64], bkB, bkB.ap[:, 128:256].rearrange("p (g d) -> p g d", g=2))
                if KSUB & 2:
                    cp("act", Ksel, Ksel.ap[0:64, :, lt * 128:(lt + 1) * 128], trb, trb_bf[0:64, 0:256].rearrange("p (g n) -> p g n", g=2))
                if need_win:
                    ws = lt % 8
                    if KSUB & 2:
                        cp("dve", Kwin, Kwin.ap[0:64, :, ws * 128:(ws + 1) * 128], trb, trb_bf[0:64, 256:512].rearrange("p (g n) -> p g n", g=2))
                    if KSUB & 16:
                        for g in range(2):
                            S.dma("sp", Kwin.ap[64:69, g, ws * 128:(ws + 1) * 128], cd["c_kaug"][:, lt * 128:(lt + 1) * 128], writes=[Kwin])
                xb_ = banks[(2 * t) % 4]
                if not (KSUB & 1):
                    continue
                for kvg in range(4):
                    for par in range(2):
                        mm(xb_, xb_.ap[par * 64:(par + 1) * 64, kvg * 64:(kvg + 1) * 64], kvb, kvb.ap[:, kvg * 64:(kvg + 1) * 64],
                           selpair, selpair.ap[:, par, :], True, True)
                cp("dve", XT2, XT2.ap[:, :, 8 + t * 64:8 + (t + 1) * 64], xb_, xb_.ap[:, 0:256].rearrange("p (a m) -> p a m", a=4))

        def compress_block(nt, lt0, Kc=Kc, Vc=Vc):
            nb = nt * 8
            i_lo = lt0 * 8 - 1
            skip = 1 if i_lo < 0 else 0
            n = nb - skip
            i0 = i_lo + skip
            for kv in range(2):
                hb = banks[kv]
                w1s, w1v_ = wload(s_w1[kv].rearrange("(c p) h -> p c h", p=128), [sc["cmpw"]], [128, 16, 128])
                for g in range(2):
                    a = kv * 2 + g
                    for c in range(16):
                        rhs = XT2.ap[:, a, c + 8 * skip: c + 8 * skip + 8 * (n - 1) + 1: 8]
                        mm(hb, hb.ap[:, g * 64:g * 64 + n], w1s, w1v_[:, c, :], XT2, rhs, c == 0, c == 15)
                gl = nxt("silu", silu_t)
                for g in range(2):
                    act(gl, gl.ap[:, g * 64:g * 64 + n], hb, hb.ap[:, g * 64:g * 64 + n], AF.Gelu_apprx_tanh,
                        extra_r=[pebias], bias=pebias.ap[:, kv:kv + 1], scale=1.0)
                ob = banks[2 + kv]
                if kv == 0:
                    for g in range(2):
                        mm(ob, ob.ap[0:64, g * 64:g * 64 + n], w2c[0], w2c[0].ap[:], gl, gl.ap[:, g * 64:g * 64 + n], True, True)
                    for g in range(2):
                        cp("act", Kc, Kc.ap[0:64, g, i0:i0 + n], ob, ob.ap[0:64, g * 64:g * 64 + n])
                else:
                    for g in range(2):
                        mm(ob, ob.ap[0:n, g * 64:(g + 1) * 64], gl, gl.ap[:, g * 64:g * 64 + n], w2c[1], w2c[1].ap[:], True, True)
                    stg = nxt("pT", pTb)
                    cp("act", stg, stg.ap[0:n, 0:128], ob, ob.ap[0:n, 0:128])
                    i = i0
                    while i < i0 + n:
                        ct, p0 = divmod(i, 128)
                        m = min(i0 + n - i, 128 - p0)
                        S.dma("sp", Vc.ap[p0:p0 + m, ct, :, 0:64], stg.ap[i - i0:i - i0 + m, 0:128].rearrange("p (g d) -> p g d", g=2),
                              reads=[stg], writes=[Vc])
                        i += m
            cp("dve", XT2, XT2.ap[:, :, 0:8], XT2, XT2.ap[:, :, nt * 64:nt * 64 + 8])

        uTc, oaT, zT, mT = chunks[0:4], chunks[4:8], chunks[8:12], chunks[12:20]

        def ln_small(src, sap, n, gap, bap, dst, dap, eps):
            S.op("dve", lambda e: e.bn_stats(out=stats.ap[:, 0, :], in_=sap), reads=[src], writes=[stats])
            S.op("dve", lambda e: e.bn_aggr(out=mv.ap[:], in_=stats.ap[:, 0, :]), reads=[stats], writes=[mv])
            act(rstd, rstd.ap[:], mv, mv.ap[:, 1:2], AF.Sqrt, bias=eps, scale=1.0)
            S.op("dve", lambda e: e.reciprocal(out=rstd.ap[:], in_=rstd.ap[:]), reads=[rstd], writes=[rstd])
            ts("dve", src, sap, src, sap, mv.ap[:, 0:1], rstd.ap[:, 0:1], ALU.subtract, ALU.mult, extra_r=[mv, rstd])
            tt("dve", src, sap, src, sap, glnp, gap, ALU.mult)
            tt("dve", dst, dap, src, sap, glnp, bap, ALU.add)

        def sbank():
            return nxt("S", banks[0:2])

        def pool_mask(P, base, cm, qstep, op):
            S.op("pool", lambda e: e.affine_select(out=P.ap[:].rearrange("p (r q) -> p r q", r=4),
                                                   in_=P.ap[:].rearrange("p (r q) -> p r q", r=4),
                                                   pattern=[[0, 4], [qstep, 128]], compare_op=op, fill=0.0,
                                                   base=base, channel_multiplier=cm), reads=[P], writes=[P])

        def pv(Ob, first, P, Vb, vap):
            for r in range(4):
                mm(Ob, Ob.ap[:, r * 66:r * 66 + 65], P, P.ap[:, r * 128:(r + 1) * 128], Vb, vap,
                   first and r == 0, False, skip=True)

        def dens(Ob, br):
            dv = Ob.ap[:, 0:264].rearrange("p (r e) -> p r e", r=4)[:, :, 64]
            ts("dve", rden, rden.ap[:, br, :], Ob, dv, 1e-30, None, ALU.max)
            S.op("dve", lambda e: e.reciprocal(out=rden.ap[:, br, :], in_=rden.ap[:, br, :]), reads=[rden], writes=[rden])

        def attention_tile(t, lt):
            q0 = lt * 128
            imax = min((q0 + 96) // 16, NCB - 1)
            nct = imax // 128 + 1
            Oc, Ow, Os, psl = banks[2], banks[3], banks[4], banks[5]
            for g in range(2):
                qrhs = qT.ap[:, 4 * g:4 * g + 4, t * 128:(t + 1) * 128]
                for ct in range(nct):
                    Sb = sbank()
                    mm(Sb, Sb.ap[:].rearrange("p (r q) -> p r q", r=4), Kc, Kc.ap[:, g, ct * 128:(ct + 1) * 128], qT, qrhs, True, True)
                    P = nxt("pT", pTb)
                    bnd = q0 - 2048 * ct - 2032 - 31 < 0
                    if bnd:
                        ts("dve", Sb, Sb.ap[:], Sb, Sb.ap[:], 30.0, None, ALU.min)
                    act(P, P.ap[:], Sb, Sb.ap[:], AF.Exp)
                    if bnd:
                        pool_mask(P, q0 - 2048 * ct - 31, -16, 1, ALU.is_ge)
                    pv(Oc, ct == 0, P, Vc, Vc.ap[:, ct, g, 0:65])
                    for r in range(4):
                        mm(psl, psl.ap[:, r * 128:(r + 1) * 128], P, P.ap[:, r * 128:(r + 1) * 128], wimp, wimp.ap[:, ct, :],
                           ct == 0 and r == 0, False, skip=True)
                dens(Oc, 0)
                ts("dve", psc, psc.ap[:], psl, psl.ap[:, 0:128], rden.ap[:, 0, 0:1], None, ALU.mult, extra_r=[rden])
                for r in range(1, 4):
                    stt(psc, psc.ap[:], psl, psl.ap[:, r * 128:(r + 1) * 128], rden.ap[:, 0, r:r + 1], psc, psc.ap[:],
                        ALU.mult, ALU.add, extra_r=[rden])
                o = 128 - 2 * lt
                tt("dve", psc, psc.ap[:], psc, psc.ap[:], gmt, gmt.ap[:, o:o + 128], ALU.mult)
                tt("dve", psc, psc.ap[:], psc, psc.ap[:], gbt, gbt.ap[:, o:o + 128], ALU.add)
                tt("dve", psc, psc.ap[:], psc, psc.ap[:], f0t, f0t.ap[:], ALU.add)
                S.op("dve", lambda e: e.max(out=m8a.ap[:], in_=psc.ap[:]), reads=[psc], writes=[m8a])
                S.op("dve", lambda e: e.match_replace(out=psc2.ap[:], in_to_replace=m8a.ap[:], in_values=psc.ap[:], imm_value=-3.0e38),
                     reads=[psc, m8a], writes=[psc2])
                S.op("dve", lambda e: e.max(out=m8b.ap[:], in_=psc2.ap[:]), reads=[psc2], writes=[m8b])
                ts("dve", nsb, nsb.ap[:], psc, psc.ap[:], m8b.ap[:, 7:8], NEGBIG, ALU.is_lt, ALU.mult, extra_r=[m8b])
                tr(trb, trb_bf[:, 0:128], nsb, nsb.ap[:], identb)
                cp("act", nsT, nsT.ap[:], trb, trb_bf[:, 0:128])
                kts = [kt for kt in range(lt - 4, lt + 1) if kt >= 0]
                for idx, kt in enumerate(kts):
                    ws = kt % 8
                    Sb = sbank()
                    mm(Sb, Sb.ap[:].rearrange("p (r q) -> p r q", r=4), Kwin, Kwin.ap[:, g, ws * 128:(ws + 1) * 128], qT, qrhs, True, True)
                    P = nxt("pT", pTb)
                    act(P, P.ap[:], Sb, Sb.ap[:], AF.Exp)
                    if kt == lt:
                        pool_mask(P, 0, -1, 1, ALU.is_ge)
                    if kt == lt - 4:
                        pool_mask(P, 0, 1, -1, ALU.is_gt)
                    pv(Ow, idx == 0, P, Vwin, Vwin.ap[:, ws, g, 0:65])
                dens(Ow, 2)
                for kt in range(0, lt + 1):
                    Sb = sbank()
                    mm(Sb, Sb.ap[:].rearrange("p (r q) -> p r q", r=4), Ksel, Ksel.ap[:, g, kt * 128:(kt + 1) * 128], qT, qrhs, True, False)
                    for r in range(4):
                        mm(Sb, Sb.ap[:, r * 128:(r + 1) * 128], Tsel, Tsel.ap[:, kt * 128:(kt + 1) * 128], nsT, nsT.ap[:], False, r == 3)
                    P = nxt("pT", pTb)
                    act(P, P.ap[:], Sb, Sb.ap[:], AF.Exp)
                    if kt == lt:
                        pool_mask(P, 0, -1, 1, ALU.is_ge)
                    pv(Os, kt == 0, P, Vsel, Vsel.ap[:, kt, g, 0:65])
                dens(Os, 1)
                gv = gts.ap[:, t, g * 12:(g + 1) * 12].rearrange("p (r b) -> p r b", b=3)
                for br in range(3):
                    tt("dve", coef, coef.ap[:, br, :], rden, rden.ap[:, br, :], gts, gv[:, :, br], ALU.mult)
                if DBG:
                    r0 = (lt - NTC) * 128
                    out_events.append(S.dma("sp", dbg_cf[r0:r0 + 128, g, :], rden.ap[:].rearrange("p a b -> p (a b)"), reads=[rden]))
                    out_events.append(S.dma("sp", dbg_ps[r0:r0 + 128, g, :], psc.ap[:], reads=[psc]))
                acc = nxt("oa", oa32)
                tmp = nxt("oa", oa32)
                for br, Ob in ((0, Oc), (1, Os), (2, Ow)):
                    ov = Ob.ap[:, 0:264].rearrange("p (r e) -> p r e", r=4)[:, :, 0:64]
                    cb = coef.ap[:, br, :].unsqueeze(2).broadcast_to([128, 4, 64])
                    if br == 0:
                        tt("dve", acc, acc.ap[:], Ob, ov, coef, cb, ALU.mult)
                    else:
                        tt("dve", tmp, tmp.ap[:], Ob, ov, coef, cb, ALU.mult)
                        if br == 1:
                            tt("dve", acc, acc.ap[:], acc, acc.ap[:], tmp, tmp.ap[:], ALU.add)
                        else:
                            tt("dve", oab, oab.ap[:, g * 256:(g + 1) * 256].rearrange("p (r d) -> p r d", r=4),
                               acc, acc.ap[:], tmp, tmp.ap[:], ALU.add)
            if DBG:
                d32 = nxt("f32", f32t)
                cp("dve", d32, d32.ap[:, 0:512], oab, oab.ap[:])
                r0 = (lt - NTC) * 128
                out_events.append(S.dma("sp", dbg_oa[r0:r0 + 128, :], d32.ap[:, 0:512], reads=[d32]))
            for kc in range(4):
                tr(trb, trb_bf[:, 256 + kc * 128:384 + kc * 128], oab, oab.ap[:, kc * 128:(kc + 1) * 128], identb)
            for kc in range(4):
                cp("act" if kc % 2 else "dve", oaT[kc], oaT[kc].ap[:, t * 128:(t + 1) * 128], trb, trb_bf[:, 256 + kc * 128:384 + kc * 128])

        def mixer_block(nt, lt0):
            ntok = nt * 128
            for hq in range(2):
                slot, w = wload(s_winu[U_Q + hq], [sc[("win", U_Q + hq)]], [128, 8, 256])
                for hh in range(4):
                    h = 4 * hq + hh
                    bk = banks[hh % 4]
                    for kc in range(8):
                        mm(bk, bk.ap[0:64, 0:ntok], slot, w[:, kc, hh * 64:(hh + 1) * 64], xT, xT.ap[:, kc, 0:ntok], kc == 0, kc == 7)
                    S.op("act", lambda e, bk=bk, h=h: e.mul(qT.ap[0:64, h, 0:ntok], bk.ap[0:64, 0:ntok], 0.125), reads=[bk], writes=[qT])
            for h in range(8):
                S.dma("sp", qT.ap[64:69, h, 0:ntok], cd["c_qaug"][h, :, lt0 * 128:lt0 * 128 + ntok], writes=[qT])
            slot, w = wload(s_winu[U_G, :, :, 0:24], [sc[("win", U_G)]], [128, 8, 24])
            for t in range(nt):
                bk = banks[4 + t % 2]
                for kc in range(8):
                    mm(bk, bk.ap[:, 0:24], xT, xT.ap[:, kc, t * 128:(t + 1) * 128], slot, w[:, kc, :], kc == 0, kc == 7)
                act(gts, gts.ap[:, t, :], bk, bk.ap[:, 0:24], AF.Sigmoid)
            for uu in range(2):
                slot, w = wload(s_winu[U_U + uu], [sc[("win", U_U + uu)]], [128, 8, 256])
                for cc in range(2):
                    c = uu * 2 + cc
                    bk = banks[c % 4]
                    for kc in range(8):
                        mm(bk, bk.ap[:, 0:ntok], slot, w[:, kc, cc * 128:(cc + 1) * 128], xT, xT.ap[:, kc, 0:ntok], kc == 0, kc == 7)
                    act(uTc[c], uTc[c].ap[:, 0:ntok], bk, bk.ap[:, 0:ntok], AF.Gelu_apprx_tanh)
            wv = [wload(s_winu[U_V + i], [sc[("win", U_V + i)]], [128, 8, 256]) for i in range(2)]
            for t in range(nt):
                bk = banks[4 + t % 2]
                for i in range(2):
                    for kc in range(8):
                        mm(bk, bk.ap[:, i * 256:(i + 1) * 256], xT, xT.ap[:, kc, t * 128:(t + 1) * 128], wv[i][0], wv[i][1][:, kc, :], kc == 0, kc == 7)
                vf = nxt("f32", f32t)
                act(vf, vf.ap[:, 0:512], bk, bk.ap[:], AF.Gelu_apprx_tanh)
                ln_small(vf, vf.ap[:, 0:512], 512, glnp.ap[:, 0, :], glnp.ap[:, 1, :], vnb, vnb.ap[:, t, :], LN_EPS)
                gb_ = banks[6]
                for hg in range(4):
                    mm(gb_, gb_.ap[:, hg * 128:(hg + 1) * 128], vnb, vnb.ap[:, t, hg * 128:(hg + 1) * 128], wsT, wsT.ap[:, hg, :], True, False)
                    mm(gb_, gb_.ap[:, hg * 128:(hg + 1) * 128], onesrow, onesrow.ap[0:1, :], bsrow, bsrow.ap[0:1, hg, :], False, True)
                for hg in range(4):
                    tt("dve", zT[hg], zT[hg].ap[:, t * 128:(t + 1) * 128], gb_, gb_.ap[:, hg * 128:(hg + 1) * 128],
                       uTc[hg], uTc[hg].ap[:, t * 128:(t + 1) * 128], ALU.mult)
            if DBG:
                for hg in range(4):
                    d32 = nxt("f32", f32t)
                    cp("dve", d32, d32.ap[:, 0:ntok], zT[hg], zT[hg].ap[:, 0:ntok])
                    c0 = (lt0 - NTC) * 128
                    out_events.append(S.dma("sp", dbg_zT[hg * 128:(hg + 1) * 128, c0:c0 + ntok], d32.ap[:, 0:ntok], reads=[d32]))
            for t in range(nt):
                attention_tile(t, lt0 + t)
            merge_tail(nt)

        def merge_tail(nt):
            ntok = nt * 128
            for c in range(8):
                cc = c % 2
                wga = wload(s_winu[U_GA + c // 2, :, :, cc * 128:(cc + 1) * 128], [sc[("win", U_GA + c // 2)]], [128, 8, 128])
                wgb = wload(s_winu[U_GB + c // 2, :, :, cc * 128:(cc + 1) * 128], [sc[("win", U_GB + c // 2)]], [128, 8, 128])
                wa_ = wload(s_wa[c], [sc[("wa", c)]], [128, 4, 128])
                wb_ = wload(s_wb[c], [sc[("wb", c)]], [128, 4, 128])
                bA, bG, bB, bH = banks[0], banks[1], banks[2], banks[3]
                for kc in range(8):
                    mm(bG, bG.ap[:, 0:ntok], wga[0], wga[1][:, kc, :], xT, xT.ap[:, kc, 0:ntok], kc == 0, kc == 7)
                for kc in range(4):
                    mm(bA, bA.ap[:, 0:ntok], wa_[0], wa_[1][:, kc, :], oaT[kc], oaT[kc].ap[:, 0:ntok], kc == 0, kc == 3)
                for kc in range(8):
                    mm(bH, bH.ap[:, 0:ntok], wgb[0], wgb[1][:, kc, :], xT, xT.ap[:, kc, 0:ntok], kc == 0, kc == 7)
                for kc in range(4):
                    mm(bB, bB.ap[:, 0:ntok], wb_[0], wb_[1][:, kc, :], zT[kc], zT[kc].ap[:, 0:ntok], kc == 0, kc == 3)
                sa = nxt("silu", silu_t)
                act(sa, sa.ap[:, 0:ntok], bG, bG.ap[:, 0:ntok], AF.Sigmoid)
                m1 = nxt("f32", f32t)
                tt("dve", m1, m1.ap[:, 0:ntok], bA, bA.ap[:, 0:ntok], sa, sa.ap[:, 0:ntok], ALU.mult)
                sb_ = nxt("silu", silu_t)
                act(sb_, sb_.ap[:, 0:ntok], bH, bH.ap[:, 0:ntok], AF.Sigmoid)
                tt("dve", sb_, sb_.ap[:, 0:ntok], bB, bB.ap[:, 0:ntok], sb_, sb_.ap[:, 0:ntok], ALU.mult)
                tt("dve", mT[c], mT[c].ap[:, 0:ntok], m1, m1.ap[:, 0:ntok], sb_, sb_.ap[:, 0:ntok], ALU.add)
            for n in range(2):
                for kc in range(8):
                    slot, w = wload(s_wo[kc * 128:(kc + 1) * 128, n * 512:(n + 1) * 512], [sc[("wo", kc)]], [128, 512])
                    for t in range(nt):
                        bk = banks[4 + t]
                        mm(bk, bk.ap[:], mT[kc], mT[kc].ap[:, t * 128:(t + 1) * 128], slot, w, kc == 0, kc == 7)
                for t in range(nt):
                    bk = banks[4 + t]
                    stt(xres[t], xres[t].ap[:, n * 512:(n + 1) * 512], xres[t], xres[t].ap[:, n * 512:(n + 1) * 512],
                        ALPHA, bk, bk.ap[:], ALU.mult, ALU.add)
            for t in range(nt):
                layer_norm_tile(xres[t], 2, LN_EPS)

        SM = {}

        def sample_phase():
            from concourse.bass import IndirectOffsetOnAxis
            past = NPG * 128
            S.barrier()
            ALIAS = NTL >= 64 or bool(_os0.environ.get("KALIAS"))
            pools_ = {"v": [Vsel.ap[:].rearrange("p a b c -> p (a b c)"), 0, NTL * 2 * 66],
                      "n": [vnb.ap[:].rearrange("p a b -> p (a b)"), 0, 4 * 512],
                      "k": [Ksel.ap[:, 0, :], 0, NPOS], "k1": [Ksel.ap[:, 1, :], 0, NPOS]}

            def carve(name, shape, dt, where):
                if not ALIAS:
                    return sb(name, shape, dt)
                esz = 2 if dt == BF16 else 4
                n = 1
                for s_ in shape[1:]:
                    n *= s_
                nb16 = (n * esz + 3) // 4 * 2
                for key in where:
                    flat, off, cap = pools_[key]
                    if off + nb16 <= cap:
                        pools_[key][1] = off + nb16
                        v = flat[0:shape[0], off:off + nb16]
                        if dt != BF16:
                            v = v.bitcast(dt)
                        v = v[:, 0:n]
                        if len(shape) == 3:
                            v = v.rearrange("p (a b) -> p a b", a=shape[1])
                        elif len(shape) == 4:
                            v = v.rearrange("p (a b c) -> p a b c", a=shape[1], b=shape[2])
                        return Buf(name, v)
                return sb(name, shape, dt)

            qS = carve("qS", [69, 8, 128], BF16, ["k"])
            gS = sb("gS", [128, 1, 24], F32)
            Kcs = carve("Kcs", [69, 2, NCTS * 128], BF16, ["k"])
            Vcs = carve("Vcs", [128, NCTS, 2, 66], BF16, ["v", "n"])
            pg = [carve("pg%d" % i, [128, 2, 128], F32, ["v", "n"]) for i in range(4)]
            pgb = [carve("pgb%d" % i, [128, 256], BF16, ["v", "n"]) for i in range(4)]
            ptb = sb("ptb", [128, NPG], I32)
            idx = sb("idx", [128, NPG], I32)
            piota = sb("piota", [128, 1], F32)
            Kr = [carve("Kr%d" % i, [69, 2, 128], BF16, ["k"]) for i in range(6)]
            Vr = [carve("Vr%d" % i, [128, 2, 66], BF16, ["v", "n"]) for i in range(6)]
            augt = carve("augt", [69, 3, 128], BF16, ["k"])
            S.dma("sp", augt.ap[64:69, 0, :], scd["c_augt"][:, 0, :], writes=[augt])
            S.dma("sp", augt.ap[64:69, 1, :], scd["c_augt"][:, 1, :], writes=[augt])
            S.dma("sp", augt.ap[64:69, 2, :], scd["c_augt"][:, 2, :], writes=[augt])
            SM["augt"] = augt
            wk32 = [carve("wk32", [128, 4, 128], F32, ["v", "n"]), carve("wv32", [128, 4, 128], F32, ["v", "n"])]
            pscs = carve("pscs", [128, 264], F32, ["v", "n"])
            pscs2 = carve("pscs2", [128, 264], F32, ["v", "n"])
            nsbs = carve("nsbs", [128, 384], BF16, ["v", "n"])
            nsTs = carve("nsTs", [128, 3, 128], BF16, ["v", "n"])
            smb = carve("smb", [128, 264], F32, ["v", "n"])
            sbb = carve("sbb", [128, 264], F32, ["v", "n"])
            w0 = sb("w0", [128, 33], BF16)
            oabS = oab
            rr.update({"pg": 0, "pgb": 0, "Kr": 0, "Vr": 0})
            S.dma("sp", piota.ap[:], scd["c_piota"], writes=[piota])
            S.dma("sp", smb.ap[:], scd["c_smb"], writes=[smb])
            S.dma("sp", sbb.ap[:], scd["c_sbb"], writes=[sbb])
            S.dma("sp", w0.ap[:], scd["c_w0"], writes=[w0])
            S.op("dve", lambda e: e.memset(Kcs.ap[:], 0.0), writes=[Kcs])
            S.op("dve", lambda e: e.memset(Vcs.ap[:], 1.0), writes=[Vcs])
            S.op("dve", lambda e: e.memset(qS.ap[:], 0.0), writes=[qS])
            S.op("dve", lambda e: e.memset(gS.ap[:], 0.0), writes=[gS])
            S.op("dve", lambda e: e.memset(oabS.ap[:], 0.0), writes=[oabS])
            S.op("dve", lambda e: e.memset(nsbs.ap[:], NEGBIG), writes=[nsbs])
            for v_ in Vr:
                S.op("dve", lambda e, v_=v_: e.memset(v_.ap[:], 1.0), writes=[v_])
            for g in range(2):
                S.dma("sp", Kcs.ap[64:69, g, :], scd["c_skcaug"], writes=[Kcs])
            for h in range(8):
                S.dma("sp", qS.ap[64:69, h, :], scd["c_sqaug"][h], writes=[qS])
            wb4 = sb("wb4", [128, 8], F32)
            for hg in range(4):
                S.dma("sp", wb4.ap[:, hg:hg + 1], wd["spatial_w"][hg, 0, 0:1].partition_broadcast(128), writes=[wb4])
                S.dma("sp", wb4.ap[:, 4 + hg:5 + hg], wd["spatial_b"][hg, 0:1].partition_broadcast(128), writes=[wb4])

            xs = xres[0]
            S.dma("sp", xs.ap[:], x_smp, writes=[xs])
            transpose_tile(xs, 0)
            ffn_block(0, 1, 0)
            transpose_tile(xs, 0)
            wl = [wload(s_winu[u], [sc[("win", u)]], [128, 8, 256]) for u in (U_KV, U_KV + 1, U_KV + 2)]
            bkA, bkB = banks[4], banks[5]
            for ui, (slot, w) in enumerate(wl):
                bk, o0 = (bkA, ui * 256) if ui < 2 else (bkB, 0)
                for kc in range(8):
                    mm(bk, bk.ap[:, o0:o0 + 256], xT, xT.ap[:, kc, 0:128], slot, w[:, kc, :], kc == 0, kc == 7)
            kvst = f32t[0]
            cp("act", kvst, kvst.ap[:, 0:512], bkA, bkA.ap[:])
            cp("act", kvst, kvst.ap[:, 512:768], bkB, bkB.ap[:, 0:256])
            out_events.append(S.dma("act", kv_s[:, :], kvst.ap[:, 0:768], reads=[kvst]))
            for hq in range(2):
                slot, w = wload(s_winu[U_Q + hq], [sc[("win", U_Q + hq)]], [128, 8, 256])
                for hh in range(4):
                    h = 4 * hq + hh
                    bk = banks[hh % 4]
                    for kc in range(8):
                        mm(bk, bk.ap[0:64, 0:128], slot, w[:, kc, hh * 64:(hh + 1) * 64], xT, xT.ap[:, kc, 0:128], kc == 0, kc == 7)
                    S.op("act", lambda e, bk=bk, h=h: e.mul(qT.ap[0:64, h, 0:128], bk.ap[0:64, 0:128], 0.125), reads=[bk], writes=[qT])
            slot, w = wload(s_winu[U_G, :, :, 0:24], [sc[("win", U_G)]], [128, 8, 24])
            bk = banks[6]
            for kc in range(8):
                mm(bk, bk.ap[:, 0:24], xT, xT.ap[:, kc, 0:128], slot, w[:, kc, :], kc == 0, kc == 7)
            act(gts, gts.ap[:, 0, :], bk, bk.ap[:, 0:24], AF.Sigmoid)
            uv = []
            for which, U0 in ((0, U_U), (1, U_V)):
                wv = [wload(s_winu[U0 + i], [sc[("win", U0 + i)]], [128, 8, 256]) for i in range(2)]
                bk = banks[4 + which]
                for i in range(2):
                    for kc in range(8):
                        mm(bk, bk.ap[:, i * 256:(i + 1) * 256], xT, xT.ap[:, kc, 0:128], wv[i][0], wv[i][1][:, kc, :], kc == 0, kc == 7)
                dst = f32t[1] if which == 0 else kvst
                if which == 1:
                    pass
                uv.append(dst)
                if which == 0:
                    act(dst, dst.ap[:, 0:512], bk, bk.ap[:], AF.Gelu_apprx_tanh)
            vf = wk32[0]
            vfa = vf.ap[:].rearrange("p a b -> p (a b)")
            act(vf, vfa, banks[5], banks[5].ap[:], AF.Gelu_apprx_tanh)
            vn32 = wk32[1]
            vna = vn32.ap[:].rearrange("p a b -> p (a b)")
            ln_small(vf, vfa, 512, glnp.ap[:, 0, :], glnp.ap[:, 1, :], vn32, vna, LN_EPS)
            out_events.append(S.dma("act", chunkv[:, :], vna, reads=[vn32]))
            for hg in range(4):
                ts("dve", vn32, vna[:, hg * 128:(hg + 1) * 128], vn32, vna[:, hg * 128:(hg + 1) * 128],
                   wb4.ap[:, hg:hg + 1], wb4.ap[:, 4 + hg:5 + hg], ALU.mult, ALU.add, extra_r=[wb4])
            zs = nxt("pT", pTb)
            tt("dve", zs, zs.ap[:], vn32, vna, f32t[1], f32t[1].ap[:, 0:512], ALU.mult)
            for kc in range(4):
                tr(trb, trb_bf[:, kc * 128:(kc + 1) * 128], zs, zs.ap[:, kc * 128:(kc + 1) * 128], identb)
            for kc in range(4):
                cp("act" if kc % 2 else "dve", zT[kc], zT[kc].ap[:, 0:128], trb, trb_bf[:, kc * 128:(kc + 1) * 128])

            for i in range(4):
                S.dma("sp", ptb.ap[:], ptab[i].partition_broadcast(128), writes=[ptb])
                ts("dve", idx, idx.ap[:], ptb, ptb.ap[:], 128.0, piota.ap[:, 0:1], ALU.mult, ALU.add, extra_r=[piota])

                def gather(dst, dap, pool_i, s_):
                    return S.dma("pool", None, None, reads=[idx], writes=[dst],
                                 fn=lambda e: e.indirect_dma_start(out=dap, out_offset=None, in_=pools[pool_i],
                                                                   in_offset=IndirectOffsetOnAxis(ap=idx.ap[:, s_:s_ + 1], axis=0)))

                for kvi in range(2):
                    wt = wk32[kvi]
                    S.dma("sp", wt.ap[:, 0:3, :], cwin[kvi][i, 1:385, :].rearrange("(t p) c -> p t c", p=128), writes=[wt])
                    S.dma("sp", wt.ap[0:127, 3, :], cwin[kvi][i, 385:512, :], writes=[wt])
                    c0 = 512 + kvi * 128
                    S.dma("sp", wt.ap[127:128, 3, :], kvst.ap[i:i + 1, c0:c0 + 128], reads=[kvst], writes=[wt])
                    out_events.append(S.dma("act", swin[kvi][i].rearrange("(t p) c -> p t c", p=128), wt.ap[:], reads=[wt]))
                cp("act", qS, qS.ap[0:64, :, 0:1], qT, qT.ap[0:64, :, i:i + 1])
                S.dma("sp", gS.ap[0:1, 0, :], gts.ap[i:i + 1, 0, :], reads=[gts], writes=[gS])
                S.op("dve", lambda e: e.memset(XT2.ap[:], 0.0), writes=[XT2])
                for s0 in range(0, NPG, 4):
                    npg = min(4, NPG - s0)
                    for t in range(npg):
                        p_ = nxt("pg", pg)
                        gather(p_, p_.ap[:].rearrange("p a b -> p (a b)"), 0, s0 + t)
                        pb_ = nxt("pgb", pgb)
                        cp("dve", pb_, pb_.ap[:], p_, p_.ap[:].rearrange("p a b -> p (a b)"))
                        xb_ = banks[(2 * t) % 4]
                        for kvg in range(4):
                            for par in range(2):
                                mm(xb_, xb_.ap[par * 64:(par + 1) * 64, kvg * 64:(kvg + 1) * 64], pb_, pb_.ap[:, kvg * 64:(kvg + 1) * 64],
                                   selpair, selpair.ap[:, par, :], True, True)
                        cp("dve", XT2, XT2.ap[:, :, 8 + t * 64:8 + (t + 1) * 64], xb_, xb_.ap[:, 0:256].rearrange("p (a m) -> p a m", a=4))
                    compress_block(npg, s0, Kcs, Vcs)
                sample_attention(i, qS, gS, Kcs, Vcs, Kr, Vr, wk32, pg, pgb, pscs, pscs2, nsbs, nsTs, smb, sbb, w0, oabS, gather, kvst)
            for kc in range(4):
                tr(trb, trb_bf[:, kc * 128:(kc + 1) * 128], oabS, oabS.ap[:, kc * 128:(kc + 1) * 128], identb)
            for kc in range(4):
                cp("act" if kc % 2 else "dve", oaT[kc], oaT[kc].ap[:, 0:128], trb, trb_bf[:, kc * 128:(kc + 1) * 128])
            merge_tail(1)
            transpose_tile(xs, 0)
            ffn_block(1, 1, 4)
            out_events.append(S.dma("act", y_s[:, :], xs.ap[:], reads=[xs]))

        def sample_attention(i, qS, gS, Kcs, Vcs, Kr, Vr, wk32, pg, pgb, pscs, pscs2, nsbs, nsTs, smb, sbb, w0, oabS, gather, kvst):
            past = NPG * 128
            q0 = past
            augt = SM["augt"]
            Oc, Ow, Os = banks[2], banks[3], banks[4]
            pslb = [banks[5], banks[6], banks[3], banks[4]]
            kaug = scd["c_skaug"]
            for g in range(2):
                qrhs = qS.ap[:, 4 * g:4 * g + 4, :]
                for ct in range(NCTS):
                    Sb = sbank()
                    mm(Sb, Sb.ap[:].rearrange("p (r q) -> p r q", r=4), Kcs, Kcs.ap[:, g, ct * 128:(ct + 1) * 128], qS, qrhs, True, True)
                    P = nxt("pT", pTb)
                    bnd = q0 - 2048 * ct - 2032 - 31 < 0
                    if bnd:
                        ts("dve", Sb, Sb.ap[:], Sb, Sb.ap[:], 30.0, None, ALU.min)
                    act(P, P.ap[:], Sb, Sb.ap[:], AF.Exp)
                    if bnd:
                        pool_mask(P, q0 - 2048 * ct - 31, -16, 1, ALU.is_ge)
                    pv(Oc, ct == 0, P, Vcs, Vcs.ap[:, ct, g, 0:65])
                    for r in range(4):
                        mm(pslb[r], pslb[r].ap[:, 32 * ct:32 * ct + 33], P, P.ap[:, r * 128:(r + 1) * 128], w0, w0.ap[:], ct == 0, False, skip=True)
                dens(Oc, 0)
                NJ_ = NJS
                ts("dve", pscs, pscs.ap[:, 0:NJ_], pslb[0], pslb[0].ap[:, 0:NJ_], rden.ap[:, 0, 0:1], None, ALU.mult, extra_r=[rden])
                for r in range(1, 4):
                    stt(pscs, pscs.ap[:, 0:NJ_], pslb[r], pslb[r].ap[:, 0:NJ_], rden.ap[:, 0, r:r + 1], pscs, pscs.ap[:, 0:NJ_],
                        ALU.mult, ALU.add, extra_r=[rden])
                tt("dve", pscs, pscs.ap[:, 0:NJ_], pscs, pscs.ap[:, 0:NJ_], smb, smb.ap[:, 0:NJ_], ALU.mult)
                tt("dve", pscs, pscs.ap[:, 0:NJ_], pscs, pscs.ap[:, 0:NJ_], sbb, sbb.ap[:, 0:NJ_], ALU.add)
                S.op("dve", lambda e: e.max(out=m8a.ap[:], in_=pscs.ap[:, 0:NJ_]), reads=[pscs], writes=[m8a])
                S.op("dve", lambda e: e.match_replace(out=pscs2.ap[:, 0:NJ_], in_to_replace=m8a.ap[:], in_values=pscs.ap[:, 0:NJ_], imm_value=-3.0e38),
                     reads=[pscs, m8a], writes=[pscs2])
                S.op("dve", lambda e: e.max(out=m8b.ap[:], in_=pscs2.ap[:, 0:NJ_]), reads=[pscs2], writes=[m8b])
                ts("dve", nsbs, nsbs.ap[:, 0:NJ_], pscs, pscs.ap[:, 0:NJ_], m8b.ap[:, 7:8], NEGBIG, ALU.is_lt, ALU.mult, extra_r=[m8b])
                for ch in range(3):
                    tr(trb, trb_bf[:, ch * 128:(ch + 1) * 128], nsbs, nsbs.ap[:, ch * 128:(ch + 1) * 128], identb)
                cp("act", nsTs, nsTs.ap[:].rearrange("p a b -> p (a b)"), trb, trb_bf[:, 0:384])
                for wtile in range(4):
                    kb = nxt("pgb", pgb)
                    cp("dve", kb, kb.ap[:, 0:128], wk32[0], wk32[0].ap[:, wtile, :])
                    tr(trb, trb_bf[0:64, 512:640], kb, kb.ap[:, g * 64:(g + 1) * 64], identb)
                    K_ = nxt("Kr", Kr)
                    cp("act", K_, K_.ap[0:64, g, :], trb, trb_bf[0:64, 512:640])
                    stt(K_, K_.ap[64:69, g, :], augt, augt.ap[64:69, 1, :], float((past - 512) // 128 + wtile), augt, augt.ap[64:69, 2, :],
                        ALU.mult, ALU.add)
                    V_ = nxt("Vr", Vr)
                    cp("act", V_, V_.ap[:, g, 0:64], wk32[1], wk32[1].ap[:, wtile, g * 64:(g + 1) * 64])
                    Sb = sbank()
                    mm(Sb, Sb.ap[:].rearrange("p (r q) -> p r q", r=4), K_, K_.ap[:, g, :], qS, qrhs, True, True)
                    P = nxt("pT", pTb)
                    act(P, P.ap[:], Sb, Sb.ap[:], AF.Exp)
                    pv(Ow, wtile == 0, P, V_, V_.ap[:, g, 0:65])
                dens(Ow, 2)
                for kt in range(NPG + 1):
                    K_ = nxt("Kr", Kr)
                    V_ = nxt("Vr", Vr)
                    if kt < NPG:
                        p_ = nxt("pg", pg)
                        gather(p_, p_.ap[:].rearrange("p a b -> p (a b)"), 1, kt)
                        kb = nxt("pgb", pgb)
                        cp("dve", kb, kb.ap[:, 0:64], p_, p_.ap[:, 0, g * 64:(g + 1) * 64])
                        cp("act", V_, V_.ap[:, g, 0:64], p_, p_.ap[:, 1, g * 64:(g + 1) * 64])
                    else:
                        kb = nxt("pgb", pgb)
                        S.op("dve", lambda e, kb=kb: e.memset(kb.ap[:, 0:64], 0.0), writes=[kb])
                        S.op("dve", lambda e, V_=V_: e.memset(V_.ap[:, g, 0:64], 0.0), writes=[V_])
                        c0 = 256 + g * 64
                        S.dma("sp", kb.ap[0:1, 0:64], kvst.ap[i:i + 1, c0:c0 + 64], reads=[kvst], writes=[kb]) if False else None
                        nb16 = nxt("pT", pTb)
                        cp("dve", nb16, nb16.ap[:, 0:256], kvst, kvst.ap[:, 256:512])
                        S.dma("sp", kb.ap[0:1, 0:64], nb16.ap[i:i + 1, g * 64:(g + 1) * 64], reads=[nb16], writes=[kb])
                        S.dma("sp", V_.ap[0:1, g, 0:64], nb16.ap[i:i + 1, 128 + g * 64:192 + g * 64], reads=[nb16], writes=[V_])
                    tr(trb, trb_bf[0:64, 512:640], kb, kb.ap[:, 0:64], identb)
                    cp("act", K_, K_.ap[0:64, g, :], trb, trb_bf[0:64, 512:640])
                    stt(K_, K_.ap[64:69, g, :], augt, augt.ap[64:69, 1, :], float(kt), augt, augt.ap[64:69, 0, :], ALU.mult, ALU.add)
                    Sb = sbank()
                    mm(Sb, Sb.ap[:].rearrange("p (r q) -> p r q", r=4), K_, K_.ap[:, g, :], qS, qrhs, True, False)
                    ch, tc = (2 * kt) // 128, (kt % 64) * 128
                    for r in range(4):
                        mm(Sb, Sb.ap[:, r * 128:(r + 1) * 128], Tsel, Tsel.ap[:, tc:tc + 128], nsTs, nsTs.ap[:, ch, :], False, r == 3)
                    P = nxt("pT", pTb)
                    act(P, P.ap[:], Sb, Sb.ap[:], AF.Exp)
                    if kt == NPG:
                        pool_mask(P, 0, -1, 1, ALU.is_ge)
                    pv(Os, kt == 0, P, V_, V_.ap[:, g, 0:65])
                dens(Os, 1)
                gv = gS.ap[:, 0, g * 12:(g + 1) * 12].rearrange("p (r b) -> p r b", b=3)
                for br in range(3):
                    tt("dve", coef, coef.ap[:, br, :], rden, rden.ap[:, br, :], gS, gv[:, :, br], ALU.mult)
                acc = nxt("oa", oa32)
                tmp = nxt("oa", oa32)
                orow = nxt("pT", pTb)
                for br, Ob in ((0, Oc), (1, Os), (2, Ow)):
                    ov = Ob.ap[:, 0:264].rearrange("p (r e) -> p r e", r=4)[:, :, 0:64]
                    cb = coef.ap[:, br, :].unsqueeze(2).broadcast_to([128, 4, 64])
                    if br == 0:
                        tt("dve", acc, acc.ap[:], Ob, ov, coef, cb, ALU.mult)
                    else:
                        tt("dve", tmp, tmp.ap[:], Ob, ov, coef, cb, ALU.mult)
                        if br == 1:
                            tt("dve", acc, acc.ap[:], acc, acc.ap[:], tmp, tmp.ap[:], ALU.add)
                        else:
                            tt("dve", orow, orow.ap[:, 0:256].rearrange("p (r d) -> p r d", r=4), acc, acc.ap[:], tmp, tmp.ap[:], ALU.add)
                S.dma("sp", oabS.ap[i:i + 1, g * 256:(g + 1) * 256], orow.ap[0:1, 0:256], reads=[orow], writes=[oabS])

        import os as _os
        CUT = int(_os.environ.get("KCUT", "99"))
        KSUB = int(_os.environ.get("KSUB", "255"))

        def load_block(src, row0, nt):
            for t in range(nt):
                S.dma("sp", xres[t].ap[:], src[row0 + t * 128:row0 + (t + 1) * 128, :], writes=[xres[t]])
                transpose_tile(xres[t], t)

        for b in range(NBC if CUT >= 1 else 0):
            load_block(x_ctx, b * 512, 4)
            if CUT >= 2:
                ffn_block(0, 4, 0)
            for t in range(4):
                transpose_tile(xres[t], t)
            if CUT >= 3:
                kv_project(4, b * 4, False, 0, b == NBC - 1)
            if CUT >= 4:
                compress_block(4, b * 4)

        for b in range(NBO):
            load_block(x_own, b * 512, 4)
            if CUT >= 2:
                ffn_block(0, 4, 0)
            for t in range(4):
                transpose_tile(xres[t], t)
            if CUT >= 3:
                kv_project(4, NTC + b * 4, True, b * 512, True)
            if CUT >= 4:
                compress_block(4, NTC + b * 4)
            if stage >= 2:
                mixer_block(4, NTC + b * 4)
            if stage >= 3:
                for t in range(4):
                    transpose_tile(xres[t], t)
                ffn_block(1, 4, 4)
            for t in range(4):
                out_events.append(S.dma("act", y_own[b * 512 + t * 128:b * 512 + (t + 1) * 128, :], xres[t].ap[:], reads=[xres[t]]))

        if NPG:
            sample_phase()
        S.finish(out_events)
        S.emit()
    return nc


def sample_consts(NPG):
    import ml_dtypes
    bf = ml_dtypes.bfloat16
    past = NPG * 128
    c = {}
    c["c_piota"] = np.arange(128, dtype=np.float32).reshape(128, 1)
    npos = (NPG + 1) * 128
    pos = np.arange(npos)
    ka = np.zeros((5, npos), np.float32)
    ka[0] = pos % 128
    ka[1] = 1.0
    ka[2] = pos // 128
    ka[3] = 1.0
    c["c_skaug"] = ka.astype(bf)
    NCS = NPG * 8
    NCTS = (NCS + 127) // 128
    ci = np.arange(NCTS * 128)
    kc = np.zeros((5, NCTS * 128), np.float32)
    kc[0] = 16 * (ci % 128)
    kc[1] = 1.0
    kc[2] = 16 * (ci // 128)
    kc[3] = 1.0
    kc[4] = 31.0
    c["c_skcaug"] = kc.astype(bf)
    w = np.arange(512)
    kw = np.zeros((5, 512), np.float32)
    kw[0] = (w % 128) + 1
    kw[1] = 1.0
    kw[2] = (past - 512) // 128 + w // 128
    kw[3] = 1.0
    c["c_swaug"] = kw.astype(bf)
    qa = np.zeros((8, 5, 128), np.float32)
    qr = np.arange(128)
    for h in range(8):
        s = SLOPES[h]
        qa[h, 0] = s
        qa[h, 1] = -s * qr
        qa[h, 2] = s * 128
        qa[h, 3] = -s * 128 * NPG
        qa[h, 4] = s
    c["c_sqaug"] = qa.astype(bf)
    at = np.zeros((5, 3, 128), np.float32)
    at[0, 0] = np.arange(128)
    at[1, 0] = 1.0
    at[3, 0] = 1.0
    at[2, 1] = 1.0
    at[0, 2] = np.arange(128) + 1
    at[1, 2] = 1.0
    at[3, 2] = 1.0
    c["c_augt"] = at.astype(bf)
    W0 = np.zeros((128, 33), np.float32)
    for ir in range(128):
        for jj in range(33):
            if 4 * jj - 1 <= ir <= 4 * jj + 3:
                W0[ir, jj] = 1.0
    c["c_w0"] = W0.astype(bf)
    NJS = 2 * NPG + 1
    smb = np.zeros((128, 264), np.float32)
    sbb = np.zeros((128, 264), np.float32)
    smb[:, :NJS] = 1.0
    for j in (0, NJS - 1, NJS - 2):
        smb[:, j] = 0.0
        sbb[:, j] = 1.0e4
    c["c_smb"] = smb
    c["c_sbb"] = sbb
    return c


_CONST_CACHE = {}


def _consts(ntl, fv):
    key = (ntl, fv)
    if key not in _CONST_CACHE:
        _CONST_CACHE[key] = host_consts(ntl, fv, 8192)
    return _CONST_CACHE[key]


def kernel(**inp):
    xp = np.asarray(inp["x_prompt"], dtype=np.float32)
    xs = np.asarray(inp["x_sample"], dtype=np.float32)
    B, T, _ = xp.shape
    DB = xs.shape[0]
    half_t = T // 2
    nb = half_t // 512
    ptab = np.asarray(inp["page_table"]).astype(np.int32)
    npg = ptab.shape[1]
    pool_names = ("cache_cmp_k", "cache_cmp_v", "cache_sel_k", "cache_sel_v")
    nphys = np.asarray(inp[pool_names[0]]).shape[0]
    pr = [np.asarray(inp[n], dtype=np.float32).reshape(nphys * 128, 128) for n in pool_names]
    pools = [np.concatenate([pr[0], pr[1]], axis=1), np.concatenate([pr[2], pr[3]], axis=1)]
    cwk = np.asarray(inp["cache_win_k"], dtype=np.float32).reshape(DB, 512, 128)
    cwv = np.asarray(inp["cache_win_v"], dtype=np.float32).reshape(DB, 512, 128)
    nc = build(nb, nb, 3, NPG=npg, NPHYS=nphys)
    weights = {n: np.ascontiguousarray(np.asarray(inp[n], dtype=np.float32)) for n in W_NAMES}
    sconst = sample_consts(npg)
    in_maps = []
    for c in range(8):
        b, half = divmod(c, 2)
        m = dict(weights)
        m.update(_consts(2 * nb * 4, nb * 4 if half == 0 else 0))
        m.update(sconst)
        m["x_ctx"] = np.ascontiguousarray(xp[b, 0:half_t])
        m["x_own"] = np.ascontiguousarray(xp[b, half * half_t:(half + 1) * half_t])
        xsm = np.zeros((128, D), np.float32)
        xsm[0:4] = xs[4 * c:4 * c + 4, 0]
        m["x_smp"] = xsm
        for i in range(2):
            m["pool%d" % i] = pools[i]
        m["cwin_k"] = np.ascontiguousarray(cwk[4 * c:4 * c + 4])
        m["cwin_v"] = np.ascontiguousarray(cwv[4 * c:4 * c + 4])
        m["ptab"] = np.ascontiguousarray(ptab[4 * c:4 * c + 4])
        in_maps.append(m)
    res = run_bass_kernel_spmd(nc, in_maps, core_ids=list(range(8))).results
    y_prompt = np.stack([np.concatenate([res[2 * b]["y_own"], res[2 * b + 1]["y_own"]], axis=0) for b in range(B)])
    kv = np.stack([np.concatenate([res[2 * b]["kv_own"], res[2 * b + 1]["kv_own"]], axis=0) for b in range(B)])
    parts = [np.ascontiguousarray(kv[:, :, i * 128:(i + 1) * 128]).reshape(B, T, 2, 64) for i in range(6)]
    p_cmp_k, p_cmp_v, p_sel_k, p_sel_v, kw, vw = parts
    p_win_k = np.ascontiguousarray(kw[:, -512:])
    p_win_v = np.ascontiguousarray(vw[:, -512:])
    y_sample = np.concatenate([res[c]["y_s"][0:4] for c in range(8)], axis=0).reshape(DB, 1, D)
    kvs = np.concatenate([res[c]["kv_s"][0:4] for c in range(8)], axis=0)
    sp = [np.ascontiguousarray(kvs[:, i * 128:(i + 1) * 128]).reshape(DB, 1, 2, 64) for i in range(4)]
    s_win_k = np.concatenate([res[c]["swin_k"] for c in range(8)], axis=0).reshape(DB, 512, 2, 64)
    s_win_v = np.concatenate([res[c]["swin_v"] for c in range(8)], axis=0).reshape(DB, 512, 2, 64)
    s_chunk_v = np.concatenate([res[c]["chunkv"][0:4] for c in range(8)], axis=0).reshape(DB, 1, 512)
    f = lambda a: np.ascontiguousarray(a, dtype=np.float32)
    return (f(y_prompt), f(y_sample), p_cmp_k, p_cmp_v, p_sel_k, p_sel_v, p_win_k, p_win_v,
            sp[0], sp[1], sp[2], sp[3], f(s_win_k), f(s_win_v), f(s_chunk_v))
```

```python
import contextlib
import numpy as np
import concourse.bass as bass
import concourse.mybir as mybir
from concourse.bass_utils import run_bass_kernel_spmd

F32 = mybir.dt.float32
FP8 = mybir.dt.float8e4
BF16 = mybir.dt.bfloat16
I32 = mybir.dt.int32
U32 = mybir.dt.uint32
AF = mybir.ActivationFunctionType
ALU = mybir.AluOpType
AX = mybir.AxisListType


class Buf:
    __slots__ = ("name", "ap", "w", "r", "excl")

    def __init__(self, name, ap=None, excl=False):
        self.name = name
        self.ap = ap
        self.excl = excl
        self.w = {}
        self.r = {}


class Sched:
    import os as _o
    EPOCH = int(_o.environ.get("KEPOCH", "16000"))
    NDMA = 14

    def __init__(self, nc, stack):
        self.nc = nc
        self.st = stack
        self.engs = {"pe": nc.tensor, "act": nc.scalar, "dve": nc.vector, "pool": nc.gpsimd, "sp": nc.sync}
        self.prog = {k: [] for k in self.engs}
        self.cnt = {k: 0 for k in self.engs}
        self.seen = {k: {} for k in self.engs}
        self.esem = {}
        self.dsem = {k: [] for k in self.engs}
        self.dnext = {k: 0 for k in self.engs}
        self.nsem = 0

    def sb(self, name, shape, dt):
        t = self.st.enter_context(self.nc.sbuf_tensor(name, list(shape), dt))
        return Buf(name, t.ap() if hasattr(t, "ap") and callable(getattr(t, "ap")) else t)

    def ps(self, name, shape, dt):
        t = self.st.enter_context(self.nc.psum_tensor(name, list(shape), dt))
        return Buf(name, t.ap() if hasattr(t, "ap") and callable(getattr(t, "ap")) else t, excl=True)

    def _sem(self, name):
        self.nsem += 1
        return self.st.enter_context(self.nc.semaphore(name))

    def _eng_sem(self, e, epoch):
        key = (e, epoch)
        if key not in self.esem:
            self.esem[key] = self._sem("s_%s_%d" % (e, epoch))
        return self.esem[key]

    def _waits(self, e, deps):
        out = []
        for ev in deps:
            if ev is None:
                continue
            if ev[0] == "eng":
                _, src, k = ev
                if src == e and e in ("pe", "sp"):
                    continue
                if self.seen[e].get(src, -1) >= k:
                    continue
                self.seen[e][src] = k
                epoch, idx = divmod(k, self.EPOCH)
                out.append((self._eng_sem(src, epoch), idx + 1))
            else:
                _, sem, val, sid = ev
                if self.seen[e].get(sid, -1) >= val:
                    continue
                self.seen[e][sid] = val
                out.append((sem, val))
        return out

    def _deps(self, reads, writes):
        deps = []
        for b in reads:
            deps.extend(b.w.values())
            if b.excl:
                deps.extend(b.r.values())
        for b in writes:
            deps.extend(b.w.values())
            deps.extend(b.r.values())
        return deps

    def _mark(self, ev, key, reads, writes):
        for b in reads:
            b.r[key] = ev
        for b in writes:
            if ev[0] == "dma" and all(v[0] == "dma" for v in b.w.values()):
                b.w[ev[3]] = ev
            else:
                b.w = {"w": ev}
            b.r = {}

    def op(self, e, fn, reads=(), writes=()):
        waits = self._waits(e, self._deps(reads, writes))
        k = self.cnt[e]
        self.cnt[e] += 1
        epoch, _ = divmod(k, self.EPOCH)
        sem = self._eng_sem(e, epoch)

        def thunk(eng, fn=fn, sem=sem, waits=waits):
            for s, v in waits:
                eng.wait_ge(s, v)
            fn(eng).then_inc(sem, 1)

        self.prog[e].append(thunk)
        ev = ("eng", e, k)
        self._mark(ev, e, reads, writes)
        return ev

    def dma(self, q, out, in_, reads=(), writes=(), fn=None, **kw):
        pool = self.dsem[q]
        if len(pool) < self.NDMA:
            pool.append([self._sem("d_%s_%d" % (q, len(pool))), 0, "d_%s_%d" % (q, len(pool))])
        i = self.dnext[q] % self.NDMA
        self.dnext[q] += 1
        ent = pool[i]
        sem, prev, sid = ent
        deps = self._deps(reads, writes)
        if prev > 0:
            deps.append(("dma", sem, prev, sid))
        waits = self._waits(q, deps)
        tgt = prev + 16
        ent[1] = tgt

        def thunk(eng, waits=waits, sem=sem, out=out, in_=in_, kw=kw, fn=fn):
            for s, v in waits:
                eng.wait_ge(s, v)
            if fn is not None:
                fn(eng).then_inc(sem, 16)
            else:
                eng.dma_start(out=out, in_=in_, **kw).then_inc(sem, 16)

        self.prog[q].append(thunk)
        ev = ("dma", sem, tgt, sid)
        self._mark(ev, "dma_%s_%d" % (q, self.dnext[q]), reads, writes)
        return ev

    def barrier(self):
        evs = []
        for src in ("pe", "act", "dve", "pool"):
            if self.cnt[src] > 0:
                evs.append(("eng", src, self.cnt[src] - 1))
        for q, pool in self.dsem.items():
            for sem, val, sid in pool:
                if val > 0:
                    evs.append(("dma", sem, val, sid))
        for e in ("pe", "act", "dve", "pool", "sp"):
            deps = [ev for ev in evs if not (ev[0] == "eng" and ev[1] == e)]
            waits = self._waits(e, deps)

            def thunk(eng, waits=waits):
                for s, v in waits:
                    eng.wait_ge(s, v)

            self.prog[e].append(thunk)

    def finish(self, evs):
        waits = self._waits("sp", list(evs))

        def thunk(eng, waits=waits):
            for s, v in waits:
                eng.wait_ge(s, v)

        self.prog["sp"].append(thunk)

    def emit(self):
        with self.nc.Block() as block:
            for name, reg in (("pe", "tensor"), ("act", "scalar"), ("dve", "vector"), ("pool", "gpsimd"), ("sp", "sync")):
                prog = self.prog[name]

                def body(eng, prog=prog):
                    for th in prog:
                        th(eng)

                getattr(block, reg)(body)


D = 1024
DFF = 2816
NJ = DFF // 128
INW = 4376
ALPHA = 2.0 ** 0.25
LN_EPS = 1e-5
C_Q, C_KV, C_G, C_U, C_V, C_GA, C_GB = 0, 512, 1280, 1304, 1816, 2328, 3352
SLOPES = [2.0 ** (-(h + 1)) for h in range(8)]
NEGBIG = -30000.0


def host_consts(NTL, first_valid_tile, tmin=0):
    import ml_dtypes
    bf = ml_dtypes.bfloat16
    NPOS = NTL * 128
    NCB = NPOS // 16 - 1
    NCT = (NCB + 127) // 128
    c = {}
    c["c_identf"] = np.eye(128, dtype=np.float32)
    ka = np.zeros((5, NPOS), np.float32)
    pos = np.arange(NPOS)
    ka[0] = pos % 128
    ka[1] = 1.0
    ka[2] = pos // 128
    ka[3] = 1.0
    ka[4] = np.where(pos < first_valid_tile * 128, -1.0e6, 0.0)
    c["c_kaug"] = ka.astype(bf)
    kc = np.zeros((5, NCT * 128), np.float32)
    ci = np.arange(NCT * 128)
    kc[0] = 16 * (ci % 128)
    kc[1] = 1.0
    kc[2] = 16 * (ci // 128)
    kc[3] = 1.0
    kc[4] = 31.0 + np.where(16 * ci < first_valid_tile * 128, -1.0e6, 0.0)
    c["c_kcaug"] = kc.astype(bf)
    qa = np.zeros((8, 5, NPOS), np.float32)
    for h in range(8):
        s = SLOPES[h]
        qa[h, 0] = s
        qa[h, 1] = -s * (pos % 128)
        qa[h, 2] = s * 128
        qa[h, 3] = -s * 128 * (pos // 128)
        qa[h, 4] = s
    c["c_qaug"] = qa.astype(bf)
    npt = max(NPOS, tmin)
    post = np.arange(npt)
    T = np.zeros((128, npt), np.float32)
    T[(post // 64) % 128, post] = 1.0
    c["c_T"] = T.astype(bf)
    W = np.zeros((128, NCT, 128), np.float32)
    for ct in range(NCT):
        for ir in range(128):
            i = ct * 128 + ir
            for j in range(128):
                if 4 * j - 1 <= i <= 4 * j + 3:
                    W[ir, ct, j] = 1.0
    c["c_wimp"] = W.astype(bf)
    Gb = np.zeros((128, 256), np.float32)
    Gm = np.zeros((128, 256), np.float32)
    for q in range(128):
        a = 1 if q >= 64 else 0
        for x in range(256):
            d = x - 128
            if d > a:
                Gb[q, x] = -1.0e30
            elif d >= a - 1:
                Gb[q, x] = 1.0e4
            else:
                Gm[q, x] = 1.0
    c["c_gb"] = Gb
    c["c_gm"] = Gm
    F0 = np.zeros((128, 128), np.float32)
    jf = first_valid_tile * 2
    F0[:, :jf] = -1.0e30
    F0[:, jf] = 1.0e4
    c["c_f0"] = F0
    sp = np.zeros((128, 2, 64), np.float32)
    for m in range(64):
        sp[2 * m, 0, m] = 1.0
        sp[2 * m + 1, 1, m] = 1.0
    c["c_selpair"] = sp.astype(bf)
    c["c_ones"] = np.ones((1, 128), np.float32).astype(bf)
    return c


W_NAMES = ["ffn1_w_in", "ffn1_w_out", "ln1_g", "ln1_b", "w_in", "cmp_pe_k", "cmp_w1_k", "cmp_w2_k",
           "cmp_pe_v", "cmp_w1_v", "cmp_w2_v", "gmlp_ln_g", "gmlp_ln_b", "spatial_w", "spatial_b",
           "w_branch_a", "w_branch_b", "w_out", "ln2_g", "ln2_b", "ffn2_w_in", "ffn2_w_out", "ln3_g", "ln3_b"]
W_SHAPES = {"ffn1_w_in": [1024, 5632], "ffn1_w_out": [2816, 1024], "ln1_g": [1024], "ln1_b": [1024],
            "w_in": [1024, 4376], "cmp_pe_k": [32, 64], "cmp_w1_k": [2048, 128], "cmp_w2_k": [128, 64],
            "cmp_pe_v": [32, 64], "cmp_w1_v": [2048, 128], "cmp_w2_v": [128, 64], "gmlp_ln_g": [512],
            "gmlp_ln_b": [512], "spatial_w": [4, 128, 128], "spatial_b": [4, 128], "w_branch_a": [512, 1024],
            "w_branch_b": [512, 1024], "w_out": [1024, 1024], "ln2_g": [1024], "ln2_b": [1024],
            "ffn2_w_in": [1024, 5632], "ffn2_w_out": [2816, 1024], "ln3_g": [1024], "ln3_b": [1024]}


def build(NBC, NBO, stage=99, NPG=0, NPHYS=0):
    NTC, NTO = 4 * NBC, 4 * NBO
    NTL = NTC + NTO
    NPOS = NTL * 128
    NCB = NPOS // 16 - 1
    NCT = (NCB + 127) // 128
    nc = bass.Bass("TRN2", target_bir_lowering=False)

    def din(name, shape, dt=F32):
        return nc.dram_tensor(name, list(shape), dt, kind="ExternalInput").ap()

    def dout(name, shape, dt=F32):
        return nc.dram_tensor(name, list(shape), dt, kind="ExternalOutput").ap()

    def dint(name, shape, dt=BF16):
        return nc.dram_tensor(name, list(shape), dt, kind="Internal").ap()

    x_ctx = din("x_ctx", [max(NTC, 1) * 128, D])
    x_own = din("x_own", [NTO * 128, D])
    wd = {n: din(n, W_SHAPES[n]) for n in W_NAMES}
    NPT = max(NPOS, 8192) if NPG else NPOS
    cshapes = {"c_identf": ([128, 128], F32), "c_kaug": ([5, NPOS], BF16), "c_kcaug": ([5, NCT * 128], BF16),
               "c_qaug": ([8, 5, NPOS], BF16), "c_T": ([128, NPT], BF16), "c_wimp": ([128, NCT, 128], BF16),
               "c_gb": ([128, 256], F32), "c_gm": ([128, 256], F32), "c_f0": ([128, 128], F32),
               "c_selpair": ([128, 2, 64], BF16), "c_ones": ([1, 128], BF16)}
    cd = {n: din(n, s, dt) for n, (s, dt) in cshapes.items()}
    import os as _os0
    DBG = bool(_os0.environ.get("KDBG"))
    if DBG:
        dbg_oa = dout("dbg_oa", [NTO * 128, 512])
        dbg_zT = dout("dbg_zT", [512, NTO * 128])
        dbg_cf = dout("dbg_cf", [NTO * 128, 2, 12])
        dbg_ps = dout("dbg_ps", [NTO * 128, 2, 128])
    if NPG:
        x_smp = din("x_smp", [128, D])
        pools = [din("pool%d" % i, [NPHYS * 128, 256]) for i in range(2)]
        cwin = [din("cwin_k", [4, 512, 128]), din("cwin_v", [4, 512, 128])]
        ptab = din("ptab", [4, NPG], I32)
        NCS = NPG * 8
        NCTS = (NCS + 127) // 128
        NJS = 2 * NPG + 1
        sconst = {"c_piota": ([128, 1], F32), "c_skaug": ([5, (NPG + 1) * 128], BF16), "c_skcaug": ([5, NCTS * 128], BF16),
                  "c_swaug": ([5, 512], BF16), "c_sqaug": ([8, 5, 128], BF16), "c_w0": ([128, 33], BF16),
                  "c_smb": ([128, 264], F32), "c_sbb": ([128, 264], F32), "c_augt": ([5, 3, 128], BF16)}
        scd = {n: din(n, sh, dt) for n, (sh, dt) in sconst.items()}
        y_s = dout("y_s", [128, D])
        kv_s = dout("kv_s", [128, 768])
        swin = [dout("swin_k", [4, 512, 128]), dout("swin_v", [4, 512, 128])]
        chunkv = dout("chunkv", [128, 512])
    y_own = dout("y_own", [NTO * 128, D])
    kv_own = dout("kv_own", [NTO * 128, 768])

    s_f_in = [dint("s_f1in", [NJ, 128, 2, 8, 128]), dint("s_f2in", [NJ, 128, 2, 8, 128])]
    s_f_out = [dint("s_f1out", [DFF, D]), dint("s_f2out", [DFF, D])]
    WIN_UNITS = [(0, 256), (256, 256), (512, 256), (768, 256), (1024, 256), (1280, 24), (1304, 256), (1560, 256),
                 (1816, 256), (2072, 256)] + [(2328 + 256 * i, 256) for i in range(4)] + [(3352 + 256 * i, 256) for i in range(4)]
    U_Q, U_KV, U_G, U_U, U_V, U_GA, U_GB = 0, 2, 5, 6, 8, 10, 14
    s_winu = dint("s_winu", [len(WIN_UNITS), 128, 8, 256])
    s_wa = dint("s_wa", [8, 128, 4, 128])
    s_wb = dint("s_wb", [8, 128, 4, 128])
    s_wo = dint("s_wo", [D, D])
    s_w1 = [dint("s_w1k", [2048, 128]), dint("s_w1v", [2048, 128])]
    s_w2 = [dint("s_w2k", [128, 64]), dint("s_w2v", [128, 64])]
    s_pe = [dint("s_pek", [2048, 1]), dint("s_pev", [2048, 1])]
    s_sw = dint("s_sw", [4, 128, 128])
    s_sb = dint("s_sb", [4, 128])

    out_events = []
    with contextlib.ExitStack() as st:
        S = Sched(nc, st)
        sb, ps = S.sb, S.ps

        identf = sb("identf", [128, 128], F32)
        identb = sb("identb", [128, 128], BF16)
        lnp = sb("lnp", [128, 6, D], F32)
        glnp = sb("glnp", [128, 2, 512], F32)
        wsT = sb("wsT", [128, 4, 128], BF16)
        bsrow = sb("bsrow", [1, 4, 128], BF16)
        onesrow = sb("onesrow", [1, 128], BF16)
        Tsel = sb("Tsel", [128, NPT], FP8)
        wimp = sb("wimp", [128, NCT, 128], BF16)
        gbt = sb("gbt", [128, 256], F32)
        gmt = sb("gmt", [128, 256], F32)
        f0t = sb("f0t", [128, 128], F32)
        selpair = sb("selpair", [128, 2, 64], BF16)
        w2c = [sb("w2k", [128, 64], BF16), sb("w2v", [128, 64], BF16)]
        pec = [sb("pek", [128, 16], BF16), sb("pev", [128, 16], BF16)]
        pebias = sb("pebias", [128, 2], F32)
        Ksel = sb("Ksel", [69, 2, NPOS], BF16)
        Vsel = sb("Vsel", [128, NTL, 2, 66], BF16)
        Kwin = sb("Kwin", [69, 2, 1024], BF16)
        Vwin = sb("Vwin", [128, 8, 2, 66], BF16)
        Kc = sb("Kc", [69, 2, NCT * 128], BF16)
        Vc = sb("Vc", [128, NCT, 2, 66], BF16)
        XT2 = sb("XT2", [128, 4, 264], BF16)
        xres = [sb("xres%d" % i, [128, D], F32) for i in range(4)]
        xT = sb("xT", [128, 8, 512], BF16)
        chunks = [sb("chunk%d" % i, [128, 512], BF16) for i in range(NJ)]
        qT = sb("qT", [69, 8, 512], BF16)
        NR = 4
        wring = [sb("wring%d" % i, [128, 2048], BF16) for i in range(NR)]
        silu_t = [sb("silu%d" % i, [128, 512], BF16) for i in range(2)]
        pTb = [sb("pT%d" % i, [128, 512], BF16) for i in range(4)]
        f32t = [sb("f32t%d" % i, [128, 792], F32) for i in range(2)]
        stats = sb("stats", [128, 2, 6], F32)
        mv = sb("mv", [128, 2], F32)
        rstd = sb("rstd", [128, 1], F32)
        banks = [ps("bank%d" % i, [128, 512], F32) for i in range(8)]
        gts = sb("gts", [128, 4, 24], F32)
        vnb = sb("vnb", [128, 2, 512], BF16)
        oab = sb("oab", [128, 512], BF16)
        psc = sb("psc", [128, 128], F32)
        psc2 = sb("psc2", [128, 128], F32)
        m8a = sb("m8a", [128, 8], F32)
        m8b = sb("m8b", [128, 8], F32)
        nsb = sb("nsb", [128, 128], BF16)
        nsT = sb("nsT", [128, 128], BF16)
        rden = sb("rden", [128, 3, 4], F32)
        coef = sb("coef", [128, 3, 4], F32)
        oa32 = [sb("oa32_%d" % i, [128, 4, 64], F32) for i in range(2)]

        rr = {"w": 0, "silu": 0, "pT": 0, "f32": 0, "S": 0, "oa": 0}

        def nxt(key, lst):
            i = rr[key] % len(lst)
            rr[key] += 1
            return lst[i]

        def wload(src_ap, reads, shape):
            slot = nxt("w", wring)
            n = 1
            for s_ in shape[1:]:
                n *= s_
            view = slot.ap[:, 0:n]
            if len(shape) == 3:
                view = view.rearrange("p (a b) -> p a b", a=shape[1])
            elif len(shape) == 4:
                view = view.rearrange("p (a b c) -> p a b c", a=shape[1], b=shape[2])
            S.dma("sp", view, src_ap, reads=reads, writes=[slot])
            return slot, view

        def mm(ob, oap, lb, lap, rb, rap, start, stop, extra=(), skip=False):
            S.op("pe", lambda e: e.matmul(oap, lhsT=lap, rhs=rap, start=start, stop=stop, skip_group_check=skip),
                 reads=[lb, rb, *extra], writes=[ob])

        def tr(ob, oap, ib, iap, idb):
            S.op("pe", lambda e: e.transpose(out=oap, in_=iap, identity=idb.ap[:iap.shape[0], :iap.shape[0]]),
                 reads=[ib, idb], writes=[ob])

        def act(ob, oap, ib, iap, func, extra_r=(), **kw):
            S.op("act", lambda e: e.activation(out=oap, in_=iap, func=func, **kw), reads=[ib, *extra_r], writes=[ob])

        def tt(eng, ob, oap, ab, aap, bb, bap, op):
            S.op(eng, lambda e: e.tensor_tensor(out=oap, in0=aap, in1=bap, op=op), reads=[ab, bb], writes=[ob])

        def ts(eng, ob, oap, ib, iap, s1, s2, op0, op1=None, extra_r=()):
            if op1 is None:
                S.op(eng, lambda e: e.tensor_scalar(out=oap, in0=iap, scalar1=s1, scalar2=None, op0=op0),
                     reads=[ib, *extra_r], writes=[ob])
            else:
                S.op(eng, lambda e: e.tensor_scalar(out=oap, in0=iap, scalar1=s1, scalar2=s2, op0=op0, op1=op1),
                     reads=[ib, *extra_r], writes=[ob])

        def stt(ob, oap, ab, aap, scalar, bb, bap, op0, op1, extra_r=()):
            S.op("dve", lambda e: e.scalar_tensor_tensor(out=oap, in0=aap, scalar=scalar, in1=bap, op0=op0, op1=op1),
                 reads=[ab, bb, *extra_r], writes=[ob])

        def cp(eng, ob, oap, ib, iap):
            if eng == "act":
                S.op("act", lambda e: e.copy(out=oap, in_=iap), reads=[ib], writes=[ob])
            else:
                S.op(eng, lambda e: e.tensor_copy(out=oap, in_=iap), reads=[ib], writes=[ob])

        S.dma("sp", identf.ap[:], cd["c_identf"], writes=[identf])
        cp("dve", identb, identb.ap[:], identf, identf.ap[:])
        for i, n in enumerate(["ln1_g", "ln1_b", "ln2_g", "ln2_b", "ln3_g", "ln3_b"]):
            S.dma("sp", lnp.ap[:, i, :], wd[n].partition_broadcast(128), writes=[lnp])
        for i, n in enumerate(["gmlp_ln_g", "gmlp_ln_b"]):
            S.dma("sp", glnp.ap[:, i, :], wd[n].partition_broadcast(128), writes=[glnp])
        for i0 in range(0, NPT, 512):
            tb_ = nxt("pT", pTb)
            S.dma("sp", tb_.ap[:], cd["c_T"][:, i0:i0 + 512], writes=[tb_])
            S.op("dve", lambda e, tb_=tb_, i0=i0: e.tensor_copy(out=Tsel.ap[:, i0:i0 + 512], in_=tb_.ap[:], saturate=False),
                 reads=[tb_], writes=[Tsel])
        S.dma("sp", wimp.ap[:], cd["c_wimp"], writes=[wimp])
        S.dma("sp", gbt.ap[:], cd["c_gb"], writes=[gbt])
        S.dma("sp", gmt.ap[:], cd["c_gm"], writes=[gmt])
        S.dma("sp", f0t.ap[:], cd["c_f0"], writes=[f0t])
        S.dma("sp", selpair.ap[:], cd["c_selpair"], writes=[selpair])
        S.dma("sp", onesrow.ap[:], cd["c_ones"], writes=[onesrow])
        S.op("dve", lambda e: e.memset(Kc.ap[:], 0.0), writes=[Kc])
        for g in range(2):
            S.dma("sp", Ksel.ap[64:69, g, :], cd["c_kaug"], writes=[Ksel])
            S.dma("sp", Kc.ap[64:69, g, :], cd["c_kcaug"], writes=[Kc])
        S.op("dve", lambda e: e.memset(Vsel.ap[:], 1.0), writes=[Vsel])
        S.op("dve", lambda e: e.memset(Vwin.ap[:], 1.0), writes=[Vwin])
        S.op("dve", lambda e: e.memset(Vc.ap[:], 1.0), writes=[Vc])
        S.op("dve", lambda e: e.memset(XT2.ap[:], 0.0), writes=[XT2])

        sc = {}

        def cast(key, out_ap, in_ap):
            b = sc.setdefault(key, Buf(key))
            S.dma("pool", out_ap, in_ap, writes=[b])

        def cast_ffn(f):
            wi = wd["ffn%d_w_in" % (f + 1)]
            wo = wd["ffn%d_w_out" % (f + 1)]
            for j in range(NJ):
                for gu in range(2):
                    c0 = gu * DFF + j * 128
                    cast(("fin", f, j), s_f_in[f][j, :, gu, :, :], wi[:, c0:c0 + 128].rearrange("(kc p) c -> p kc c", p=128))
            for j in range(NJ):
                cast(("fout", f, j), s_f_out[f][j * 128:(j + 1) * 128, :], wo[j * 128:(j + 1) * 128, :])

        cast_ffn(0)
        for u, (c0, ncol) in enumerate(WIN_UNITS):
            cast(("win", u), s_winu[u, :, :, 0:ncol], wd["w_in"][:, c0:c0 + ncol].rearrange("(kc p) c -> p kc c", p=128))
        for kv, (n1, n2, npe) in enumerate((("cmp_w1_k", "cmp_w2_k", "cmp_pe_k"), ("cmp_w1_v", "cmp_w2_v", "cmp_pe_v"))):
            cast("cmpw", s_w1[kv][:, :], wd[n1][:, :])
            cast("cmpw", s_w2[kv][:, :], wd[n2][:, :])
            cast("cmpw", s_pe[kv].rearrange("(a b) o -> a (b o)", b=64), wd[npe][:, :])
        cast("sw", s_sw.rearrange("h a b -> (h a) b"), wd["spatial_w"].rearrange("h a b -> (h a) b"))
        cast("sw", s_sb[:, :], wd["spatial_b"][:, :])
        for c in range(8):
            cast(("wa", c), s_wa[c], wd["w_branch_a"][:, c * 128:(c + 1) * 128].rearrange("(kc p) n -> p kc n", p=128))
            cast(("wb", c), s_wb[c], wd["w_branch_b"][:, c * 128:(c + 1) * 128].rearrange("(kc p) n -> p kc n", p=128))
        for r0 in range(0, 1024, 128):
            cast(("wo", r0 // 128), s_wo[r0:r0 + 128, :], wd["w_out"][r0:r0 + 128, :])
        cast_ffn(1)

        for kv in range(2):
            S.dma("sp", w2c[kv].ap[:], s_w2[kv][:, :], reads=[sc["cmpw"]], writes=[w2c[kv]])
            pl = nxt("pT", pTb)
            S.dma("sp", pl.ap[0:16, 0:128], s_pe[kv].rearrange("(c p) o -> c (p o)", p=128), reads=[sc["cmpw"]], writes=[pl])
            tr(banks[7], banks[7].ap.bitcast(BF16)[:, kv * 16:(kv + 1) * 16], pl, pl.ap[0:16, 0:128], identb)
            cp("dve", pec[kv], pec[kv].ap[:], banks[7], banks[7].ap.bitcast(BF16)[:, kv * 16:(kv + 1) * 16])
        S.dma("sp", bsrow.ap[:], s_sb.rearrange("(o h) t -> o h t", o=1), reads=[sc["sw"]], writes=[bsrow])
        swl = nxt("pT", pTb)
        S.dma("sp", swl.ap[:].rearrange("p (h t) -> p h t", h=4), s_sw.rearrange("h a b -> a h b"), reads=[sc["sw"]], writes=[swl])
        trb = banks[7]
        trb_bf = trb.ap.bitcast(BF16)
        for h in range(4):
            tr(trb, trb_bf[:, h * 128:(h + 1) * 128], swl, swl.ap[:, h * 128:(h + 1) * 128], identb)
        cp("dve", wsT, wsT.ap[:].rearrange("p h t -> p (h t)"), trb, trb_bf[:, 0:512])
        S.op("pool", lambda e: e.affine_select(out=wsT.ap[:], in_=wsT.ap[:], pattern=[[0, 4], [1, 128]],
                                               compare_op=ALU.is_ge, fill=0.0, base=0, channel_multiplier=-1),
             reads=[wsT], writes=[wsT])
        pb = banks[5]
        for kv in range(2):
            w1s, w1v_ = wload(s_w1[kv].rearrange("(c p) h -> p c h", p=128), [sc["cmpw"]], [128, 16, 128])
            for c in range(16):
                mm(pb, pb.ap[:, kv:kv + 1], w1s, w1v_[:, c, :], pec[kv], pec[kv].ap[:, c:c + 1], c == 0, c == 15)
        cp("dve", pebias, pebias.ap[:], pb, pb.ap[:, 0:2])

        def transpose_tile(src, t):
            for half in range(2):
                bk = banks[(2 * t + half) % 4]
                for k in range(4):
                    kc = half * 4 + k
                    tr(bk, bk.ap[:, k * 128:(k + 1) * 128], src, src.ap[:, kc * 128:(kc + 1) * 128], identf)
                eng = "act" if half == 0 else "dve"
                cp(eng, xT, xT.ap[:, half * 4:half * 4 + 4, t * 128:(t + 1) * 128],
                   bk, bk.ap[:].rearrange("p (k n) -> p k n", k=4))

        def layer_norm_tile(xb, gi, eps):
            for h in range(2):
                S.op("dve", lambda e, h=h: e.bn_stats(out=stats.ap[:, h, :], in_=xb.ap[:, h * 512:(h + 1) * 512]),
                     reads=[xb], writes=[stats])
            S.op("dve", lambda e: e.bn_aggr(out=mv.ap[:], in_=stats.ap[:].rearrange("p a b -> p (a b)")),
                 reads=[stats], writes=[mv])
            act(rstd, rstd.ap[:], mv, mv.ap[:, 1:2], AF.Sqrt, bias=eps, scale=1.0)
            S.op("dve", lambda e: e.reciprocal(out=rstd.ap[:], in_=rstd.ap[:]), reads=[rstd], writes=[rstd])
            ts("dve", xb, xb.ap[:], xb, xb.ap[:], mv.ap[:, 0:1], rstd.ap[:, 0:1], ALU.subtract, ALU.mult, extra_r=[mv, rstd])
            tt("dve", xb, xb.ap[:], xb, xb.ap[:], lnp, lnp.ap[:, gi, :], ALU.mult)
            tt("dve", xb, xb.ap[:], xb, xb.ap[:], lnp, lnp.ap[:, gi + 1, :], ALU.add)

        def ffn_block(f, nt, gi):
            ntok = nt * 128
            for j in range(NJ):
                slot, w = wload(s_f_in[f][j], [sc[("fin", f, j)]], [128, 2, 8, 128])
                gb_, ub_ = banks[j % 2], banks[2 + j % 2]
                for gu, bk in ((0, gb_), (1, ub_)):
                    for kc in range(8):
                        mm(bk, bk.ap[:, 0:ntok], slot, w[:, gu, kc, :], xT, xT.ap[:, kc, 0:ntok], kc == 0, kc == 7)
                sl = nxt("silu", silu_t)
                act(sl, sl.ap[:, 0:ntok], gb_, gb_.ap[:, 0:ntok], AF.Silu)
                tt("dve", chunks[j], chunks[j].ap[:, 0:ntok], sl, sl.ap[:, 0:ntok], ub_, ub_.ap[:, 0:ntok], ALU.mult)
            for n in range(2):
                for j in range(NJ):
                    slot, w = wload(s_f_out[f][j * 128:(j + 1) * 128, n * 512:(n + 1) * 512], [sc[("fout", f, j)]], [128, 512])
                    for t in range(nt):
                        bk = banks[4 + t]
                        mm(bk, bk.ap[:], chunks[j], chunks[j].ap[:, t * 128:(t + 1) * 128], slot, w, j == 0, j == NJ - 1)
                for t in range(nt):
                    bk = banks[4 + t]
                    stt(xres[t], xres[t].ap[:, n * 512:(n + 1) * 512], xres[t], xres[t].ap[:, n * 512:(n + 1) * 512],
                        2.0 * ALPHA, bk, bk.ap[:], ALU.mult, ALU.add)
            for t in range(nt):
                layer_norm_tile(xres[t], gi, 4.0 * LN_EPS)

        def kv_project(nt, lt0, is_own, own_row0, need_win):
            wl = []
            for u in ([U_KV, U_KV + 1] + ([U_KV + 2] if need_win else [])):
                wl.append(wload(s_winu[u], [sc[("win", u)]], [128, 8, 256]))
            for t in range(nt):
                lt = lt0 + t
                bkA, bkB = banks[4 + (t % 2)], banks[6]
                for ui, (slot, w) in enumerate(wl):
                    bk, o0 = (bkA, ui * 256) if ui < 2 else (bkB, 0)
                    for kc in range(8):
                        mm(bk, bk.ap[:, o0:o0 + 256], xT, xT.ap[:, kc, t * 128:(t + 1) * 128], slot, w[:, kc, :], kc == 0, kc == 7)
                if is_own and (KSUB & 8):
                    stg = nxt("f32", f32t)
                    cp("act", stg, stg.ap[:, 0:512], bkA, bkA.ap[:])
                    if need_win:
                        cp("act", stg, stg.ap[:, 512:768], bkB, bkB.ap[:, 0:256])
                    r0 = own_row0 + t * 128
                    ncol = 768 if need_win else 512
                    out_events.append(S.dma("act", kv_own[r0:r0 + 128, 0:ncol], stg.ap[:, 0:ncol], reads=[stg]))
                kvb = nxt("pT", pTb)
                cp("dve", kvb, kvb.ap[:], bkA, bkA.ap[:])
                if KSUB & 4:
                    cp("act", Vsel, Vsel.ap[:, lt, :, 0:64], bkA, bkA.ap[:, 384:512].rearrange("p (g d) -> p g d", g=2))
                if KSUB & 2:
                    for g in range(2):
                        tr(trb, trb_bf[0:64, g * 128:(g + 1) * 128], kvb, kvb.ap[:, 256 + g * 64:320 + g * 64], identb)
                if need_win:
                    kwb = nxt("pT", pTb)
                    cp("dve", kwb, kwb.ap[:, 0:256], bkB, bkB.ap[:, 0:256])
                    if KSUB & 2:
                        for g in range(2):
                            tr(trb, trb_bf[0:64, 256 + g * 128:384 + g * 128], kwb, kwb.ap[:, g * 64:(g + 1) * 64], identb)
                    ws = lt % 8
                    if KSUB & 4:
                        cp("act", Vwin, Vwin.ap[:, ws, :, 0:64], bkB, bkB.ap[:, 128:256].rearrange("p (g d) -> p g d", g=2))
                if KSUB & 2:
                    cp("act", Ksel, Ksel.ap[0:64, :, lt * 128:(lt + 1) * 128], trb, trb_bf[0:64, 0:256].rearrange("p (g n) -> p g n", g=2))
                if need_win:
                    ws = lt % 8
                    if KSUB & 2:
                        cp("dve", Kwin, Kwin.ap[0:64, :, ws * 128:(ws + 1) * 128], trb, trb_bf[0:64, 256:512].rearrange("p (g n) -> p g n", g=2))
                    if KSUB & 16:
                        for g in range(2):
                            S.dma("sp", Kwin.ap[64:69, g, ws * 128:(ws + 1) * 128], cd["c_kaug"][:, lt * 128:(lt + 1) * 128], writes=[Kwin])
                xb_ = banks[(2 * t) % 4]
                if not (KSUB & 1):
                    continue
                for kvg in range(4):
                    for par in range(2):
                        mm(xb_, xb_.ap[par * 64:(par + 1) * 64, kvg * 64:(kvg + 1) * 64], kvb, kvb.ap[:, kvg * 64:(kvg + 1) * 64],
                           selpair, selpair.ap[:, par, :], True, True)
                cp("dve", XT2, XT2.ap[:, :, 8 + t * 64:8 + (t + 1) * 64], xb_, xb_.ap[:, 0:256].rearrange("p (a m) -> p a m", a=4))

        def compress_block(nt, lt0, Kc=Kc, Vc=Vc):
            nb = nt * 8
            i_lo = lt0 * 8 - 1
            skip = 1 if i_lo < 0 else 0
            n = nb - skip
            i0 = i_lo + skip
            for kv in range(2):
                hb = banks[kv]
                w1s, w1v_ = wload(s_w1[kv].rearrange("(c p) h -> p c h", p=128), [sc["cmpw"]], [128, 16, 128])
                for g in range(2):
                    a = kv * 2 + g
                    for c in range(16):
                        rhs = XT2.ap[:, a, c + 8 * skip: c + 8 * skip + 8 * (n - 1) + 1: 8]
                        mm(hb, hb.ap[:, g * 64:g * 64 + n], w1s, w1v_[:, c, :], XT2, rhs, c == 0, c == 15)
                gl = nxt("silu", silu_t)
                for g in range(2):
                    act(gl, gl.ap[:, g * 64:g * 64 + n], hb, hb.ap[:, g * 64:g * 64 + n], AF.Gelu_apprx_tanh,
                        extra_r=[pebias], bias=pebias.ap[:, kv:kv + 1], scale=1.0)
                ob = banks[2 + kv]
                if kv == 0:
                    for g in range(2):
                        mm(ob, ob.ap[0:64, g * 64:g * 64 + n], w2c[0], w2c[0].ap[:], gl, gl.ap[:, g * 64:g * 64 + n], True, True)
                    for g in range(2):
                        cp("act", Kc, Kc.ap[0:64, g, i0:i0 + n], ob, ob.ap[0:64, g * 64:g * 64 + n])
                else:
                    for g in range(2):
                        mm(ob, ob.ap[0:n, g * 64:(g + 1) * 64], gl, gl.ap[:, g * 64:g * 64 + n], w2c[1], w2c[1].ap[:], True, True)
                    stg = nxt("pT", pTb)
                    cp("act", stg, stg.ap[0:n, 0:128], ob, ob.ap[0:n, 0:128])
                    i = i0
                    while i < i0 + n:
                        ct, p0 = divmod(i, 128)
                        m = min(i0 + n - i, 128 - p0)
                        S.dma("sp", Vc.ap[p0:p0 + m, ct, :, 0:64], stg.ap[i - i0:i - i0 + m, 0:128].rearrange("p (g d) -> p g d", g=2),
                              reads=[stg], writes=[Vc])
                        i += m
            cp("dve", XT2, XT2.ap[:, :, 0:8], XT2, XT2.ap[:, :, nt * 64:nt * 64 + 8])

        uTc, oaT, zT, mT = chunks[0:4], chunks[4:8], chunks[8:12], chunks[12:20]

        def ln_small(src, sap, n, gap, bap, dst, dap, eps):
            S.op("dve", lambda e: e.bn_stats(out=stats.ap[:, 0, :], in_=sap), reads=[src], writes=[stats])
            S.op("dve", lambda e: e.bn_aggr(out=mv.ap[:], in_=stats.ap[:, 0, :]), reads=[stats], writes=[mv])
            act(rstd, rstd.ap[:], mv, mv.ap[:, 1:2], AF.Sqrt, bias=eps, scale=1.0)
            S.op("dve", lambda e: e.reciprocal(out=rstd.ap[:], in_=rstd.ap[:]), reads=[rstd], writes=[rstd])
            ts("dve", src, sap, src, sap, mv.ap[:, 0:1], rstd.ap[:, 0:1], ALU.subtract, ALU.mult, extra_r=[mv, rstd])
            tt("dve", src, sap, src, sap, glnp, gap, ALU.mult)
            tt("dve", dst, dap, src, sap, glnp, bap, ALU.add)

        def sbank():
            return nxt("S", banks[0:2])

        def pool_mask(P, base, cm, qstep, op):
            S.op("pool", lambda e: e.affine_select(out=P.ap[:].rearrange("p (r q) -> p r q", r=4),
                                                   in_=P.ap[:].rearrange("p (r q) -> p r q", r=4),
                                                   pattern=[[0, 4], [qstep, 128]], compare_op=op, fill=0.0,
                                                   base=base, channel_multiplier=cm), reads=[P], writes=[P])

        def pv(Ob, first, P, Vb, vap):
            for r in range(4):
                mm(Ob, Ob.ap[:, r * 66:r * 66 + 65], P, P.ap[:, r * 128:(r + 1) * 128], Vb, vap,
                   first and r == 0, False, skip=True)

        def dens(Ob, br):
            dv = Ob.ap[:, 0:264].rearrange("p (r e) -> p r e", r=4)[:, :, 64]
            ts("dve", rden, rden.ap[:, br, :], Ob, dv, 1e-30, None, ALU.max)
            S.op("dve", lambda e: e.reciprocal(out=rden.ap[:, br, :], in_=rden.ap[:, br, :]), reads=[rden], writes=[rden])

        CH = []
        for ci in range(2):
            CH.append(dict(
                Ob=banks[2 + 2 * ci], psl=banks[3 + 2 * ci],
                psc=sb("psc_%d" % ci, [128, 128], F32) if ci else psc,
                psc2=sb("psc2_%d" % ci, [128, 128], F32) if ci else psc2,
                m8a=sb("m8a_%d" % ci, [128, 8], F32) if ci else m8a,
                m8b=sb("m8b_%d" % ci, [128, 8], F32) if ci else m8b,
                nsb=sb("nsb_%d" % ci, [128, 128], BF16) if ci else nsb,
                nsT=sb("nsT_%d" % ci, [128, 128], BF16) if ci else nsT,
                rden=sb("rden_%d" % ci, [128, 3, 4], F32) if ci else rden,
                coef=sb("coef_%d" % ci, [128, 3, 4], F32) if ci else coef,
                acc=oa32[0] if ci == 0 else sb("acc_1", [128, 4, 64], F32),
                tmp=oa32[1] if ci == 0 else sb("tmp_1", [128, 4, 64], F32),
                trc=ci * 128))
        SB3 = [banks[0], banks[1], banks[6]]

        def sbank3():
            return nxt("S", SB3)

        def attn_chain(t, lt, g, R):
            q0 = lt * 128
            imax = min((q0 + 96) // 16, NCB - 1)
            nct = imax // 128 + 1
            Ob, psl = R["Ob"], R["psl"]
            psc_, psc2_, m8a_, m8b_, nsb_, nsT_, rden_, coef_, acc, tmp = (R[k] for k in
                ("psc", "psc2", "m8a", "m8b", "nsb", "nsT", "rden", "coef", "acc", "tmp"))
            qrhs = qT.ap[:, 4 * g:4 * g + 4, t * 128:(t + 1) * 128]
            gv = gts.ap[:, t, g * 12:(g + 1) * 12].rearrange("p (r b) -> p r b", b=3)
            ov = Ob.ap[:, 0:264].rearrange("p (r e) -> p r e", r=4)[:, :, 0:64]

            def pv_(first, P, Vb, vap):
                for r in range(4):
                    mm(Ob, Ob.ap[:, r * 66:r * 66 + 65], P, P.ap[:, r * 128:(r + 1) * 128], Vb, vap, first and r == 0, False, skip=True)

            def fin(br, last):
                dv = Ob.ap[:, 0:264].rearrange("p (r e) -> p r e", r=4)[:, :, 64]
                ts("dve", rden_, rden_.ap[:, br, :], Ob, dv, 1e-30, None, ALU.max)
                S.op("dve", lambda e: e.reciprocal(out=rden_.ap[:, br, :], in_=rden_.ap[:, br, :]), reads=[rden_], writes=[rden_])
                tt("dve", coef_, coef_.ap[:, br, :], rden_, rden_.ap[:, br, :], gts, gv[:, :, br], ALU.mult)
                cb = coef_.ap[:, br, :].unsqueeze(2).broadcast_to([128, 4, 64])
                if br == 0:
                    tt("dve", acc, acc.ap[:], Ob, ov, coef_, cb, ALU.mult)
                else:
                    tt("dve", tmp, tmp.ap[:], Ob, ov, coef_, cb, ALU.mult)
                    if not last:
                        tt("dve", acc, acc.ap[:], acc, acc.ap[:], tmp, tmp.ap[:], ALU.add)
                    else:
                        tt("dve", oab, oab.ap[:, g * 256:(g + 1) * 256].rearrange("p (r d) -> p r d", r=4),
                           acc, acc.ap[:], tmp, tmp.ap[:], ALU.add)

            for ct in range(nct):
                Sb = sbank3()
                mm(Sb, Sb.ap[:].rearrange("p (r q) -> p r q", r=4), Kc, Kc.ap[:, g, ct * 128:(ct + 1) * 128], qT, qrhs, True, True)
                yield
                P = nxt("pT", pTb)
                bnd = q0 - 2048 * ct - 2032 - 31 < 0
                if bnd:
                    ts("dve", Sb, Sb.ap[:], Sb, Sb.ap[:], 30.0, None, ALU.min)
                    yield
                act(P, P.ap[:], Sb, Sb.ap[:], AF.Exp)
                yield
                if bnd:
                    pool_mask(P, q0 - 2048 * ct - 31, -16, 1, ALU.is_ge)
                    yield
                pv_(ct == 0, P, Vc, Vc.ap[:, ct, g, 0:65])
                for r in range(4):
                    mm(psl, psl.ap[:, r * 128:(r + 1) * 128], P, P.ap[:, r * 128:(r + 1) * 128], wimp, wimp.ap[:, ct, :],
                       ct == 0 and r == 0, False, skip=True)
                yield
            fin(0, False)
            yield
            ts("dve", psc_, psc_.ap[:], psl, psl.ap[:, 0:128], rden_.ap[:, 0, 0:1], None, ALU.mult, extra_r=[rden_])
            for r in range(1, 4):
                stt(psc_, psc_.ap[:], psl, psl.ap[:, r * 128:(r + 1) * 128], rden_.ap[:, 0, r:r + 1], psc_, psc_.ap[:],
                    ALU.mult, ALU.add, extra_r=[rden_])
            yield
            o = 128 - 2 * lt
            tt("dve", psc_, psc_.ap[:], psc_, psc_.ap[:], gmt, gmt.ap[:, o:o + 128], ALU.mult)
            tt("dve", psc_, psc_.ap[:], psc_, psc_.ap[:], gbt, gbt.ap[:, o:o + 128], ALU.add)
            tt("dve", psc_, psc_.ap[:], psc_, psc_.ap[:], f0t, f0t.ap[:], ALU.add)
            yield
            S.op("dve", lambda e: e.max(out=m8a_.ap[:], in_=psc_.ap[:]), reads=[psc_], writes=[m8a_])
            S.op("dve", lambda e: e.match_replace(out=psc2_.ap[:], in_to_replace=m8a_.ap[:], in_values=psc_.ap[:], imm_value=-3.0e38),
                 reads=[psc_, m8a_], writes=[psc2_])
            yield
            S.op("dve", lambda e: e.max(out=m8b_.ap[:], in_=psc2_.ap[:]), reads=[psc2_], writes=[m8b_])
            ts("dve", nsb_, nsb_.ap[:], psc_, psc_.ap[:], m8b_.ap[:, 7:8], NEGBIG, ALU.is_lt, ALU.mult, extra_r=[m8b_])
            yield
            c0 = R["trc"]
            tr(trb, trb_bf[:, c0:c0 + 128], nsb_, nsb_.ap[:], identb)
            yield
            cp("act", nsT_, nsT_.ap[:], trb, trb_bf[:, c0:c0 + 128])
            yield
            kts = [kt for kt in range(lt - 4, lt + 1) if kt >= 0]
            for idx_, kt in enumerate(kts):
                ws = kt % 8
                Sb = sbank3()
                mm(Sb, Sb.ap[:].rearrange("p (r q) -> p r q", r=4), Kwin, Kwin.ap[:, g, ws * 128:(ws + 1) * 128], qT, qrhs, True, True)
                yield
                P = nxt("pT", pTb)
                act(P, P.ap[:], Sb, Sb.ap[:], AF.Exp)
                yield
                if kt == lt:
                    pool_mask(P, 0, -1, 1, ALU.is_ge)
                    yield
                if kt == lt - 4:
                    pool_mask(P, 0, 1, -1, ALU.is_gt)
                    yield
                pv_(idx_ == 0, P, Vwin, Vwin.ap[:, ws, g, 0:65])
                yield
            fin(2, False)
            yield
            for kt in range(0, lt + 1):
                Sb = sbank3()
                mm(Sb, Sb.ap[:].rearrange("p (r q) -> p r q", r=4), Ksel, Ksel.ap[:, g, kt * 128:(kt + 1) * 128], qT, qrhs, True, False)
                for r in range(4):
                    mm(Sb, Sb.ap[:, r * 128:(r + 1) * 128], Tsel, Tsel.ap[:, kt * 128:(kt + 1) * 128], nsT_, nsT_.ap[:], False, r == 3)
                yield
                P = nxt("pT", pTb)
                act(P, P.ap[:], Sb, Sb.ap[:], AF.Exp)
                yield
                if kt == lt:
                    pool_mask(P, 0, -1, 1, ALU.is_ge)
                    yield
                pv_(kt == 0, P, Vsel, Vsel.ap[:, kt, g, 0:65])
                yield
            fin(1, True)
            yield

        def attention_tile(t, lt):
            gens = [attn_chain(t, lt, 0, CH[0]), attn_chain(t, lt, 1, CH[1])]
            live = list(gens)
            while live:
                for gch in list(live):
                    try:
                        next(gch)
                    except StopIteration:
                        live.remove(gch)
            if DBG:
                d32 = nxt("f32", f32t)
                cp("dve", d32, d32.ap[:, 0:512], oab, oab.ap[:])
                r0 = (lt - NTC) * 128
                out_events.append(S.dma("sp", dbg_oa[r0:r0 + 128, :], d32.ap[:, 0:512], reads=[d32]))
            for kc in range(4):
                tr(trb, trb_bf[:, 256 + kc * 128:384 + kc * 128], oab, oab.ap[:, kc * 128:(kc + 1) * 128], identb)
            for kc in range(4):
                cp("act" if kc % 2 else "dve", oaT[kc], oaT[kc].ap[:, t * 128:(t + 1) * 128], trb, trb_bf[:, 256 + kc * 128:384 + kc * 128])

        def mixer_block(nt, lt0):
            ntok = nt * 128
            for hq in range(2):
                slot, w = wload(s_winu[U_Q + hq], [sc[("win", U_Q + hq)]], [128, 8, 256])
                for hh in range(4):
                    h = 4 * hq + hh
                    bk = banks[hh % 4]
                    for kc in range(8):
                        mm(bk, bk.ap[0:64, 0:ntok], slot, w[:, kc, hh * 64:(hh + 1) * 64], xT, xT.ap[:, kc, 0:ntok], kc == 0, kc == 7)
                    S.op("act", lambda e, bk=bk, h=h: e.mul(qT.ap[0:64, h, 0:ntok], bk.ap[0:64, 0:ntok], 0.125), reads=[bk], writes=[qT])
            for h in range(8):
                S.dma("sp", qT.ap[64:69, h, 0:ntok], cd["c_qaug"][h, :, lt0 * 128:lt0 * 128 + ntok], writes=[qT])
            slot, w = wload(s_winu[U_G, :, :, 0:24], [sc[("win", U_G)]], [128, 8, 24])
            for t in range(nt):
                bk = banks[4 + t % 2]
                for kc in range(8):
                    mm(bk, bk.ap[:, 0:24], xT, xT.ap[:, kc, t * 128:(t + 1) * 128], slot, w[:, kc, :], kc == 0, kc == 7)
                act(gts, gts.ap[:, t, :], bk, bk.ap[:, 0:24], AF.Sigmoid)
            for uu in range(2):
                slot, w = wload(s_winu[U_U + uu], [sc[("win", U_U + uu)]], [128, 8, 256])
                for cc in range(2):
                    c = uu * 2 + cc
                    bk = banks[c % 4]
                    for kc in range(8):
                        mm(bk, bk.ap[:, 0:ntok], slot, w[:, kc, cc * 128:(cc + 1) * 128], xT, xT.ap[:, kc, 0:ntok], kc == 0, kc == 7)
                    act(uTc[c], uTc[c].ap[:, 0:ntok], bk, bk.ap[:, 0:ntok], AF.Gelu_apprx_tanh)
            wv = [wload(s_winu[U_V + i], [sc[("win", U_V + i)]], [128, 8, 256]) for i in range(2)]
            for t in range(nt):
                bk = banks[4 + t % 2]
                for i in range(2):
                    for kc in range(8):
                        mm(bk, bk.ap[:, i * 256:(i + 1) * 256], xT, xT.ap[:, kc, t * 128:(t + 1) * 128], wv[i][0], wv[i][1][:, kc, :], kc == 0, kc == 7)
                vf = nxt("f32", f32t)
                act(vf, vf.ap[:, 0:512], bk, bk.ap[:], AF.Gelu_apprx_tanh)
                ln_small(vf, vf.ap[:, 0:512], 512, glnp.ap[:, 0, :], glnp.ap[:, 1, :], vnb, vnb.ap[:, t % 2, :], LN_EPS)
                gb_ = banks[6]
                for hg in range(4):
                    mm(gb_, gb_.ap[:, hg * 128:(hg + 1) * 128], vnb, vnb.ap[:, t % 2, hg * 128:(hg + 1) * 128], wsT, wsT.ap[:, hg, :], True, False)
                    mm(gb_, gb_.ap[:, hg * 128:(hg + 1) * 128], onesrow, onesrow.ap[0:1, :], bsrow, bsrow.ap[0:1, hg, :], False, True)
                for hg in range(4):
                    tt("dve", zT[hg], zT[hg].ap[:, t * 128:(t + 1) * 128], gb_, gb_.ap[:, hg * 128:(hg + 1) * 128],
                       uTc[hg], uTc[hg].ap[:, t * 128:(t + 1) * 128], ALU.mult)
            if DBG:
                for hg in range(4):
                    d32 = nxt("f32", f32t)
                    cp("dve", d32, d32.ap[:, 0:ntok], zT[hg], zT[hg].ap[:, 0:ntok])
                    c0 = (lt0 - NTC) * 128
                    out_events.append(S.dma("sp", dbg_zT[hg * 128:(hg + 1) * 128, c0:c0 + ntok], d32.ap[:, 0:ntok], reads=[d32]))
            for t in range(nt):
                attention_tile(t, lt0 + t)
            merge_tail(nt)

        def merge_tail(nt):
            ntok = nt * 128
            for c in range(8):
                cc = c % 2
                wga = wload(s_winu[U_GA + c // 2, :, :, cc * 128:(cc + 1) * 128], [sc[("win", U_GA + c // 2)]], [128, 8, 128])
                wgb = wload(s_winu[U_GB + c // 2, :, :, cc * 128:(cc + 1) * 128], [sc[("win", U_GB + c // 2)]], [128, 8, 128])
                wa_ = wload(s_wa[c], [sc[("wa", c)]], [128, 4, 128])
                wb_ = wload(s_wb[c], [sc[("wb", c)]], [128, 4, 128])
                bA, bG, bB, bH = banks[0], banks[1], banks[2], banks[3]
                for kc in range(8):
                    mm(bG, bG.ap[:, 0:ntok], wga[0], wga[1][:, kc, :], xT, xT.ap[:, kc, 0:ntok], kc == 0, kc == 7)
                for kc in range(4):
                    mm(bA, bA.ap[:, 0:ntok], wa_[0], wa_[1][:, kc, :], oaT[kc], oaT[kc].ap[:, 0:ntok], kc == 0, kc == 3)
                for kc in range(8):
                    mm(bH, bH.ap[:, 0:ntok], wgb[0], wgb[1][:, kc, :], xT, xT.ap[:, kc, 0:ntok], kc == 0, kc == 7)
                for kc in range(4):
                    mm(bB, bB.ap[:, 0:ntok], wb_[0], wb_[1][:, kc, :], zT[kc], zT[kc].ap[:, 0:ntok], kc == 0, kc == 3)
                sa = nxt("silu", silu_t)
                act(sa, sa.ap[:, 0:ntok], bG, bG.ap[:, 0:ntok], AF.Sigmoid)
                m1 = nxt("f32", f32t)
                tt("dve", m1, m1.ap[:, 0:ntok], bA, bA.ap[:, 0:ntok], sa, sa.ap[:, 0:ntok], ALU.mult)
                sb_ = nxt("silu", silu_t)
                act(sb_, sb_.ap[:, 0:ntok], bH, bH.ap[:, 0:ntok], AF.Sigmoid)
                tt("dve", sb_, sb_.ap[:, 0:ntok], bB, bB.ap[:, 0:ntok], sb_, sb_.ap[:, 0:ntok], ALU.mult)
                tt("dve", mT[c], mT[c].ap[:, 0:ntok], m1, m1.ap[:, 0:ntok], sb_, sb_.ap[:, 0:ntok], ALU.add)
            for n in range(2):
                for kc in range(8):
                    slot, w = wload(s_wo[kc * 128:(kc + 1) * 128, n * 512:(n + 1) * 512], [sc[("wo", kc)]], [128, 512])
                    for t in range(nt):
                        bk = banks[4 + t]
                        mm(bk, bk.ap[:], mT[kc], mT[kc].ap[:, t * 128:(t + 1) * 128], slot, w, kc == 0, kc == 7)
                for t in range(nt):
                    bk = banks[4 + t]
                    stt(xres[t], xres[t].ap[:, n * 512:(n + 1) * 512], xres[t], xres[t].ap[:, n * 512:(n + 1) * 512],
                        ALPHA, bk, bk.ap[:], ALU.mult, ALU.add)
            for t in range(nt):
                layer_norm_tile(xres[t], 2, LN_EPS)

        SM = {}

        def sample_phase():
            from concourse.bass import IndirectOffsetOnAxis
            past = NPG * 128
            S.barrier()
            ALIAS = NTL >= 64 or bool(_os0.environ.get("KALIAS"))
            pools_ = {"v": [Vsel.ap[:].rearrange("p a b c -> p (a b c)"), 0, NTL * 2 * 66],
                      "n": [vnb.ap[:].rearrange("p a b -> p (a b)"), 0, 2 * 512],
                      "k": [Ksel.ap[:, 0, :], 0, NPOS], "k1": [Ksel.ap[:, 1, :], 0, NPOS]}

            def carve(name, shape, dt, where):
                if not ALIAS:
                    return sb(name, shape, dt)
                esz = 2 if dt == BF16 else 4
                n = 1
                for s_ in shape[1:]:
                    n *= s_
                nb16 = (n * esz + 3) // 4 * 2
                for key in where:
                    flat, off, cap = pools_[key]
                    if off + nb16 <= cap:
                        pools_[key][1] = off + nb16
                        v = flat[0:shape[0], off:off + nb16]
                        if dt != BF16:
                            v = v.bitcast(dt)
                        v = v[:, 0:n]
                        if len(shape) == 3:
                            v = v.rearrange("p (a b) -> p a b", a=shape[1])
                        elif len(shape) == 4:
                            v = v.rearrange("p (a b c) -> p a b c", a=shape[1], b=shape[2])
                        return Buf(name, v)
                return sb(name, shape, dt)

            qS = carve("qS", [69, 8, 128], BF16, ["k"])
            gS = sb("gS", [128, 1, 24], F32)
            Kcs = carve("Kcs", [69, 2, NCTS * 128], BF16, ["k"])
            Vcs = carve("Vcs", [128, NCTS, 2, 66], BF16, ["v", "n"])
            pg = [carve("pg%d" % i, [128, 2, 128], F32, ["v", "n"]) for i in range(4)]
            pgb = [carve("pgb%d" % i, [128, 256], BF16, ["v", "n"]) for i in range(4)]
            ptb = sb("ptb", [128, NPG], I32)
            idx = sb("idx", [128, NPG], I32)
            piota = sb("piota", [128, 1], F32)
            Kr = [carve("Kr%d" % i, [69, 2, 128], BF16, ["k"]) for i in range(6)]
            Vr = [carve("Vr%d" % i, [128, 2, 66], BF16, ["v", "n"]) for i in range(6)]
            augt = carve("augt", [69, 3, 128], BF16, ["k"])
            S.dma("sp", augt.ap[64:69, 0, :], scd["c_augt"][:, 0, :], writes=[augt])
            S.dma("sp", augt.ap[64:69, 1, :], scd["c_augt"][:, 1, :], writes=[augt])
            S.dma("sp", augt.ap[64:69, 2, :], scd["c_augt"][:, 2, :], writes=[augt])
            SM["augt"] = augt
            wk32 = [carve("wk32", [128, 4, 128], F32, ["v", "n"]), carve("wv32", [128, 4, 128], F32, ["v", "n"])]
            pscs = carve("pscs", [128, 264], F32, ["v", "n"])
            pscs2 = carve("pscs2", [128, 264], F32, ["v", "n"])
            nsbs = carve("nsbs", [128, 384], BF16, ["v", "n"])
            nsTs = carve("nsTs", [128, 3, 128], BF16, ["v", "n"])
            smb = carve("smb", [128, 264], F32, ["v", "n"])
            sbb = carve("sbb", [128, 264], F32, ["v", "n"])
            w0 = sb("w0", [128, 33], BF16)
            oabS = oab
            rr.update({"pg": 0, "pgb": 0, "Kr": 0, "Vr": 0})
            S.dma("sp", piota.ap[:], scd["c_piota"], writes=[piota])
            S.dma("sp", smb.ap[:], scd["c_smb"], writes=[smb])
            S.dma("sp", sbb.ap[:], scd["c_sbb"], writes=[sbb])
            S.dma("sp", w0.ap[:], scd["c_w0"], writes=[w0])
            S.op("dve", lambda e: e.memset(Kcs.ap[:], 0.0), writes=[Kcs])
            S.op("dve", lambda e: e.memset(Vcs.ap[:], 1.0), writes=[Vcs])
            S.op("dve", lambda e: e.memset(qS.ap[:], 0.0), writes=[qS])
            S.op("dve", lambda e: e.memset(gS.ap[:], 0.0), writes=[gS])
            S.op("dve", lambda e: e.memset(oabS.ap[:], 0.0), writes=[oabS])
            S.op("dve", lambda e: e.memset(nsbs.ap[:], NEGBIG), writes=[nsbs])
            for v_ in Vr:
                S.op("dve", lambda e, v_=v_: e.memset(v_.ap[:], 1.0), writes=[v_])
            for g in range(2):
                S.dma("sp", Kcs.ap[64:69, g, :], scd["c_skcaug"], writes=[Kcs])
            for h in range(8):
                S.dma("sp", qS.ap[64:69, h, :], scd["c_sqaug"][h], writes=[qS])
            wb4 = sb("wb4", [128, 8], F32)
            for hg in range(4):
                S.dma("sp", wb4.ap[:, hg:hg + 1], wd["spatial_w"][hg, 0, 0:1].partition_broadcast(128), writes=[wb4])
                S.dma("sp", wb4.ap[:, 4 + hg:5 + hg], wd["spatial_b"][hg, 0:1].partition_broadcast(128), writes=[wb4])

            xs = xres[0]
            S.dma("sp", xs.ap[:], x_smp, writes=[xs])
            transpose_tile(xs, 0)
            ffn_block(0, 1, 0)
            transpose_tile(xs, 0)
            wl = [wload(s_winu[u], [sc[("win", u)]], [128, 8, 256]) for u in (U_KV, U_KV + 1, U_KV + 2)]
            bkA, bkB = banks[4], banks[5]
            for ui, (slot, w) in enumerate(wl):
                bk, o0 = (bkA, ui * 256) if ui < 2 else (bkB, 0)
                for kc in range(8):
                    mm(bk, bk.ap[:, o0:o0 + 256], xT, xT.ap[:, kc, 0:128], slot, w[:, kc, :], kc == 0, kc == 7)
            kvst = f32t[0]
            cp("act", kvst, kvst.ap[:, 0:512], bkA, bkA.ap[:])
            cp("act", kvst, kvst.ap[:, 512:768], bkB, bkB.ap[:, 0:256])
            out_events.append(S.dma("act", kv_s[:, :], kvst.ap[:, 0:768], reads=[kvst]))
            for hq in range(2):
                slot, w = wload(s_winu[U_Q + hq], [sc[("win", U_Q + hq)]], [128, 8, 256])
                for hh in range(4):
                    h = 4 * hq + hh
                    bk = banks[hh % 4]
                    for kc in range(8):
                        mm(bk, bk.ap[0:64, 0:128], slot, w[:, kc, hh * 64:(hh + 1) * 64], xT, xT.ap[:, kc, 0:128], kc == 0, kc == 7)
                    S.op("act", lambda e, bk=bk, h=h: e.mul(qT.ap[0:64, h, 0:128], bk.ap[0:64, 0:128], 0.125), reads=[bk], writes=[qT])
            slot, w = wload(s_winu[U_G, :, :, 0:24], [sc[("win", U_G)]], [128, 8, 24])
            bk = banks[6]
            for kc in range(8):
                mm(bk, bk.ap[:, 0:24], xT, xT.ap[:, kc, 0:128], slot, w[:, kc, :], kc == 0, kc == 7)
            act(gts, gts.ap[:, 0, :], bk, bk.ap[:, 0:24], AF.Sigmoid)
            uv = []
            for which, U0 in ((0, U_U), (1, U_V)):
                wv = [wload(s_winu[U0 + i], [sc[("win", U0 + i)]], [128, 8, 256]) for i in range(2)]
                bk = banks[4 + which]
                for i in range(2):
                    for kc in range(8):
                        mm(bk, bk.ap[:, i * 256:(i + 1) * 256], xT, xT.ap[:, kc, 0:128], wv[i][0], wv[i][1][:, kc, :], kc == 0, kc == 7)
                dst = f32t[1] if which == 0 else kvst
                if which == 1:
                    pass
                uv.append(dst)
                if which == 0:
                    act(dst, dst.ap[:, 0:512], bk, bk.ap[:], AF.Gelu_apprx_tanh)
            vf = wk32[0]
            vfa = vf.ap[:].rearrange("p a b -> p (a b)")
            act(vf, vfa, banks[5], banks[5].ap[:], AF.Gelu_apprx_tanh)
            vn32 = wk32[1]
            vna = vn32.ap[:].rearrange("p a b -> p (a b)")
            ln_small(vf, vfa, 512, glnp.ap[:, 0, :], glnp.ap[:, 1, :], vn32, vna, LN_EPS)
            out_events.append(S.dma("act", chunkv[:, :], vna, reads=[vn32]))
            for hg in range(4):
                ts("dve", vn32, vna[:, hg * 128:(hg + 1) * 128], vn32, vna[:, hg * 128:(hg + 1) * 128],
                   wb4.ap[:, hg:hg + 1], wb4.ap[:, 4 + hg:5 + hg], ALU.mult, ALU.add, extra_r=[wb4])
            zs = nxt("pT", pTb)
            tt("dve", zs, zs.ap[:], vn32, vna, f32t[1], f32t[1].ap[:, 0:512], ALU.mult)
            for kc in range(4):
                tr(trb, trb_bf[:, kc * 128:(kc + 1) * 128], zs, zs.ap[:, kc * 128:(kc + 1) * 128], identb)
            for kc in range(4):
                cp("act" if kc % 2 else "dve", zT[kc], zT[kc].ap[:, 0:128], trb, trb_bf[:, kc * 128:(kc + 1) * 128])

            for i in range(4):
                S.dma("sp", ptb.ap[:], ptab[i].partition_broadcast(128), writes=[ptb])
                ts("dve", idx, idx.ap[:], ptb, ptb.ap[:], 128.0, piota.ap[:, 0:1], ALU.mult, ALU.add, extra_r=[piota])

                def gather(dst, dap, pool_i, s_):
                    return S.dma("pool", None, None, reads=[idx], writes=[dst],
                                 fn=lambda e: e.indirect_dma_start(out=dap, out_offset=None, in_=pools[pool_i],
                                                                   in_offset=IndirectOffsetOnAxis(ap=idx.ap[:, s_:s_ + 1], axis=0)))

                for kvi in range(2):
                    wt = wk32[kvi]
                    S.dma("sp", wt.ap[:, 0:3, :], cwin[kvi][i, 1:385, :].rearrange("(t p) c -> p t c", p=128), writes=[wt])
                    S.dma("sp", wt.ap[0:127, 3, :], cwin[kvi][i, 385:512, :], writes=[wt])
                    c0 = 512 + kvi * 128
                    S.dma("sp", wt.ap[127:128, 3, :], kvst.ap[i:i + 1, c0:c0 + 128], reads=[kvst], writes=[wt])
                    out_events.append(S.dma("act", swin[kvi][i].rearrange("(t p) c -> p t c", p=128), wt.ap[:], reads=[wt]))
                cp("act", qS, qS.ap[0:64, :, 0:1], qT, qT.ap[0:64, :, i:i + 1])
                S.dma("sp", gS.ap[0:1, 0, :], gts.ap[i:i + 1, 0, :], reads=[gts], writes=[gS])
                S.op("dve", lambda e: e.memset(XT2.ap[:], 0.0), writes=[XT2])
                for s0 in range(0, NPG, 4):
                    npg = min(4, NPG - s0)
                    for t in range(npg):
                        p_ = nxt("pg", pg)
                        gather(p_, p_.ap[:].rearrange("p a b -> p (a b)"), 0, s0 + t)
                        pb_ = nxt("pgb", pgb)
                        cp("dve", pb_, pb_.ap[:], p_, p_.ap[:].rearrange("p a b -> p (a b)"))
                        xb_ = banks[(2 * t) % 4]
                        for kvg in range(4):
                            for par in range(2):
                                mm(xb_, xb_.ap[par * 64:(par + 1) * 64, kvg * 64:(kvg + 1) * 64], pb_, pb_.ap[:, kvg * 64:(kvg + 1) * 64],
                                   selpair, selpair.ap[:, par, :], True, True)
                        cp("dve", XT2, XT2.ap[:, :, 8 + t * 64:8 + (t + 1) * 64], xb_, xb_.ap[:, 0:256].rearrange("p (a m) -> p a m", a=4))
                    compress_block(npg, s0, Kcs, Vcs)
                sample_attention(i, qS, gS, Kcs, Vcs, Kr, Vr, wk32, pg, pgb, pscs, pscs2, nsbs, nsTs, smb, sbb, w0, oabS, gather, kvst)
            for kc in range(4):
                tr(trb, trb_bf[:, kc * 128:(kc + 1) * 128], oabS, oabS.ap[:, kc * 128:(kc + 1) * 128], identb)
            for kc in range(4):
                cp("act" if kc % 2 else "dve", oaT[kc], oaT[kc].ap[:, 0:128], trb, trb_bf[:, kc * 128:(kc + 1) * 128])
            merge_tail(1)
            transpose_tile(xs, 0)
            ffn_block(1, 1, 4)
            out_events.append(S.dma("act", y_s[:, :], xs.ap[:], reads=[xs]))

        def sample_attention(i, qS, gS, Kcs, Vcs, Kr, Vr, wk32, pg, pgb, pscs, pscs2, nsbs, nsTs, smb, sbb, w0, oabS, gather, kvst):
            past = NPG * 128
            q0 = past
            augt = SM["augt"]
            Oc, Ow, Os = banks[2], banks[3], banks[4]
            pslb = [banks[5], banks[6], banks[3], banks[4]]
            kaug = scd["c_skaug"]
            for g in range(2):
                qrhs = qS.ap[:, 4 * g:4 * g + 4, :]
                for ct in range(NCTS):
                    Sb = sbank()
                    mm(Sb, Sb.ap[:].rearrange("p (r q) -> p r q", r=4), Kcs, Kcs.ap[:, g, ct * 128:(ct + 1) * 128], qS, qrhs, True, True)
                    P = nxt("pT", pTb)
                    bnd = q0 - 2048 * ct - 2032 - 31 < 0
                    if bnd:
                        ts("dve", Sb, Sb.ap[:], Sb, Sb.ap[:], 30.0, None, ALU.min)
                    act(P, P.ap[:], Sb, Sb.ap[:], AF.Exp)
                    if bnd:
                        pool_mask(P, q0 - 2048 * ct - 31, -16, 1, ALU.is_ge)
                    pv(Oc, ct == 0, P, Vcs, Vcs.ap[:, ct, g, 0:65])
                    for r in range(4):
                        mm(pslb[r], pslb[r].ap[:, 32 * ct:32 * ct + 33], P, P.ap[:, r * 128:(r + 1) * 128], w0, w0.ap[:], ct == 0, False, skip=True)
                dens(Oc, 0)
                NJ_ = NJS
                ts("dve", pscs, pscs.ap[:, 0:NJ_], pslb[0], pslb[0].ap[:, 0:NJ_], rden.ap[:, 0, 0:1], None, ALU.mult, extra_r=[rden])
                for r in range(1, 4):
                    stt(pscs, pscs.ap[:, 0:NJ_], pslb[r], pslb[r].ap[:, 0:NJ_], rden.ap[:, 0, r:r + 1], pscs, pscs.ap[:, 0:NJ_],
                        ALU.mult, ALU.add, extra_r=[rden])
                tt("dve", pscs, pscs.ap[:, 0:NJ_], pscs, pscs.ap[:, 0:NJ_], smb, smb.ap[:, 0:NJ_], ALU.mult)
                tt("dve", pscs, pscs.ap[:, 0:NJ_], pscs, pscs.ap[:, 0:NJ_], sbb, sbb.ap[:, 0:NJ_], ALU.add)
                S.op("dve", lambda e: e.max(out=m8a.ap[:], in_=pscs.ap[:, 0:NJ_]), reads=[pscs], writes=[m8a])
                S.op("dve", lambda e: e.match_replace(out=pscs2.ap[:, 0:NJ_], in_to_replace=m8a.ap[:], in_values=pscs.ap[:, 0:NJ_], imm_value=-3.0e38),
                     reads=[pscs, m8a], writes=[pscs2])
                S.op("dve", lambda e: e.max(out=m8b.ap[:], in_=pscs2.ap[:, 0:NJ_]), reads=[pscs2], writes=[m8b])
                ts("dve", nsbs, nsbs.ap[:, 0:NJ_], pscs, pscs.ap[:, 0:NJ_], m8b.ap[:, 7:8], NEGBIG, ALU.is_lt, ALU.mult, extra_r=[m8b])
                for ch in range(3):
                    tr(trb, trb_bf[:, ch * 128:(ch + 1) * 128], nsbs, nsbs.ap[:, ch * 128:(ch + 1) * 128], identb)
                cp("act", nsTs, nsTs.ap[:].rearrange("p a b -> p (a b)"), trb, trb_bf[:, 0:384])
                for wtile in range(4):
                    kb = nxt("pgb", pgb)
                    cp("dve", kb, kb.ap[:, 0:128], wk32[0], wk32[0].ap[:, wtile, :])
                    tr(trb, trb_bf[0:64, 512:640], kb, kb.ap[:, g * 64:(g + 1) * 64], identb)
                    K_ = nxt("Kr", Kr)
                    cp("act", K_, K_.ap[0:64, g, :], trb, trb_bf[0:64, 512:640])
                    stt(K_, K_.ap[64:69, g, :], augt, augt.ap[64:69, 1, :], float((past - 512) // 128 + wtile), augt, augt.ap[64:69, 2, :],
                        ALU.mult, ALU.add)
                    V_ = nxt("Vr", Vr)
                    cp("act", V_, V_.ap[:, g, 0:64], wk32[1], wk32[1].ap[:, wtile, g * 64:(g + 1) * 64])
                    Sb = sbank()
                    mm(Sb, Sb.ap[:].rearrange("p (r q) -> p r q", r=4), K_, K_.ap[:, g, :], qS, qrhs, True, True)
                    P = nxt("pT", pTb)
                    act(P, P.ap[:], Sb, Sb.ap[:], AF.Exp)
                    pv(Ow, wtile == 0, P, V_, V_.ap[:, g, 0:65])
                dens(Ow, 2)
                for kt in range(NPG + 1):
                    K_ = nxt("Kr", Kr)
                    V_ = nxt("Vr", Vr)
                    if kt < NPG:
                        p_ = nxt("pg", pg)
                        gather(p_, p_.ap[:].rearrange("p a b -> p (a b)"), 1, kt)
                        kb = nxt("pgb", pgb)
                        cp("dve", kb, kb.ap[:, 0:64], p_, p_.ap[:, 0, g * 64:(g + 1) * 64])
                        cp("act", V_, V_.ap[:, g, 0:64], p_, p_.ap[:, 1, g * 64:(g + 1) * 64])
                    else:
                        kb = nxt("pgb", pgb)
                        S.op("dve", lambda e, kb=kb: e.memset(kb.ap[:, 0:64], 0.0), writes=[kb])
                        S.op("dve", lambda e, V_=V_: e.memset(V_.ap[:, g, 0:64], 0.0), writes=[V_])
                        c0 = 256 + g * 64
                        S.dma("sp", kb.ap[0:1, 0:64], kvst.ap[i:i + 1, c0:c0 + 64], reads=[kvst], writes=[kb]) if False else None
                        nb16 = nxt("pT", pTb)
                        cp("dve", nb16, nb16.ap[:, 0:256], kvst, kvst.ap[:, 256:512])
                        S.dma("sp", kb.ap[0:1, 0:64], nb16.ap[i:i + 1, g * 64:(g + 1) * 64], reads=[nb16], writes=[kb])
                        S.dma("sp", V_.ap[0:1, g, 0:64], nb16.ap[i:i + 1, 128 + g * 64:192 + g * 64], reads=[nb16], writes=[V_])
                    tr(trb, trb_bf[0:64, 512:640], kb, kb.ap[:, 0:64], identb)
                    cp("act", K_, K_.ap[0:64, g, :], trb, trb_bf[0:64, 512:640])
                    stt(K_, K_.ap[64:69, g, :], augt, augt.ap[64:69, 1, :], float(kt), augt, augt.ap[64:69, 0, :], ALU.mult, ALU.add)
                    Sb = sbank()
                    mm(Sb, Sb.ap[:].rearrange("p (r q) -> p r q", r=4), K_, K_.ap[:, g, :], qS, qrhs, True, False)
                    ch, tc = (2 * kt) // 128, (kt % 64) * 128
                    for r in range(4):
                        mm(Sb, Sb.ap[:, r * 128:(r + 1) * 128], Tsel, Tsel.ap[:, tc:tc + 128], nsTs, nsTs.ap[:, ch, :], False, r == 3)
                    P = nxt("pT", pTb)
                    act(P, P.ap[:], Sb, Sb.ap[:], AF.Exp)
                    if kt == NPG:
                        pool_mask(P, 0, -1, 1, ALU.is_ge)
                    pv(Os, kt == 0, P, V_, V_.ap[:, g, 0:65])
                dens(Os, 1)
                gv = gS.ap[:, 0, g * 12:(g + 1) * 12].rearrange("p (r b) -> p r b", b=3)
                for br in range(3):
                    tt("dve", coef, coef.ap[:, br, :], rden, rden.ap[:, br, :], gS, gv[:, :, br], ALU.mult)
                acc = nxt("oa", oa32)
                tmp = nxt("oa", oa32)
                orow = nxt("pT", pTb)
                for br, Ob in ((0, Oc), (1, Os), (2, Ow)):
                    ov = Ob.ap[:, 0:264].rearrange("p (r e) -> p r e", r=4)[:, :, 0:64]
                    cb = coef.ap[:, br, :].unsqueeze(2).broadcast_to([128, 4, 64])
                    if br == 0:
                        tt("dve", acc, acc.ap[:], Ob, ov, coef, cb, ALU.mult)
                    else:
                        tt("dve", tmp, tmp.ap[:], Ob, ov, coef, cb, ALU.mult)
                        if br == 1:
                            tt("dve", acc, acc.ap[:], acc, acc.ap[:], tmp, tmp.ap[:], ALU.add)
                        else:
                            tt("dve", orow, orow.ap[:, 0:256].rearrange("p (r d) -> p r d", r=4), acc, acc.ap[:], tmp, tmp.ap[:], ALU.add)
                S.dma("sp", oabS.ap[i:i + 1, g * 256:(g + 1) * 256], orow.ap[0:1, 0:256], reads=[orow], writes=[oabS])

        import os as _os
        CUT = int(_os.environ.get("KCUT", "99"))
        KSUB = int(_os.environ.get("KSUB", "255"))

        def load_block(src, row0, nt):
            for t in range(nt):
                S.dma("sp", xres[t].ap[:], src[row0 + t * 128:row0 + (t + 1) * 128, :], writes=[xres[t]])
                transpose_tile(xres[t], t)

        for b in range(NBC if CUT >= 1 else 0):
            load_block(x_ctx, b * 512, 4)
            if CUT >= 2:
                ffn_block(0, 4, 0)
            for t in range(4):
                transpose_tile(xres[t], t)
            if CUT >= 3:
                kv_project(4, b * 4, False, 0, b == NBC - 1)
            if CUT >= 4:
                compress_block(4, b * 4)

        for b in range(NBO):
            load_block(x_own, b * 512, 4)
            if CUT >= 2:
                ffn_block(0, 4, 0)
            for t in range(4):
                transpose_tile(xres[t], t)
            if CUT >= 3:
                kv_project(4, NTC + b * 4, True, b * 512, True)
            if CUT >= 4:
                compress_block(4, NTC + b * 4)
            if stage >= 2:
                mixer_block(4, NTC + b * 4)
            if stage >= 3:
                for t in range(4):
                    transpose_tile(xres[t], t)
                ffn_block(1, 4, 4)
            for t in range(4):
                out_events.append(S.dma("act", y_own[b * 512 + t * 128:b * 512 + (t + 1) * 128, :], xres[t].ap[:], reads=[xres[t]]))

        if NPG:
            sample_phase()
        S.finish(out_events)
        S.emit()
    return nc


def sample_consts(NPG):
    import ml_dtypes
    bf = ml_dtypes.bfloat16
    past = NPG * 128
    c = {}
    c["c_piota"] = np.arange(128, dtype=np.float32).reshape(128, 1)
    npos = (NPG + 1) * 128
    pos = np.arange(npos)
    ka = np.zeros((5, npos), np.float32)
    ka[0] = pos % 128
    ka[1] = 1.0
    ka[2] = pos // 128
    ka[3] = 1.0
    c["c_skaug"] = ka.astype(bf)
    NCS = NPG * 8
    NCTS = (NCS + 127) // 128
    ci = np.arange(NCTS * 128)
    kc = np.zeros((5, NCTS * 128), np.float32)
    kc[0] = 16 * (ci % 128)
    kc[1] = 1.0
    kc[2] = 16 * (ci // 128)
    kc[3] = 1.0
    kc[4] = 31.0
    c["c_skcaug"] = kc.astype(bf)
    w = np.arange(512)
    kw = np.zeros((5, 512), np.float32)
    kw[0] = (w % 128) + 1
    kw[1] = 1.0
    kw[2] = (past - 512) // 128 + w // 128
    kw[3] = 1.0
    c["c_swaug"] = kw.astype(bf)
    qa = np.zeros((8, 5, 128), np.float32)
    qr = np.arange(128)
    for h in range(8):
        s = SLOPES[h]
        qa[h, 0] = s
        qa[h, 1] = -s * qr
        qa[h, 2] = s * 128
        qa[h, 3] = -s * 128 * NPG
        qa[h, 4] = s
    c["c_sqaug"] = qa.astype(bf)
    at = np.zeros((5, 3, 128), np.float32)
    at[0, 0] = np.arange(128)
    at[1, 0] = 1.0
    at[3, 0] = 1.0
    at[2, 1] = 1.0
    at[0, 2] = np.arange(128) + 1
    at[1, 2] = 1.0
    at[3, 2] = 1.0
    c["c_augt"] = at.astype(bf)
    W0 = np.zeros((128, 33), np.float32)
    for ir in range(128):
        for jj in range(33):
            if 4 * jj - 1 <= ir <= 4 * jj + 3:
                W0[ir, jj] = 1.0
    c["c_w0"] = W0.astype(bf)
    NJS = 2 * NPG + 1
    smb = np.zeros((128, 264), np.float32)
    sbb = np.zeros((128, 264), np.float32)
    smb[:, :NJS] = 1.0
    for j in (0, NJS - 1, NJS - 2):
        smb[:, j] = 0.0
        sbb[:, j] = 1.0e4
    c["c_smb"] = smb
    c["c_sbb"] = sbb
    return c


_CONST_CACHE = {}


def _consts(ntl, fv):
    key = (ntl, fv)
    if key not in _CONST_CACHE:
        _CONST_CACHE[key] = host_consts(ntl, fv, 8192)
    return _CONST_CACHE[key]


def kernel(**inp):
    xp = np.asarray(inp["x_prompt"], dtype=np.float32)
    xs = np.asarray(inp["x_sample"], dtype=np.float32)
    B, T, _ = xp.shape
    DB = xs.shape[0]
    half_t = T // 2
    nb = half_t // 512
    ptab = np.asarray(inp["page_table"]).astype(np.int32)
    npg = ptab.shape[1]
    pool_names = ("cache_cmp_k", "cache_cmp_v", "cache_sel_k", "cache_sel_v")
    nphys = np.asarray(inp[pool_names[0]]).shape[0]
    pr = [np.asarray(inp[n], dtype=np.float32).reshape(nphys * 128, 128) for n in pool_names]
    pools = [np.concatenate([pr[0], pr[1]], axis=1), np.concatenate([pr[2], pr[3]], axis=1)]
    cwk = np.asarray(inp["cache_win_k"], dtype=np.float32).reshape(DB, 512, 128)
    cwv = np.asarray(inp["cache_win_v"], dtype=np.float32).reshape(DB, 512, 128)
    nc = build(nb, nb, 3, NPG=npg, NPHYS=nphys)
    weights = {n: np.ascontiguousarray(np.asarray(inp[n], dtype=np.float32)) for n in W_NAMES}
    sconst = sample_consts(npg)
    in_maps = []
    for c in range(8):
        b, half = divmod(c, 2)
        m = dict(weights)
        m.update(_consts(2 * nb * 4, nb * 4 if half == 0 else 0))
        m.update(sconst)
        m["x_ctx"] = np.ascontiguousarray(xp[b, 0:half_t])
        m["x_own"] = np.ascontiguousarray(xp[b, half * half_t:(half + 1) * half_t])
        xsm = np.zeros((128, D), np.float32)
        xsm[0:4] = xs[4 * c:4 * c + 4, 0]
        m["x_smp"] = xsm
        for i in range(2):
            m["pool%d" % i] = pools[i]
        m["cwin_k"] = np.ascontiguousarray(cwk[4 * c:4 * c + 4])
        m["cwin_v"] = np.ascontiguousarray(cwv[4 * c:4 * c + 4])
        m["ptab"] = np.ascontiguousarray(ptab[4 * c:4 * c + 4])
        in_maps.append(m)
    res = run_bass_kernel_spmd(nc, in_maps, core_ids=list(range(8))).results
    y_prompt = np.stack([np.concatenate([res[2 * b]["y_own"], res[2 * b + 1]["y_own"]], axis=0) for b in range(B)])
    kv = np.stack([np.concatenate([res[2 * b]["kv_own"], res[2 * b + 1]["kv_own"]], axis=0) for b in range(B)])
    parts = [np.ascontiguousarray(kv[:, :, i * 128:(i + 1) * 128]).reshape(B, T, 2, 64) for i in range(6)]
    p_cmp_k, p_cmp_v, p_sel_k, p_sel_v, kw, vw = parts
    p_win_k = np.ascontiguousarray(kw[:, -512:])
    p_win_v = np.ascontiguousarray(vw[:, -512:])
    y_sample = np.concatenate([res[c]["y_s"][0:4] for c in range(8)], axis=0).reshape(DB, 1, D)
    kvs = np.concatenate([res[c]["kv_s"][0:4] for c in range(8)], axis=0)
    sp = [np.ascontiguousarray(kvs[:, i * 128:(i + 1) * 128]).reshape(DB, 1, 2, 64) for i in range(4)]
    s_win_k = np.concatenate([res[c]["swin_k"] for c in range(8)], axis=0).reshape(DB, 512, 2, 64)
    s_win_v = np.concatenate([res[c]["swin_v"] for c in range(8)], axis=0).reshape(DB, 512, 2, 64)
    s_chunk_v = np.concatenate([res[c]["chunkv"][0:4] for c in range(8)], axis=0).reshape(DB, 1, 512)
    f = lambda a: np.ascontiguousarray(a, dtype=np.float32)
    return (f(y_prompt), f(y_sample), p_cmp_k, p_cmp_v, p_sel_k, p_sel_v, p_win_k, p_win_v,
            sp[0], sp[1], sp[2], sp[3], f(s_win_k), f(s_win_v), f(s_chunk_v))
```

```python
import contextlib
import numpy as np
import concourse.bass as bass
import concourse.mybir as mybir
from concourse.bass_utils import run_bass_kernel_spmd

F32 = mybir.dt.float32
FP8 = mybir.dt.float8e4
BF16 = mybir.dt.bfloat16
I32 = mybir.dt.int32
U32 = mybir.dt.uint32
AF = mybir.ActivationFunctionType
ALU = mybir.AluOpType
AX = mybir.AxisListType


class Buf:
    __slots__ = ("name", "ap", "w", "r", "excl")

    def __init__(self, name, ap=None, excl=False):
        self.name = name
        self.ap = ap
        self.excl = excl
        self.w = {}
        self.r = {}


class Sched:
    import os as _o
    EPOCH = int(_o.environ.get("KEPOCH", "16000"))
    NDMA = 14

    def __init__(self, nc, stack):
        self.nc = nc
        self.st = stack
        self.engs = {"pe": nc.tensor, "act": nc.scalar, "dve": nc.vector, "pool": nc.gpsimd, "sp": nc.sync}
        self.prog = {k: [] for k in self.engs}
        self.cnt = {k: 0 for k in self.engs}
        self.seen = {k: {} for k in self.engs}
        self.esem = {}
        self.dsem = {k: [] for k in self.engs}
        self.dnext = {k: 0 for k in self.engs}
        self.nsem = 0

    def sb(self, name, shape, dt):
        t = self.st.enter_context(self.nc.sbuf_tensor(name, list(shape), dt))
        return Buf(name, t.ap() if hasattr(t, "ap") and callable(getattr(t, "ap")) else t)

    def ps(self, name, shape, dt):
        t = self.st.enter_context(self.nc.psum_tensor(name, list(shape), dt))
        return Buf(name, t.ap() if hasattr(t, "ap") and callable(getattr(t, "ap")) else t, excl=True)

    def _sem(self, name):
        self.nsem += 1
        return self.st.enter_context(self.nc.semaphore(name))

    def _eng_sem(self, e, epoch):
        key = (e, epoch)
        if key not in self.esem:
            self.esem[key] = self._sem("s_%s_%d" % (e, epoch))
        return self.esem[key]

    def _waits(self, e, deps):
        out = []
        for ev in deps:
            if ev is None:
                continue
            if ev[0] == "eng":
                _, src, k = ev
                if src == e and e in ("pe", "sp"):
                    continue
                if self.seen[e].get(src, -1) >= k:
                    continue
                self.seen[e][src] = k
                epoch, idx = divmod(k, self.EPOCH)
                out.append((self._eng_sem(src, epoch), idx + 1))
            else:
                _, sem, val, sid = ev
                if self.seen[e].get(sid, -1) >= val:
                    continue
                self.seen[e][sid] = val
                out.append((sem, val))
        return out

    def _deps(self, reads, writes):
        deps = []
        for b in reads:
            deps.extend(b.w.values())
            if b.excl:
                deps.extend(b.r.values())
        for b in writes:
            deps.extend(b.w.values())
            deps.extend(b.r.values())
        return deps

    def _mark(self, ev, key, reads, writes):
        for b in reads:
            b.r[key] = ev
        for b in writes:
            if ev[0] == "dma" and all(v[0] == "dma" for v in b.w.values()):
                b.w[ev[3]] = ev
            else:
                b.w = {"w": ev}
            b.r = {}

    def op(self, e, fn, reads=(), writes=()):
        waits = self._waits(e, self._deps(reads, writes))
        k = self.cnt[e]
        self.cnt[e] += 1
        epoch, _ = divmod(k, self.EPOCH)
        sem = self._eng_sem(e, epoch)

        def thunk(eng, fn=fn, sem=sem, waits=waits):
            for s, v in waits:
                eng.wait_ge(s, v)
            fn(eng).then_inc(sem, 1)

        self.prog[e].append(thunk)
        ev = ("eng", e, k)
        self._mark(ev, e, reads, writes)
        return ev

    def dma(self, q, out, in_, reads=(), writes=(), fn=None, **kw):
        pool = self.dsem[q]
        if len(pool) < self.NDMA:
            pool.append([self._sem("d_%s_%d" % (q, len(pool))), 0, "d_%s_%d" % (q, len(pool))])
        i = self.dnext[q] % self.NDMA
        self.dnext[q] += 1
        ent = pool[i]
        sem, prev, sid = ent
        deps = self._deps(reads, writes)
        if prev > 0:
            deps.append(("dma", sem, prev, sid))
        waits = self._waits(q, deps)
        tgt = prev + 16
        ent[1] = tgt

        def thunk(eng, waits=waits, sem=sem, out=out, in_=in_, kw=kw, fn=fn):
            for s, v in waits:
                eng.wait_ge(s, v)
            if fn is not None:
                fn(eng).then_inc(sem, 16)
            else:
                eng.dma_start(out=out, in_=in_, **kw).then_inc(sem, 16)

        self.prog[q].append(thunk)
        ev = ("dma", sem, tgt, sid)
        self._mark(ev, "dma_%s_%d" % (q, self.dnext[q]), reads, writes)
        return ev

    def barrier(self):
        evs = []
        for src in ("pe", "act", "dve", "pool"):
            if self.cnt[src] > 0:
                evs.append(("eng", src, self.cnt[src] - 1))
        for q, pool in self.dsem.items():
            for sem, val, sid in pool:
                if val > 0:
                    evs.append(("dma", sem, val, sid))
        for e in ("pe", "act", "dve", "pool", "sp"):
            deps = [ev for ev in evs if not (ev[0] == "eng" and ev[1] == e)]
            waits = self._waits(e, deps)

            def thunk(eng, waits=waits):
                for s, v in waits:
                    eng.wait_ge(s, v)

            self.prog[e].append(thunk)

    def finish(self, evs):
        waits = self._waits("sp", list(evs))

        def thunk(eng, waits=waits):
            for s, v in waits:
                eng.wait_ge(s, v)

        self.prog["sp"].append(thunk)

    def emit(self):
        with self.nc.Block() as block:
            for name, reg in (("pe", "tensor"), ("act", "scalar"), ("dve", "vector"), ("pool", "gpsimd"), ("sp", "sync")):
                prog = self.prog[name]

                def body(eng, prog=prog):
                    for th in prog:
                        th(eng)

                getattr(block, reg)(body)


D = 1024
DFF = 2816
NJ = DFF // 128
INW = 4376
ALPHA = 2.0 ** 0.25
LN_EPS = 1e-5
C_Q, C_KV, C_G, C_U, C_V, C_GA, C_GB = 0, 512, 1280, 1304, 1816, 2328, 3352
SLOPES = [2.0 ** (-(h + 1)) for h in range(8)]
NEGBIG = -30000.0


def host_consts(NTL, first_valid_tile, tmin=0):
    import ml_dtypes
    bf = ml_dtypes.bfloat16
    NPOS = NTL * 128
    NCB = NPOS // 16 - 1
    NCT = (NCB + 127) // 128
    c = {}
    c["c_identf"] = np.eye(128, dtype=np.float32)
    ka = np.zeros((5, NPOS), np.float32)
    pos = np.arange(NPOS)
    ka[0] = pos % 128
    ka[1] = 1.0
    ka[2] = pos // 128
    ka[3] = 1.0
    ka[4] = np.where(pos < first_valid_tile * 128, -1.0e6, 0.0)
    c["c_kaug"] = ka.astype(bf)
    kc = np.zeros((5, NCT * 128), np.float32)
    ci = np.arange(NCT * 128)
    kc[0] = 16 * (ci % 128)
    kc[1] = 1.0
    kc[2] = 16 * (ci // 128)
    kc[3] = 1.0
    kc[4] = 31.0 + np.where(16 * ci < first_valid_tile * 128, -1.0e6, 0.0)
    c["c_kcaug"] = kc.astype(bf)
    qa = np.zeros((8, 5, NPOS), np.float32)
    for h in range(8):
        s = SLOPES[h]
        qa[h, 0] = s
        qa[h, 1] = -s * (pos % 128)
        qa[h, 2] = s * 128
        qa[h, 3] = -s * 128 * (pos // 128)
        qa[h, 4] = s
    c["c_qaug"] = qa.astype(bf)
    npt = max(NPOS, tmin)
    post = np.arange(npt)
    T = np.zeros((128, npt), np.float32)
    T[(post // 64) % 128, post] = 1.0
    c["c_T"] = T.astype(bf)
    W = np.zeros((128, NCT, 128), np.float32)
    for ct in range(NCT):
        for ir in range(128):
            i = ct * 128 + ir
            for j in range(128):
                if 4 * j - 1 <= i <= 4 * j + 3:
                    W[ir, ct, j] = 1.0
    c["c_wimp"] = W.astype(bf)
    Gb = np.zeros((128, 256), np.float32)
    Gm = np.zeros((128, 256), np.float32)
    for q in range(128):
        a = 1 if q >= 64 else 0
        for x in range(256):
            d = x - 128
            if d > a:
                Gb[q, x] = -1.0e30
            elif d >= a - 1:
                Gb[q, x] = 1.0e4
            else:
                Gm[q, x] = 1.0
    c["c_gb"] = Gb
    c["c_gm"] = Gm
    F0 = np.zeros((128, 128), np.float32)
    jf = first_valid_tile * 2
    F0[:, :jf] = -1.0e30
    F0[:, jf] = 1.0e4
    c["c_f0"] = F0
    sp = np.zeros((128, 2, 64), np.float32)
    for m in range(64):
        sp[2 * m, 0, m] = 1.0
        sp[2 * m + 1, 1, m] = 1.0
    c["c_selpair"] = sp.astype(bf)
    c["c_ones"] = np.ones((1, 128), np.float32).astype(bf)
    return c


W_NAMES = ["ffn1_w_in", "ffn1_w_out", "ln1_g", "ln1_b", "w_in", "cmp_pe_k", "cmp_w1_k", "cmp_w2_k",
           "cmp_pe_v", "cmp_w1_v", "cmp_w2_v", "gmlp_ln_g", "gmlp_ln_b", "spatial_w", "spatial_b",
           "w_branch_a", "w_branch_b", "w_out", "ln2_g", "ln2_b", "ffn2_w_in", "ffn2_w_out", "ln3_g", "ln3_b"]
W_SHAPES = {"ffn1_w_in": [1024, 5632], "ffn1_w_out": [2816, 1024], "ln1_g": [1024], "ln1_b": [1024],
            "w_in": [1024, 4376], "cmp_pe_k": [32, 64], "cmp_w1_k": [2048, 128], "cmp_w2_k": [128, 64],
            "cmp_pe_v": [32, 64], "cmp_w1_v": [2048, 128], "cmp_w2_v": [128, 64], "gmlp_ln_g": [512],
            "gmlp_ln_b": [512], "spatial_w": [4, 128, 128], "spatial_b": [4, 128], "w_branch_a": [512, 1024],
            "w_branch_b": [512, 1024], "w_out": [1024, 1024], "ln2_g": [1024], "ln2_b": [1024],
            "ffn2_w_in": [1024, 5632], "ffn2_w_out": [2816, 1024], "ln3_g": [1024], "ln3_b": [1024]}


def build(NBC, NBO, stage=99, NPG=0, NPHYS=0):
    NTC, NTO = 4 * NBC, 4 * NBO
    NTL = NTC + NTO
    NPOS = NTL * 128
    NCB = NPOS // 16 - 1
    NCT = (NCB + 127) // 128
    nc = bass.Bass("TRN2", target_bir_lowering=False)

    def din(name, shape, dt=F32):
        return nc.dram_tensor(name, list(shape), dt, kind="ExternalInput").ap()

    def dout(name, shape, dt=F32):
        return nc.dram_tensor(name, list(shape), dt, kind="ExternalOutput").ap()

    def dint(name, shape, dt=BF16):
        return nc.dram_tensor(name, list(shape), dt, kind="Internal").ap()

    x_ctx = din("x_ctx", [max(NTC, 1) * 128, D])
    x_own = din("x_own", [NTO * 128, D])
    wd = {n: din(n, W_SHAPES[n]) for n in W_NAMES}
    NPT = max(NPOS, 8192) if NPG else NPOS
    cshapes = {"c_identf": ([128, 128], F32), "c_kaug": ([5, NPOS], BF16), "c_kcaug": ([5, NCT * 128], BF16),
               "c_qaug": ([8, 5, NPOS], BF16), "c_T": ([128, NPT], BF16), "c_wimp": ([128, NCT, 128], BF16),
               "c_gb": ([128, 256], F32), "c_gm": ([128, 256], F32), "c_f0": ([128, 128], F32),
               "c_selpair": ([128, 2, 64], BF16), "c_ones": ([1, 128], BF16)}
    cd = {n: din(n, s, dt) for n, (s, dt) in cshapes.items()}
    import os as _os0
    DBG = bool(_os0.environ.get("KDBG"))
    if DBG:
        dbg_oa = dout("dbg_oa", [NTO * 128, 512])
        dbg_zT = dout("dbg_zT", [512, NTO * 128])
        dbg_cf = dout("dbg_cf", [NTO * 128, 2, 12])
        dbg_ps = dout("dbg_ps", [NTO * 128, 2, 128])
    if NPG:
        x_smp = din("x_smp", [128, D])
        pools = [din("pool%d" % i, [NPHYS * 128, 256]) for i in range(2)]
        cwin = [din("cwin_k", [4, 512, 128]), din("cwin_v", [4, 512, 128])]
        ptab = din("ptab", [4, NPG], I32)
        NCS = NPG * 8
        NCTS = (NCS + 127) // 128
        NJS = 2 * NPG + 1
        sconst = {"c_piota": ([128, 1], F32), "c_skaug": ([5, (NPG + 1) * 128], BF16), "c_skcaug": ([5, NCTS * 128], BF16),
                  "c_swaug": ([5, 512], BF16), "c_sqaug": ([8, 5, 128], BF16), "c_w0": ([128, 33], BF16),
                  "c_smb": ([128, 264], F32), "c_sbb": ([128, 264], F32), "c_augt": ([5, 3, 128], BF16)}
        scd = {n: din(n, sh, dt) for n, (sh, dt) in sconst.items()}
        y_s = dout("y_s", [128, D])
        kv_s = dout("kv_s", [128, 768])
        swin = [dout("swin_k", [4, 512, 128]), dout("swin_v", [4, 512, 128])]
        chunkv = dout("chunkv", [128, 512])
    y_own = dout("y_own", [NTO * 128, D])
    kv_own = dout("kv_own", [NTO * 128, 768])

    s_f_in = [dint("s_f1in", [NJ, 128, 2, 8, 128]), dint("s_f2in", [NJ, 128, 2, 8, 128])]
    s_f_out = [dint("s_f1out", [DFF, D]), dint("s_f2out", [DFF, D])]
    WIN_UNITS = [(0, 256), (256, 256), (512, 256), (768, 256), (1024, 256), (1280, 24), (1304, 256), (1560, 256),
                 (1816, 256), (2072, 256)] + [(2328 + 256 * i, 256) for i in range(4)] + [(3352 + 256 * i, 256) for i in range(4)]
    U_Q, U_KV, U_G, U_U, U_V, U_GA, U_GB = 0, 2, 5, 6, 8, 10, 14
    s_winu = dint("s_winu", [len(WIN_UNITS), 128, 8, 256])
    s_wa = dint("s_wa", [8, 128, 4, 128])
    s_wb = dint("s_wb", [8, 128, 4, 128])
    s_wo = dint("s_wo", [D, D])
    s_w1 = [dint("s_w1k", [2048, 128]), dint("s_w1v", [2048, 128])]
    s_w2 = [dint("s_w2k", [128, 64]), dint("s_w2v", [128, 64])]
    s_pe = [dint("s_pek", [2048, 1]), dint("s_pev", [2048, 1])]
    s_sw = dint("s_sw", [4, 128, 128])
    s_sb = dint("s_sb", [4, 128])

    out_events = []
    with contextlib.ExitStack() as st:
        S = Sched(nc, st)
        sb, ps = S.sb, S.ps

        identf = sb("identf", [128, 128], F32)
        identb = sb("identb", [128, 128], BF16)
        lnp = sb("lnp", [128, 6, D], F32)
        glnp = sb("glnp", [128, 2, 512], F32)
        wsT = sb("wsT", [128, 4, 128], BF16)
        bsrow = sb("bsrow", [1, 4, 128], BF16)
        onesrow = sb("onesrow", [1, 128], BF16)
        Tsel = sb("Tsel", [128, NPT], FP8)
        wimp = sb("wimp", [128, NCT, 128], BF16)
        gbt = sb("gbt", [128, 256], F32)
        gmt = sb("gmt", [128, 256], F32)
        f0t = sb("f0t", [128, 128], F32)
        selpair = sb("selpair", [128, 2, 64], BF16)
        w2c = [sb("w2k", [128, 64], BF16), sb("w2v", [128, 64], BF16)]
        pec = [sb("pek", [128, 16], BF16), sb("pev", [128, 16], BF16)]
        pebias = sb("pebias", [128, 2], F32)
        Ksel = sb("Ksel", [69, 2, NPOS], BF16)
        Vsel = sb("Vsel", [128, NTL, 2, 66], BF16)
        Kwin = sb("Kwin", [69, 2, 1024], BF16)
        Vwin = sb("Vwin", [128, 8, 2, 66], BF16)
        Kc = sb("Kc", [69, 2, NCT * 128], BF16)
        Vc = sb("Vc", [128, NCT, 2, 66], BF16)
        XT2 = sb("XT2", [128, 4, 264], BF16)
        xres = [sb("xres%d" % i, [128, D], F32) for i in range(4)]
        xT = sb("xT", [128, 8, 512], BF16)
        chunks = [sb("chunk%d" % i, [128, 512], BF16) for i in range(NJ)]
        qT = sb("qT", [69, 8, 512], BF16)
        NR = 4
        wring = [sb("wring%d" % i, [128, 2048], BF16) for i in range(NR)]
        silu_t = [sb("silu%d" % i, [128, 512], BF16) for i in range(2)]
        pTb = [sb("pT%d" % i, [128, 512], BF16) for i in range(4)]
        f32t = [sb("f32t%d" % i, [128, 792], F32) for i in range(2)]
        stats = sb("stats", [128, 2, 6], F32)
        mv = sb("mv", [128, 2], F32)
        rstd = sb("rstd", [128, 1], F32)
        banks = [ps("bank%d" % i, [128, 512], F32) for i in range(8)]
        gts = sb("gts", [128, 4, 24], F32)
        vnb = sb("vnb", [128, 2, 512], BF16)
        oab = sb("oab", [128, 512], BF16)
        psc = sb("psc", [128, 128], F32)
        psc2 = sb("psc2", [128, 128], F32)
        m8a = sb("m8a", [128, 8], F32)
        m8b = sb("m8b", [128, 8], F32)
        nsb = sb("nsb", [128, 128], BF16)
        nsT = sb("nsT", [128, 128], BF16)
        rden = sb("rden", [128, 3, 4], F32)
        coef = sb("coef", [128, 3, 4], F32)
        oa32 = [sb("oa32_%d" % i, [128, 4, 64], F32) for i in range(2)]

        rr = {"w": 0, "silu": 0, "pT": 0, "f32": 0, "S": 0, "oa": 0}

        def nxt(key, lst):
            i = rr[key] % len(lst)
            rr[key] += 1
            return lst[i]

        def wload(src_ap, reads, shape):
            slot = nxt("w", wring)
            n = 1
            for s_ in shape[1:]:
                n *= s_
            view = slot.ap[:, 0:n]
            if len(shape) == 3:
                view = view.rearrange("p (a b) -> p a b", a=shape[1])
            elif len(shape) == 4:
                view = view.rearrange("p (a b c) -> p a b c", a=shape[1], b=shape[2])
            S.dma("sp", view, src_ap, reads=reads, writes=[slot])
            return slot, view

        def mm(ob, oap, lb, lap, rb, rap, start, stop, extra=(), skip=False):
            S.op("pe", lambda e: e.matmul(oap, lhsT=lap, rhs=rap, start=start, stop=stop, skip_group_check=skip),
                 reads=[lb, rb, *extra], writes=[ob])

        def tr(ob, oap, ib, iap, idb):
            S.op("pe", lambda e: e.transpose(out=oap, in_=iap, identity=idb.ap[:iap.shape[0], :iap.shape[0]]),
                 reads=[ib, idb], writes=[ob])

        def act(ob, oap, ib, iap, func, extra_r=(), **kw):
            S.op("act", lambda e: e.activation(out=oap, in_=iap, func=func, **kw), reads=[ib, *extra_r], writes=[ob])

        def tt(eng, ob, oap, ab, aap, bb, bap, op):
            S.op(eng, lambda e: e.tensor_tensor(out=oap, in0=aap, in1=bap, op=op), reads=[ab, bb], writes=[ob])

        def ts(eng, ob, oap, ib, iap, s1, s2, op0, op1=None, extra_r=()):
            if op1 is None:
                S.op(eng, lambda e: e.tensor_scalar(out=oap, in0=iap, scalar1=s1, scalar2=None, op0=op0),
                     reads=[ib, *extra_r], writes=[ob])
            else:
                S.op(eng, lambda e: e.tensor_scalar(out=oap, in0=iap, scalar1=s1, scalar2=s2, op0=op0, op1=op1),
                     reads=[ib, *extra_r], writes=[ob])

        def stt(ob, oap, ab, aap, scalar, bb, bap, op0, op1, extra_r=()):
            S.op("dve", lambda e: e.scalar_tensor_tensor(out=oap, in0=aap, scalar=scalar, in1=bap, op0=op0, op1=op1),
                 reads=[ab, bb, *extra_r], writes=[ob])

        def cp(eng, ob, oap, ib, iap):
            if eng == "act":
                S.op("act", lambda e: e.copy(out=oap, in_=iap), reads=[ib], writes=[ob])
            else:
                S.op(eng, lambda e: e.tensor_copy(out=oap, in_=iap), reads=[ib], writes=[ob])

        S.dma("sp", identf.ap[:], cd["c_identf"], writes=[identf])
        cp("dve", identb, identb.ap[:], identf, identf.ap[:])
        for i, n in enumerate(["ln1_g", "ln1_b", "ln2_g", "ln2_b", "ln3_g", "ln3_b"]):
            S.dma("sp", lnp.ap[:, i, :], wd[n].partition_broadcast(128), writes=[lnp])
        for i, n in enumerate(["gmlp_ln_g", "gmlp_ln_b"]):
            S.dma("sp", glnp.ap[:, i, :], wd[n].partition_broadcast(128), writes=[glnp])
        for i0 in range(0, NPT, 512):
            tb_ = nxt("pT", pTb)
            S.dma("sp", tb_.ap[:], cd["c_T"][:, i0:i0 + 512], writes=[tb_])
            S.op("dve", lambda e, tb_=tb_, i0=i0: e.tensor_copy(out=Tsel.ap[:, i0:i0 + 512], in_=tb_.ap[:], saturate=False),
                 reads=[tb_], writes=[Tsel])
        S.dma("sp", wimp.ap[:], cd["c_wimp"], writes=[wimp])
        S.dma("sp", gbt.ap[:], cd["c_gb"], writes=[gbt])
        S.dma("sp", gmt.ap[:], cd["c_gm"], writes=[gmt])
        S.dma("sp", f0t.ap[:], cd["c_f0"], writes=[f0t])
        S.dma("sp", selpair.ap[:], cd["c_selpair"], writes=[selpair])
        S.dma("sp", onesrow.ap[:], cd["c_ones"], writes=[onesrow])
        S.op("dve", lambda e: e.memset(Kc.ap[:], 0.0), writes=[Kc])
        for g in range(2):
            S.dma("sp", Ksel.ap[64:69, g, :], cd["c_kaug"], writes=[Ksel])
            S.dma("sp", Kc.ap[64:69, g, :], cd["c_kcaug"], writes=[Kc])
        S.op("dve", lambda e: e.memset(Vsel.ap[:], 1.0), writes=[Vsel])
        S.op("dve", lambda e: e.memset(Vwin.ap[:], 1.0), writes=[Vwin])
        S.op("dve", lambda e: e.memset(Vc.ap[:], 1.0), writes=[Vc])
        S.op("dve", lambda e: e.memset(XT2.ap[:], 0.0), writes=[XT2])

        sc = {}

        def cast(key, out_ap, in_ap):
            b = sc.setdefault(key, Buf(key))
            S.dma("pool", out_ap, in_ap, writes=[b])

        def cast_ffn(f):
            wi = wd["ffn%d_w_in" % (f + 1)]
            wo = wd["ffn%d_w_out" % (f + 1)]
            for j in range(NJ):
                for gu in range(2):
                    c0 = gu * DFF + j * 128
                    cast(("fin", f, j), s_f_in[f][j, :, gu, :, :], wi[:, c0:c0 + 128].rearrange("(kc p) c -> p kc c", p=128))
            for j in range(NJ):
                cast(("fout", f, j), s_f_out[f][j * 128:(j + 1) * 128, :], wo[j * 128:(j + 1) * 128, :])

        cast_ffn(0)
        for u, (c0, ncol) in enumerate(WIN_UNITS):
            cast(("win", u), s_winu[u, :, :, 0:ncol], wd["w_in"][:, c0:c0 + ncol].rearrange("(kc p) c -> p kc c", p=128))
        for kv, (n1, n2, npe) in enumerate((("cmp_w1_k", "cmp_w2_k", "cmp_pe_k"), ("cmp_w1_v", "cmp_w2_v", "cmp_pe_v"))):
            cast("cmpw", s_w1[kv][:, :], wd[n1][:, :])
            cast("cmpw", s_w2[kv][:, :], wd[n2][:, :])
            cast("cmpw", s_pe[kv].rearrange("(a b) o -> a (b o)", b=64), wd[npe][:, :])
        cast("sw", s_sw.rearrange("h a b -> (h a) b"), wd["spatial_w"].rearrange("h a b -> (h a) b"))
        cast("sw", s_sb[:, :], wd["spatial_b"][:, :])
        for c in range(8):
            cast(("wa", c), s_wa[c], wd["w_branch_a"][:, c * 128:(c + 1) * 128].rearrange("(kc p) n -> p kc n", p=128))
            cast(("wb", c), s_wb[c], wd["w_branch_b"][:, c * 128:(c + 1) * 128].rearrange("(kc p) n -> p kc n", p=128))
        for r0 in range(0, 1024, 128):
            cast(("wo", r0 // 128), s_wo[r0:r0 + 128, :], wd["w_out"][r0:r0 + 128, :])
        cast_ffn(1)

        for kv in range(2):
            S.dma("sp", w2c[kv].ap[:], s_w2[kv][:, :], reads=[sc["cmpw"]], writes=[w2c[kv]])
            pl = nxt("pT", pTb)
            S.dma("sp", pl.ap[0:16, 0:128], s_pe[kv].rearrange("(c p) o -> c (p o)", p=128), reads=[sc["cmpw"]], writes=[pl])
            tr(banks[7], banks[7].ap.bitcast(BF16)[:, kv * 16:(kv + 1) * 16], pl, pl.ap[0:16, 0:128], identb)
            cp("dve", pec[kv], pec[kv].ap[:], banks[7], banks[7].ap.bitcast(BF16)[:, kv * 16:(kv + 1) * 16])
        S.dma("sp", bsrow.ap[:], s_sb.rearrange("(o h) t -> o h t", o=1), reads=[sc["sw"]], writes=[bsrow])
        swl = nxt("pT", pTb)
        S.dma("sp", swl.ap[:].rearrange("p (h t) -> p h t", h=4), s_sw.rearrange("h a b -> a h b"), reads=[sc["sw"]], writes=[swl])
        trb = banks[7]
        trb_bf = trb.ap.bitcast(BF16)
        for h in range(4):
            tr(trb, trb_bf[:, h * 128:(h + 1) * 128], swl, swl.ap[:, h * 128:(h + 1) * 128], identb)
        cp("dve", wsT, wsT.ap[:].rearrange("p h t -> p (h t)"), trb, trb_bf[:, 0:512])
        S.op("pool", lambda e: e.affine_select(out=wsT.ap[:], in_=wsT.ap[:], pattern=[[0, 4], [1, 128]],
                                               compare_op=ALU.is_ge, fill=0.0, base=0, channel_multiplier=-1),
             reads=[wsT], writes=[wsT])
        pb = banks[5]
        for kv in range(2):
            w1s, w1v_ = wload(s_w1[kv].rearrange("(c p) h -> p c h", p=128), [sc["cmpw"]], [128, 16, 128])
            for c in range(16):
                mm(pb, pb.ap[:, kv:kv + 1], w1s, w1v_[:, c, :], pec[kv], pec[kv].ap[:, c:c + 1], c == 0, c == 15)
        cp("dve", pebias, pebias.ap[:], pb, pb.ap[:, 0:2])

        def transpose_tile(src, t):
            for half in range(2):
                bk = banks[(2 * t + half) % 4]
                for k in range(4):
                    kc = half * 4 + k
                    tr(bk, bk.ap[:, k * 128:(k + 1) * 128], src, src.ap[:, kc * 128:(kc + 1) * 128], identf)
                eng = "act" if half == 0 else "dve"
                cp(eng, xT, xT.ap[:, half * 4:half * 4 + 4, t * 128:(t + 1) * 128],
                   bk, bk.ap[:].rearrange("p (k n) -> p k n", k=4))

        def layer_norm_tile(xb, gi, eps):
            for h in range(2):
                S.op("dve", lambda e, h=h: e.bn_stats(out=stats.ap[:, h, :], in_=xb.ap[:, h * 512:(h + 1) * 512]),
                     reads=[xb], writes=[stats])
            S.op("dve", lambda e: e.bn_aggr(out=mv.ap[:], in_=stats.ap[:].rearrange("p a b -> p (a b)")),
                 reads=[stats], writes=[mv])
            act(rstd, rstd.ap[:], mv, mv.ap[:, 1:2], AF.Sqrt, bias=eps, scale=1.0)
            S.op("dve", lambda e: e.reciprocal(out=rstd.ap[:], in_=rstd.ap[:]), reads=[rstd], writes=[rstd])
            ts("dve", xb, xb.ap[:], xb, xb.ap[:], mv.ap[:, 0:1], rstd.ap[:, 0:1], ALU.subtract, ALU.mult, extra_r=[mv, rstd])
            tt("dve", xb, xb.ap[:], xb, xb.ap[:], lnp, lnp.ap[:, gi, :], ALU.mult)
            tt("dve", xb, xb.ap[:], xb, xb.ap[:], lnp, lnp.ap[:, gi + 1, :], ALU.add)

        def ffn_block(f, nt, gi):
            ntok = nt * 128
            for j in range(NJ):
                slot, w = wload(s_f_in[f][j], [sc[("fin", f, j)]], [128, 2, 8, 128])
                gb_, ub_ = banks[j % 2], banks[2 + j % 2]
                for gu, bk in ((0, gb_), (1, ub_)):
                    for kc in range(8):
                        mm(bk, bk.ap[:, 0:ntok], slot, w[:, gu, kc, :], xT, xT.ap[:, kc, 0:ntok], kc == 0, kc == 7)
                sl = nxt("silu", silu_t)
                act(sl, sl.ap[:, 0:ntok], gb_, gb_.ap[:, 0:ntok], AF.Silu)
                tt("dve", chunks[j], chunks[j].ap[:, 0:ntok], sl, sl.ap[:, 0:ntok], ub_, ub_.ap[:, 0:ntok], ALU.mult)
            for n in range(2):
                for j in range(NJ):
                    slot, w = wload(s_f_out[f][j * 128:(j + 1) * 128, n * 512:(n + 1) * 512], [sc[("fout", f, j)]], [128, 512])
                    for t in range(nt):
                        bk = banks[4 + t]
                        mm(bk, bk.ap[:], chunks[j], chunks[j].ap[:, t * 128:(t + 1) * 128], slot, w, j == 0, j == NJ - 1)
                for t in range(nt):
                    bk = banks[4 + t]
                    stt(xres[t], xres[t].ap[:, n * 512:(n + 1) * 512], xres[t], xres[t].ap[:, n * 512:(n + 1) * 512],
                        2.0 * ALPHA, bk, bk.ap[:], ALU.mult, ALU.add)
            for t in range(nt):
                layer_norm_tile(xres[t], gi, 4.0 * LN_EPS)

        def kv_project(nt, lt0, is_own, own_row0, need_win):
            wl = []
            for u in ([U_KV, U_KV + 1] + ([U_KV + 2] if need_win else [])):
                wl.append(wload(s_winu[u], [sc[("win", u)]], [128, 8, 256]))
            for t in range(nt):
                lt = lt0 + t
                bkA, bkB = banks[4 + (t % 2)], banks[6]
                for ui, (slot, w) in enumerate(wl):
                    bk, o0 = (bkA, ui * 256) if ui < 2 else (bkB, 0)
                    for kc in range(8):
                        mm(bk, bk.ap[:, o0:o0 + 256], xT, xT.ap[:, kc, t * 128:(t + 1) * 128], slot, w[:, kc, :], kc == 0, kc == 7)
                if is_own and (KSUB & 8):
                    stg = nxt("f32", f32t)
                    cp("act", stg, stg.ap[:, 0:512], bkA, bkA.ap[:])
                    if need_win:
                        cp("act", stg, stg.ap[:, 512:768], bkB, bkB.ap[:, 0:256])
                    r0 = own_row0 + t * 128
                    ncol = 768 if need_win else 512
                    out_events.append(S.dma("act", kv_own[r0:r0 + 128, 0:ncol], stg.ap[:, 0:ncol], reads=[stg]))
                kvb = nxt("pT", pTb)
                cp("dve", kvb, kvb.ap[:], bkA, bkA.ap[:])
                if KSUB & 4:
                    cp("act", Vsel, Vsel.ap[:, lt, :, 0:64], bkA, bkA.ap[:, 384:512].rearrange("p (g d) -> p g d", g=2))
                if KSUB & 2:
                    for g in range(2):
                        tr(trb, trb_bf[0:64, g * 128:(g + 1) * 128], kvb, kvb.ap[:, 256 + g * 64:320 + g * 64], identb)
                if need_win:
                    kwb = nxt("pT", pTb)
                    cp("dve", kwb, kwb.ap[:, 0:256], bkB, bkB.ap[:, 0:256])
                    if KSUB & 2:
                        for g in range(2):
                            tr(trb, trb_bf[0:64, 256 + g * 128:384 + g * 128], kwb, kwb.ap[:, g * 64:(g + 1) * 64], identb)
                    ws = lt % 8
                    if KSUB & 4:
                        cp("act", Vwin, Vwin.ap[:, ws, :, 0:64], bkB, bkB.ap[:, 128:256].rearrange("p (g d) -> p g d", g=2))
                if KSUB & 2:
                    cp("act", Ksel, Ksel.ap[0:64, :, lt * 128:(lt + 1) * 128], trb, trb_bf[0:64, 0:256].rearrange("p (g n) -> p g n", g=2))
                if need_win:
                    ws = lt % 8
                    if KSUB & 2:
                        cp("dve", Kwin, Kwin.ap[0:64, :, ws * 128:(ws + 1) * 128], trb, trb_bf[0:64, 256:512].rearrange("p (g n) -> p g n", g=2))
                    if KSUB & 16:
                        for g in range(2):
                            S.dma("sp", Kwin.ap[64:69, g, ws * 128:(ws + 1) * 128], cd["c_kaug"][:, lt * 128:(lt + 1) * 128], writes=[Kwin])
                xb_ = banks[(2 * t) % 4]
                if not (KSUB & 1):
                    continue
                for kvg in range(4):
                    for par in range(2):
                        mm(xb_, xb_.ap[par * 64:(par + 1) * 64, kvg * 64:(kvg + 1) * 64], kvb, kvb.ap[:, kvg * 64:(kvg + 1) * 64],
                           selpair, selpair.ap[:, par, :], True, True)
                cp("dve", XT2, XT2.ap[:, :, 8 + t * 64:8 + (t + 1) * 64], xb_, xb_.ap[:, 0:256].rearrange("p (a m) -> p a m", a=4))

        def compress_block(nt, lt0, Kc=Kc, Vc=Vc, w1res=None):
            nb = nt * 8
            i_lo = lt0 * 8 - 1
            skip = 1 if i_lo < 0 else 0
            n = nb - skip
            i0 = i_lo + skip
            for kv in range(2):
                hb = banks[kv]
                if w1res is None:
                    w1s, w1v_ = wload(s_w1[kv].rearrange("(c p) h -> p c h", p=128), [sc["cmpw"]], [128, 16, 128])
                else:
                    w1s, w1v_ = w1res[kv], w1res[kv].ap
                for g in range(2):
                    a = kv * 2 + g
                    for c in range(16):
                        rhs = XT2.ap[:, a, c + 8 * skip: c + 8 * skip + 8 * (n - 1) + 1: 8]
                        mm(hb, hb.ap[:, g * 64:g * 64 + n], w1s, w1v_[:, c, :], XT2, rhs, c == 0, c == 15)
                gl = nxt("silu", silu_t)
                for g in range(2):
                    act(gl, gl.ap[:, g * 64:g * 64 + n], hb, hb.ap[:, g * 64:g * 64 + n], AF.Gelu_apprx_tanh,
                        extra_r=[pebias], bias=pebias.ap[:, kv:kv + 1], scale=1.0)
                ob = banks[2 + kv]
                if kv == 0:
                    for g in range(2):
                        mm(ob, ob.ap[0:64, g * 64:g * 64 + n], w2c[0], w2c[0].ap[:], gl, gl.ap[:, g * 64:g * 64 + n], True, True)
                    for g in range(2):
                        cp("act", Kc, Kc.ap[0:64, g, i0:i0 + n], ob, ob.ap[0:64, g * 64:g * 64 + n])
                else:
                    for g in range(2):
                        mm(ob, ob.ap[0:n, g * 64:(g + 1) * 64], gl, gl.ap[:, g * 64:g * 64 + n], w2c[1], w2c[1].ap[:], True, True)
                    stg = nxt("pT", pTb)
                    cp("act", stg, stg.ap[0:n, 0:128], ob, ob.ap[0:n, 0:128])
                    i = i0
                    while i < i0 + n:
                        ct, p0 = divmod(i, 128)
                        m = min(i0 + n - i, 128 - p0)
                        S.dma("sp", Vc.ap[p0:p0 + m, ct, :, 0:64], stg.ap[i - i0:i - i0 + m, 0:128].rearrange("p (g d) -> p g d", g=2),
                              reads=[stg], writes=[Vc])
                        i += m
            cp("dve", XT2, XT2.ap[:, :, 0:8], XT2, XT2.ap[:, :, nt * 64:nt * 64 + 8])

        uTc, oaT, zT, mT = chunks[0:4], chunks[4:8], chunks[8:12], chunks[12:20]

        def ln_small(src, sap, n, gap, bap, dst, dap, eps):
            S.op("dve", lambda e: e.bn_stats(out=stats.ap[:, 0, :], in_=sap), reads=[src], writes=[stats])
            S.op("dve", lambda e: e.bn_aggr(out=mv.ap[:], in_=stats.ap[:, 0, :]), reads=[stats], writes=[mv])
            act(rstd, rstd.ap[:], mv, mv.ap[:, 1:2], AF.Sqrt, bias=eps, scale=1.0)
            S.op("dve", lambda e: e.reciprocal(out=rstd.ap[:], in_=rstd.ap[:]), reads=[rstd], writes=[rstd])
            ts("dve", src, sap, src, sap, mv.ap[:, 0:1], rstd.ap[:, 0:1], ALU.subtract, ALU.mult, extra_r=[mv, rstd])
            tt("dve", src, sap, src, sap, glnp, gap, ALU.mult)
            tt("dve", dst, dap, src, sap, glnp, bap, ALU.add)

        def sbank():
            return nxt("S", banks[0:2])

        def pool_mask(P, base, cm, qstep, op):
            S.op("pool", lambda e: e.affine_select(out=P.ap[:].rearrange("p (r q) -> p r q", r=4),
                                                   in_=P.ap[:].rearrange("p (r q) -> p r q", r=4),
                                                   pattern=[[0, 4], [qstep, 128]], compare_op=op, fill=0.0,
                                                   base=base, channel_multiplier=cm), reads=[P], writes=[P])

        def pv(Ob, first, P, Vb, vap):
            for r in range(4):
                mm(Ob, Ob.ap[:, r * 66:r * 66 + 65], P, P.ap[:, r * 128:(r + 1) * 128], Vb, vap,
                   first and r == 0, False, skip=True)

        def dens(Ob, br):
            dv = Ob.ap[:, 0:264].rearrange("p (r e) -> p r e", r=4)[:, :, 64]
            ts("dve", rden, rden.ap[:, br, :], Ob, dv, 1e-30, None, ALU.max)
            S.op("dve", lambda e: e.reciprocal(out=rden.ap[:, br, :], in_=rden.ap[:, br, :]), reads=[rden], writes=[rden])

        CH = []
        for ci in range(2):
            CH.append(dict(
                Ob=banks[2 + 2 * ci], psl=banks[3 + 2 * ci],
                psc=sb("psc_%d" % ci, [128, 128], F32) if ci else psc,
                psc2=sb("psc2_%d" % ci, [128, 128], F32) if ci else psc2,
                m8a=sb("m8a_%d" % ci, [128, 8], F32) if ci else m8a,
                m8b=sb("m8b_%d" % ci, [128, 8], F32) if ci else m8b,
                nsb=sb("nsb_%d" % ci, [128, 128], BF16) if ci else nsb,
                nsT=sb("nsT_%d" % ci, [128, 128], BF16) if ci else nsT,
                rden=sb("rden_%d" % ci, [128, 3, 4], F32) if ci else rden,
                coef=sb("coef_%d" % ci, [128, 3, 4], F32) if ci else coef,
                acc=oa32[0] if ci == 0 else sb("acc_1", [128, 4, 64], F32),
                tmp=oa32[1] if ci == 0 else sb("tmp_1", [128, 4, 64], F32),
                trc=ci * 128))
        SB3 = [banks[0], banks[1], banks[6]]

        def sbank3():
            return nxt("S", SB3)

        def attn_chain(t, lt, g, R):
            q0 = lt * 128
            imax = min((q0 + 96) // 16, NCB - 1)
            nct = imax // 128 + 1
            Ob, psl = R["Ob"], R["psl"]
            psc_, psc2_, m8a_, m8b_, nsb_, nsT_, rden_, coef_, acc, tmp = (R[k] for k in
                ("psc", "psc2", "m8a", "m8b", "nsb", "nsT", "rden", "coef", "acc", "tmp"))
            qrhs = qT.ap[:, 4 * g:4 * g + 4, t * 128:(t + 1) * 128]
            gv = gts.ap[:, t, g * 12:(g + 1) * 12].rearrange("p (r b) -> p r b", b=3)
            ov = Ob.ap[:, 0:264].rearrange("p (r e) -> p r e", r=4)[:, :, 0:64]

            def pv_(first, P, Vb, vap):
                for r in range(4):
                    mm(Ob, Ob.ap[:, r * 66:r * 66 + 65], P, P.ap[:, r * 128:(r + 1) * 128], Vb, vap, first and r == 0, False, skip=True)

            def fin(br, last):
                dv = Ob.ap[:, 0:264].rearrange("p (r e) -> p r e", r=4)[:, :, 64]
                ts("dve", rden_, rden_.ap[:, br, :], Ob, dv, 1e-30, None, ALU.max)
                S.op("dve", lambda e: e.reciprocal(out=rden_.ap[:, br, :], in_=rden_.ap[:, br, :]), reads=[rden_], writes=[rden_])
                tt("dve", coef_, coef_.ap[:, br, :], rden_, rden_.ap[:, br, :], gts, gv[:, :, br], ALU.mult)
                cb = coef_.ap[:, br, :].unsqueeze(2).broadcast_to([128, 4, 64])
                if br == 0:
                    tt("dve", acc, acc.ap[:], Ob, ov, coef_, cb, ALU.mult)
                else:
                    tt("dve", tmp, tmp.ap[:], Ob, ov, coef_, cb, ALU.mult)
                    if not last:
                        tt("dve", acc, acc.ap[:], acc, acc.ap[:], tmp, tmp.ap[:], ALU.add)
                    else:
                        tt("dve", oab, oab.ap[:, g * 256:(g + 1) * 256].rearrange("p (r d) -> p r d", r=4),
                           acc, acc.ap[:], tmp, tmp.ap[:], ALU.add)

            for ct in range(nct):
                Sb = sbank3()
                mm(Sb, Sb.ap[:].rearrange("p (r q) -> p r q", r=4), Kc, Kc.ap[:, g, ct * 128:(ct + 1) * 128], qT, qrhs, True, True)
                yield
                P = nxt("pT", pTb)
                bnd = q0 - 2048 * ct - 2032 - 31 < 0
                if bnd:
                    ts("dve", Sb, Sb.ap[:], Sb, Sb.ap[:], 30.0, None, ALU.min)
                    yield
                act(P, P.ap[:], Sb, Sb.ap[:], AF.Exp)
                yield
                if bnd:
                    pool_mask(P, q0 - 2048 * ct - 31, -16, 1, ALU.is_ge)
                    yield
                pv_(ct == 0, P, Vc, Vc.ap[:, ct, g, 0:65])
                for r in range(4):
                    mm(psl, psl.ap[:, r * 128:(r + 1) * 128], P, P.ap[:, r * 128:(r + 1) * 128], wimp, wimp.ap[:, ct, :],
                       ct == 0 and r == 0, False, skip=True)
                yield
            fin(0, False)
            yield
            ts("dve", psc_, psc_.ap[:], psl, psl.ap[:, 0:128], rden_.ap[:, 0, 0:1], None, ALU.mult, extra_r=[rden_])
            for r in range(1, 4):
                stt(psc_, psc_.ap[:], psl, psl.ap[:, r * 128:(r + 1) * 128], rden_.ap[:, 0, r:r + 1], psc_, psc_.ap[:],
                    ALU.mult, ALU.add, extra_r=[rden_])
            yield
            o = 128 - 2 * lt
            tt("dve", psc_, psc_.ap[:], psc_, psc_.ap[:], gmt, gmt.ap[:, o:o + 128], ALU.mult)
            tt("dve", psc_, psc_.ap[:], psc_, psc_.ap[:], gbt, gbt.ap[:, o:o + 128], ALU.add)
            tt("dve", psc_, psc_.ap[:], psc_, psc_.ap[:], f0t, f0t.ap[:], ALU.add)
            yield
            S.op("dve", lambda e: e.max(out=m8a_.ap[:], in_=psc_.ap[:]), reads=[psc_], writes=[m8a_])
            S.op("dve", lambda e: e.match_replace(out=psc2_.ap[:], in_to_replace=m8a_.ap[:], in_values=psc_.ap[:], imm_value=-3.0e38),
                 reads=[psc_, m8a_], writes=[psc2_])
            yield
            S.op("dve", lambda e: e.max(out=m8b_.ap[:], in_=psc2_.ap[:]), reads=[psc2_], writes=[m8b_])
            ts("dve", nsb_, nsb_.ap[:], psc_, psc_.ap[:], m8b_.ap[:, 7:8], NEGBIG, ALU.is_lt, ALU.mult, extra_r=[m8b_])
            yield
            c0 = R["trc"]
            tr(trb, trb_bf[:, c0:c0 + 128], nsb_, nsb_.ap[:], identb)
            yield
            cp("act", nsT_, nsT_.ap[:], trb, trb_bf[:, c0:c0 + 128])
            yield
            kts = [kt for kt in range(lt - 4, lt + 1) if kt >= 0]
            for idx_, kt in enumerate(kts):
                ws = kt % 8
                Sb = sbank3()
                mm(Sb, Sb.ap[:].rearrange("p (r q) -> p r q", r=4), Kwin, Kwin.ap[:, g, ws * 128:(ws + 1) * 128], qT, qrhs, True, True)
                yield
                P = nxt("pT", pTb)
                act(P, P.ap[:], Sb, Sb.ap[:], AF.Exp)
                yield
                if kt == lt:
                    pool_mask(P, 0, -1, 1, ALU.is_ge)
                    yield
                if kt == lt - 4:
                    pool_mask(P, 0, 1, -1, ALU.is_gt)
                    yield
                pv_(idx_ == 0, P, Vwin, Vwin.ap[:, ws, g, 0:65])
                yield
            fin(2, False)
            yield
            for kt in range(0, lt + 1):
                Sb = sbank3()
                mm(Sb, Sb.ap[:].rearrange("p (r q) -> p r q", r=4), Ksel, Ksel.ap[:, g, kt * 128:(kt + 1) * 128], qT, qrhs, True, False)
                for r in range(4):
                    mm(Sb, Sb.ap[:, r * 128:(r + 1) * 128], Tsel, Tsel.ap[:, kt * 128:(kt + 1) * 128], nsT_, nsT_.ap[:], False, r == 3)
                yield
                P = nxt("pT", pTb)
                act(P, P.ap[:], Sb, Sb.ap[:], AF.Exp)
                yield
                if kt == lt:
                    pool_mask(P, 0, -1, 1, ALU.is_ge)
                    yield
                pv_(kt == 0, P, Vsel, Vsel.ap[:, kt, g, 0:65])
                yield
            fin(1, True)
            yield

        def attention_tile(t, lt):
            gens = [attn_chain(t, lt, 0, CH[0]), attn_chain(t, lt, 1, CH[1])]
            live = list(gens)
            while live:
                for gch in list(live):
                    try:
                        next(gch)
                    except StopIteration:
                        live.remove(gch)
            if DBG:
                d32 = nxt("f32", f32t)
                cp("dve", d32, d32.ap[:, 0:512], oab, oab.ap[:])
                r0 = (lt - NTC) * 128
                out_events.append(S.dma("sp", dbg_oa[r0:r0 + 128, :], d32.ap[:, 0:512], reads=[d32]))
            for kc in range(4):
                tr(trb, trb_bf[:, 256 + kc * 128:384 + kc * 128], oab, oab.ap[:, kc * 128:(kc + 1) * 128], identb)
            for kc in range(4):
                cp("act" if kc % 2 else "dve", oaT[kc], oaT[kc].ap[:, t * 128:(t + 1) * 128], trb, trb_bf[:, 256 + kc * 128:384 + kc * 128])

        def mixer_block(nt, lt0):
            ntok = nt * 128
            for hq in range(2):
                slot, w = wload(s_winu[U_Q + hq], [sc[("win", U_Q + hq)]], [128, 8, 256])
                for hh in range(4):
                    h = 4 * hq + hh
                    bk = banks[hh % 4]
                    for kc in range(8):
                        mm(bk, bk.ap[0:64, 0:ntok], slot, w[:, kc, hh * 64:(hh + 1) * 64], xT, xT.ap[:, kc, 0:ntok], kc == 0, kc == 7)
                    S.op("act", lambda e, bk=bk, h=h: e.mul(qT.ap[0:64, h, 0:ntok], bk.ap[0:64, 0:ntok], 0.125), reads=[bk], writes=[qT])
            for h in range(8):
                S.dma("sp", qT.ap[64:69, h, 0:ntok], cd["c_qaug"][h, :, lt0 * 128:lt0 * 128 + ntok], writes=[qT])
            slot, w = wload(s_winu[U_G, :, :, 0:24], [sc[("win", U_G)]], [128, 8, 24])
            for t in range(nt):
                bk = banks[4 + t % 2]
                for kc in range(8):
                    mm(bk, bk.ap[:, 0:24], xT, xT.ap[:, kc, t * 128:(t + 1) * 128], slot, w[:, kc, :], kc == 0, kc == 7)
                act(gts, gts.ap[:, t, :], bk, bk.ap[:, 0:24], AF.Sigmoid)
            for uu in range(2):
                slot, w = wload(s_winu[U_U + uu], [sc[("win", U_U + uu)]], [128, 8, 256])
                for cc in range(2):
                    c = uu * 2 + cc
                    bk = banks[c % 4]
                    for kc in range(8):
                        mm(bk, bk.ap[:, 0:ntok], slot, w[:, kc, cc * 128:(cc + 1) * 128], xT, xT.ap[:, kc, 0:ntok], kc == 0, kc == 7)
                    act(uTc[c], uTc[c].ap[:, 0:ntok], bk, bk.ap[:, 0:ntok], AF.Gelu_apprx_tanh)
            wv = [wload(s_winu[U_V + i], [sc[("win", U_V + i)]], [128, 8, 256]) for i in range(2)]
            for t in range(nt):
                bk = banks[4 + t % 2]
                for i in range(2):
                    for kc in range(8):
                        mm(bk, bk.ap[:, i * 256:(i + 1) * 256], xT, xT.ap[:, kc, t * 128:(t + 1) * 128], wv[i][0], wv[i][1][:, kc, :], kc == 0, kc == 7)
                vf = nxt("f32", f32t)
                act(vf, vf.ap[:, 0:512], bk, bk.ap[:], AF.Gelu_apprx_tanh)
                ln_small(vf, vf.ap[:, 0:512], 512, glnp.ap[:, 0, :], glnp.ap[:, 1, :], vnb, vnb.ap[:, t % 2, :], LN_EPS)
                gb_ = banks[6]
                for hg in range(4):
                    mm(gb_, gb_.ap[:, hg * 128:(hg + 1) * 128], vnb, vnb.ap[:, t % 2, hg * 128:(hg + 1) * 128], wsT, wsT.ap[:, hg, :], True, False)
                    mm(gb_, gb_.ap[:, hg * 128:(hg + 1) * 128], onesrow, onesrow.ap[0:1, :], bsrow, bsrow.ap[0:1, hg, :], False, True)
                for hg in range(4):
                    tt("dve", zT[hg], zT[hg].ap[:, t * 128:(t + 1) * 128], gb_, gb_.ap[:, hg * 128:(hg + 1) * 128],
                       uTc[hg], uTc[hg].ap[:, t * 128:(t + 1) * 128], ALU.mult)
            if DBG:
                for hg in range(4):
                    d32 = nxt("f32", f32t)
                    cp("dve", d32, d32.ap[:, 0:ntok], zT[hg], zT[hg].ap[:, 0:ntok])
                    c0 = (lt0 - NTC) * 128
                    out_events.append(S.dma("sp", dbg_zT[hg * 128:(hg + 1) * 128, c0:c0 + ntok], d32.ap[:, 0:ntok], reads=[d32]))
            for t in range(nt):
                attention_tile(t, lt0 + t)
            merge_tail(nt)

        def merge_tail(nt):
            ntok = nt * 128
            for c in range(8):
                cc = c % 2
                wga = wload(s_winu[U_GA + c // 2, :, :, cc * 128:(cc + 1) * 128], [sc[("win", U_GA + c // 2)]], [128, 8, 128])
                wgb = wload(s_winu[U_GB + c // 2, :, :, cc * 128:(cc + 1) * 128], [sc[("win", U_GB + c // 2)]], [128, 8, 128])
                wa_ = wload(s_wa[c], [sc[("wa", c)]], [128, 4, 128])
                wb_ = wload(s_wb[c], [sc[("wb", c)]], [128, 4, 128])
                bA, bG, bB, bH = banks[0], banks[1], banks[2], banks[3]
                for kc in range(8):
                    mm(bG, bG.ap[:, 0:ntok], wga[0], wga[1][:, kc, :], xT, xT.ap[:, kc, 0:ntok], kc == 0, kc == 7)
                for kc in range(4):
                    mm(bA, bA.ap[:, 0:ntok], wa_[0], wa_[1][:, kc, :], oaT[kc], oaT[kc].ap[:, 0:ntok], kc == 0, kc == 3)
                for kc in range(8):
                    mm(bH, bH.ap[:, 0:ntok], wgb[0], wgb[1][:, kc, :], xT, xT.ap[:, kc, 0:ntok], kc == 0, kc == 7)
                for kc in range(4):
                    mm(bB, bB.ap[:, 0:ntok], wb_[0], wb_[1][:, kc, :], zT[kc], zT[kc].ap[:, 0:ntok], kc == 0, kc == 3)
                sa = nxt("silu", silu_t)
                act(sa, sa.ap[:, 0:ntok], bG, bG.ap[:, 0:ntok], AF.Sigmoid)
                m1 = nxt("f32", f32t)
                tt("dve", m1, m1.ap[:, 0:ntok], bA, bA.ap[:, 0:ntok], sa, sa.ap[:, 0:ntok], ALU.mult)
                sb_ = nxt("silu", silu_t)
                act(sb_, sb_.ap[:, 0:ntok], bH, bH.ap[:, 0:ntok], AF.Sigmoid)
                tt("dve", sb_, sb_.ap[:, 0:ntok], bB, bB.ap[:, 0:ntok], sb_, sb_.ap[:, 0:ntok], ALU.mult)
                tt("dve", mT[c], mT[c].ap[:, 0:ntok], m1, m1.ap[:, 0:ntok], sb_, sb_.ap[:, 0:ntok], ALU.add)
            for n in range(2):
                for kc in range(8):
                    slot, w = wload(s_wo[kc * 128:(kc + 1) * 128, n * 512:(n + 1) * 512], [sc[("wo", kc)]], [128, 512])
                    for t in range(nt):
                        bk = banks[4 + t]
                        mm(bk, bk.ap[:], mT[kc], mT[kc].ap[:, t * 128:(t + 1) * 128], slot, w, kc == 0, kc == 7)
                for t in range(nt):
                    bk = banks[4 + t]
                    stt(xres[t], xres[t].ap[:, n * 512:(n + 1) * 512], xres[t], xres[t].ap[:, n * 512:(n + 1) * 512],
                        ALPHA, bk, bk.ap[:], ALU.mult, ALU.add)
            for t in range(nt):
                layer_norm_tile(xres[t], 2, LN_EPS)

        SM = {}

        def sample_phase():
            from concourse.bass import IndirectOffsetOnAxis
            past = NPG * 128
            S.barrier()
            ALIAS = NTL >= 64 or bool(_os0.environ.get("KALIAS"))
            pools_ = {"v": [Vsel.ap[:].rearrange("p a b c -> p (a b c)"), 0, NTL * 2 * 66],
                      "n": [vnb.ap[:].rearrange("p a b -> p (a b)"), 0, 2 * 512],
                      "k": [Ksel.ap[:, 0, :], 0, NPOS], "k1": [Ksel.ap[:, 1, :], 0, NPOS]}

            def carve(name, shape, dt, where):
                if not ALIAS:
                    return sb(name, shape, dt)
                esz = 2 if dt == BF16 else 4
                n = 1
                for s_ in shape[1:]:
                    n *= s_
                nb16 = (n * esz + 3) // 4 * 2
                for key in where:
                    flat, off, cap = pools_[key]
                    if off + nb16 <= cap:
                        pools_[key][1] = off + nb16
                        v = flat[0:shape[0], off:off + nb16]
                        if dt != BF16:
                            v = v.bitcast(dt)
                        v = v[:, 0:n]
                        if len(shape) == 3:
                            v = v.rearrange("p (a b) -> p a b", a=shape[1])
                        elif len(shape) == 4:
                            v = v.rearrange("p (a b c) -> p a b c", a=shape[1], b=shape[2])
                        return Buf(name, v)
                return sb(name, shape, dt)

            qS = carve("qS", [69, 8, 128], BF16, ["k"])
            gS = sb("gS", [128, 1, 24], F32)
            Kcs = carve("Kcs", [69, 2, NCTS * 128], BF16, ["k"])
            Vcs = carve("Vcs", [128, NCTS, 2, 66], BF16, ["v", "n"])
            pg = [carve("pg%d" % i, [128, 2, 128], F32, ["v", "n"]) for i in range(4)]
            pgb = [carve("pgb%d" % i, [128, 256], BF16, ["v", "n"]) for i in range(4)]
            ptb = sb("ptb", [128, NPG], I32)
            idx = sb("idx", [128, NPG], I32)
            piota = sb("piota", [128, 1], F32)
            Kr = [carve("Kr%d" % i, [69, 2, 128], BF16, ["k"]) for i in range(6)]
            Vr = [carve("Vr%d" % i, [128, 2, 66], BF16, ["v", "n"]) for i in range(6)]
            augt = carve("augt", [69, 3, 128], BF16, ["k"])
            S.dma("sp", augt.ap[64:69, 0, :], scd["c_augt"][:, 0, :], writes=[augt])
            S.dma("sp", augt.ap[64:69, 1, :], scd["c_augt"][:, 1, :], writes=[augt])
            S.dma("sp", augt.ap[64:69, 2, :], scd["c_augt"][:, 2, :], writes=[augt])
            SM["augt"] = augt
            w1res = []
            for kv in range(2):
                wb_ = Buf("w1res%d" % kv, xres[1 + kv].ap[:].bitcast(BF16).rearrange("p (c h) -> p c h", c=16))
                S.dma("sp", wb_.ap[:], s_w1[kv].rearrange("(c p) h -> p c h", p=128), reads=[sc["cmpw"]], writes=[wb_])
                w1res.append(wb_)
            SM["w1res"] = w1res
            wk32 = [carve("wk32", [128, 4, 128], F32, ["v", "n"]), carve("wv32", [128, 4, 128], F32, ["v", "n"])]
            pscs = carve("pscs", [128, 264], F32, ["v", "n"])
            pscs2 = carve("pscs2", [128, 264], F32, ["v", "n"])
            nsbs = carve("nsbs", [128, 384], BF16, ["v", "n"])
            nsTs = carve("nsTs", [128, 3, 128], BF16, ["v", "n"])
            SM["nsT2"] = carve("nsT2", [128, 3, 128], BF16, ["v", "n"])
            smb = carve("smb", [128, 264], F32, ["v", "n"])
            sbb = carve("sbb", [128, 264], F32, ["v", "n"])
            w0 = sb("w0", [128, 33], BF16)
            oabS = oab
            rr.update({"pg": 0, "pgb": 0, "Kr": 0, "Vr": 0})
            S.dma("sp", piota.ap[:], scd["c_piota"], writes=[piota])
            S.dma("sp", smb.ap[:], scd["c_smb"], writes=[smb])
            S.dma("sp", sbb.ap[:], scd["c_sbb"], writes=[sbb])
            S.dma("sp", w0.ap[:], scd["c_w0"], writes=[w0])
            S.op("dve", lambda e: e.memset(Kcs.ap[:], 0.0), writes=[Kcs])
            S.op("dve", lambda e: e.memset(Vcs.ap[:], 1.0), writes=[Vcs])
            S.op("dve", lambda e: e.memset(qS.ap[:], 0.0), writes=[qS])
            S.op("dve", lambda e: e.memset(gS.ap[:], 0.0), writes=[gS])
            S.op("dve", lambda e: e.memset(oabS.ap[:], 0.0), writes=[oabS])
            S.op("dve", lambda e: e.memset(nsbs.ap[:], NEGBIG), writes=[nsbs])
            for v_ in Vr:
                S.op("dve", lambda e, v_=v_: e.memset(v_.ap[:], 1.0), writes=[v_])
            for g in range(2):
                S.dma("sp", Kcs.ap[64:69, g, :], scd["c_skcaug"], writes=[Kcs])
            for h in range(8):
                S.dma("sp", qS.ap[64:69, h, :], scd["c_sqaug"][h], writes=[qS])
            wb4 = sb("wb4", [128, 8], F32)
            for hg in range(4):
                S.dma("sp", wb4.ap[:, hg:hg + 1], wd["spatial_w"][hg, 0, 0:1].partition_broadcast(128), writes=[wb4])
                S.dma("sp", wb4.ap[:, 4 + hg:5 + hg], wd["spatial_b"][hg, 0:1].partition_broadcast(128), writes=[wb4])

            xs = xres[0]
            S.dma("sp", xs.ap[:], x_smp, writes=[xs])
            transpose_tile(xs, 0)
            ffn_block(0, 1, 0)
            transpose_tile(xs, 0)
            wl = [wload(s_winu[u], [sc[("win", u)]], [128, 8, 256]) for u in (U_KV, U_KV + 1, U_KV + 2)]
            bkA, bkB = banks[4], banks[5]
            for ui, (slot, w) in enumerate(wl):
                bk, o0 = (bkA, ui * 256) if ui < 2 else (bkB, 0)
                for kc in range(8):
                    mm(bk, bk.ap[:, o0:o0 + 256], xT, xT.ap[:, kc, 0:128], slot, w[:, kc, :], kc == 0, kc == 7)
            kvst = f32t[0]
            cp("act", kvst, kvst.ap[:, 0:512], bkA, bkA.ap[:])
            cp("act", kvst, kvst.ap[:, 512:768], bkB, bkB.ap[:, 0:256])
            out_events.append(S.dma("act", kv_s[:, :], kvst.ap[:, 0:768], reads=[kvst]))
            for hq in range(2):
                slot, w = wload(s_winu[U_Q + hq], [sc[("win", U_Q + hq)]], [128, 8, 256])
                for hh in range(4):
                    h = 4 * hq + hh
                    bk = banks[hh % 4]
                    for kc in range(8):
                        mm(bk, bk.ap[0:64, 0:128], slot, w[:, kc, hh * 64:(hh + 1) * 64], xT, xT.ap[:, kc, 0:128], kc == 0, kc == 7)
                    S.op("act", lambda e, bk=bk, h=h: e.mul(qT.ap[0:64, h, 0:128], bk.ap[0:64, 0:128], 0.125), reads=[bk], writes=[qT])
            slot, w = wload(s_winu[U_G, :, :, 0:24], [sc[("win", U_G)]], [128, 8, 24])
            bk = banks[6]
            for kc in range(8):
                mm(bk, bk.ap[:, 0:24], xT, xT.ap[:, kc, 0:128], slot, w[:, kc, :], kc == 0, kc == 7)
            act(gts, gts.ap[:, 0, :], bk, bk.ap[:, 0:24], AF.Sigmoid)
            uv = []
            for which, U0 in ((0, U_U), (1, U_V)):
                wv = [wload(s_winu[U0 + i], [sc[("win", U0 + i)]], [128, 8, 256]) for i in range(2)]
                bk = banks[4 + which]
                for i in range(2):
                    for kc in range(8):
                        mm(bk, bk.ap[:, i * 256:(i + 1) * 256], xT, xT.ap[:, kc, 0:128], wv[i][0], wv[i][1][:, kc, :], kc == 0, kc == 7)
                dst = f32t[1] if which == 0 else kvst
                if which == 1:
                    pass
                uv.append(dst)
                if which == 0:
                    act(dst, dst.ap[:, 0:512], bk, bk.ap[:], AF.Gelu_apprx_tanh)
            vf = wk32[0]
            vfa = vf.ap[:].rearrange("p a b -> p (a b)")
            act(vf, vfa, banks[5], banks[5].ap[:], AF.Gelu_apprx_tanh)
            vn32 = wk32[1]
            vna = vn32.ap[:].rearrange("p a b -> p (a b)")
            ln_small(vf, vfa, 512, glnp.ap[:, 0, :], glnp.ap[:, 1, :], vn32, vna, LN_EPS)
            out_events.append(S.dma("act", chunkv[:, :], vna, reads=[vn32]))
            for hg in range(4):
                ts("dve", vn32, vna[:, hg * 128:(hg + 1) * 128], vn32, vna[:, hg * 128:(hg + 1) * 128],
                   wb4.ap[:, hg:hg + 1], wb4.ap[:, 4 + hg:5 + hg], ALU.mult, ALU.add, extra_r=[wb4])
            zs = nxt("pT", pTb)
            tt("dve", zs, zs.ap[:], vn32, vna, f32t[1], f32t[1].ap[:, 0:512], ALU.mult)
            for kc in range(4):
                tr(trb, trb_bf[:, kc * 128:(kc + 1) * 128], zs, zs.ap[:, kc * 128:(kc + 1) * 128], identb)
            for kc in range(4):
                cp("act" if kc % 2 else "dve", zT[kc], zT[kc].ap[:, 0:128], trb, trb_bf[:, kc * 128:(kc + 1) * 128])

            for i in range(4):
                S.dma("sp", ptb.ap[:], ptab[i].partition_broadcast(128), writes=[ptb])
                ts("dve", idx, idx.ap[:], ptb, ptb.ap[:], 128.0, piota.ap[:, 0:1], ALU.mult, ALU.add, extra_r=[piota])

                def gather(dst, dap, pool_i, s_):
                    return S.dma("pool", None, None, reads=[idx], writes=[dst],
                                 fn=lambda e: e.indirect_dma_start(out=dap, out_offset=None, in_=pools[pool_i],
                                                                   in_offset=IndirectOffsetOnAxis(ap=idx.ap[:, s_:s_ + 1], axis=0)))

                for kvi in range(2):
                    wt = wk32[kvi]
                    S.dma("sp", wt.ap[:, 0:3, :], cwin[kvi][i, 1:385, :].rearrange("(t p) c -> p t c", p=128), writes=[wt])
                    S.dma("sp", wt.ap[0:127, 3, :], cwin[kvi][i, 385:512, :], writes=[wt])
                    c0 = 512 + kvi * 128
                    S.dma("sp", wt.ap[127:128, 3, :], kvst.ap[i:i + 1, c0:c0 + 128], reads=[kvst], writes=[wt])
                    out_events.append(S.dma("act", swin[kvi][i].rearrange("(t p) c -> p t c", p=128), wt.ap[:], reads=[wt]))
                cp("act", qS, qS.ap[0:64, :, 0:1], qT, qT.ap[0:64, :, i:i + 1])
                S.dma("sp", gS.ap[0:1, 0, :], gts.ap[i:i + 1, 0, :], reads=[gts], writes=[gS])
                S.op("dve", lambda e: e.memset(XT2.ap[:], 0.0), writes=[XT2])
                for s0 in range(0, NPG, 4):
                    npg = min(4, NPG - s0)
                    for t in range(npg):
                        p_ = nxt("pg", pg)
                        gather(p_, p_.ap[:].rearrange("p a b -> p (a b)"), 0, s0 + t)
                        pb_ = nxt("pgb", pgb)
                        cp("dve", pb_, pb_.ap[:], p_, p_.ap[:].rearrange("p a b -> p (a b)"))
                        xb_ = banks[(2 * t) % 4]
                        for kvg in range(4):
                            for par in range(2):
                                mm(xb_, xb_.ap[par * 64:(par + 1) * 64, kvg * 64:(kvg + 1) * 64], pb_, pb_.ap[:, kvg * 64:(kvg + 1) * 64],
                                   selpair, selpair.ap[:, par, :], True, True)
                        cp("dve", XT2, XT2.ap[:, :, 8 + t * 64:8 + (t + 1) * 64], xb_, xb_.ap[:, 0:256].rearrange("p (a m) -> p a m", a=4))
                    compress_block(npg, s0, Kcs, Vcs, SM["w1res"])
                sample_attention(i, qS, gS, Kcs, Vcs, Kr, Vr, wk32, pg, pgb, pscs, pscs2, nsbs, nsTs, smb, sbb, w0, oabS, gather, kvst)
            for kc in range(4):
                tr(trb, trb_bf[:, kc * 128:(kc + 1) * 128], oabS, oabS.ap[:, kc * 128:(kc + 1) * 128], identb)
            for kc in range(4):
                cp("act" if kc % 2 else "dve", oaT[kc], oaT[kc].ap[:, 0:128], trb, trb_bf[:, kc * 128:(kc + 1) * 128])
            merge_tail(1)
            transpose_tile(xs, 0)
            ffn_block(1, 1, 4)
            out_events.append(S.dma("act", y_s[:, :], xs.ap[:], reads=[xs]))

        def sample_attention(i, qS, gS, Kcs, Vcs, Kr, Vr, wk32, pg, pgb, pscs, pscs2, nsbs, nsTs, smb, sbb, w0, oabS, gather, kvst):
            past = NPG * 128
            q0 = past
            augt = SM["augt"]
            nsT2 = SM["nsT2"]
            Oc = banks[2]
            pslb = [banks[5], banks[6], banks[3], banks[4]]
            Ob = [banks[2], banks[3]]
            NJ_ = NJS

            def pv_(O_, first, P, Vb, vap):
                for r in range(4):
                    mm(O_, O_.ap[:, r * 66:r * 66 + 65], P, P.ap[:, r * 128:(r + 1) * 128], Vb, vap, first and r == 0, False, skip=True)

            def fin(g, O_, br, last, orow=None):
                R = CH[g]
                rden_, coef_, acc, tmp = R["rden"], R["coef"], R["acc"], R["tmp"]
                gv = gS.ap[:, 0, g * 12:(g + 1) * 12].rearrange("p (r b) -> p r b", b=3)
                ov = O_.ap[:, 0:264].rearrange("p (r e) -> p r e", r=4)[:, :, 0:64]
                dv = O_.ap[:, 0:264].rearrange("p (r e) -> p r e", r=4)[:, :, 64]
                ts("dve", rden_, rden_.ap[:, br, :], O_, dv, 1e-30, None, ALU.max)
                S.op("dve", lambda e: e.reciprocal(out=rden_.ap[:, br, :], in_=rden_.ap[:, br, :]), reads=[rden_], writes=[rden_])
                tt("dve", coef_, coef_.ap[:, br, :], rden_, rden_.ap[:, br, :], gS, gv[:, :, br], ALU.mult)
                cb = coef_.ap[:, br, :].unsqueeze(2).broadcast_to([128, 4, 64])
                if br == 0:
                    tt("dve", acc, acc.ap[:], O_, ov, coef_, cb, ALU.mult)
                else:
                    tt("dve", tmp, tmp.ap[:], O_, ov, coef_, cb, ALU.mult)
                    if not last:
                        tt("dve", acc, acc.ap[:], acc, acc.ap[:], tmp, tmp.ap[:], ALU.add)
                    else:
                        tt("dve", orow, orow.ap[:, 0:256].rearrange("p (r d) -> p r d", r=4), acc, acc.ap[:], tmp, tmp.ap[:], ALU.add)

            for g in range(2):
                qrhs = qS.ap[:, 4 * g:4 * g + 4, :]
                rden_ = CH[g]["rden"]
                for ct in range(NCTS):
                    Sb = sbank()
                    mm(Sb, Sb.ap[:].rearrange("p (r q) -> p r q", r=4), Kcs, Kcs.ap[:, g, ct * 128:(ct + 1) * 128], qS, qrhs, True, True)
                    P = nxt("pT", pTb)
                    bnd = q0 - 2048 * ct - 2032 - 31 < 0
                    if bnd:
                        ts("dve", Sb, Sb.ap[:], Sb, Sb.ap[:], 30.0, None, ALU.min)
                    act(P, P.ap[:], Sb, Sb.ap[:], AF.Exp)
                    if bnd:
                        pool_mask(P, q0 - 2048 * ct - 31, -16, 1, ALU.is_ge)
                    pv_(Oc, ct == 0, P, Vcs, Vcs.ap[:, ct, g, 0:65])
                    for r in range(4):
                        mm(pslb[r], pslb[r].ap[:, 32 * ct:32 * ct + 33], P, P.ap[:, r * 128:(r + 1) * 128], w0, w0.ap[:], ct == 0, False, skip=True)
                fin(g, Oc, 0, False)
                ts("dve", pscs, pscs.ap[:, 0:NJ_], pslb[0], pslb[0].ap[:, 0:NJ_], rden_.ap[:, 0, 0:1], None, ALU.mult, extra_r=[rden_])
                for r in range(1, 4):
                    stt(pscs, pscs.ap[:, 0:NJ_], pslb[r], pslb[r].ap[:, 0:NJ_], rden_.ap[:, 0, r:r + 1], pscs, pscs.ap[:, 0:NJ_],
                        ALU.mult, ALU.add, extra_r=[rden_])
                tt("dve", pscs, pscs.ap[:, 0:NJ_], pscs, pscs.ap[:, 0:NJ_], smb, smb.ap[:, 0:NJ_], ALU.mult)
                tt("dve", pscs, pscs.ap[:, 0:NJ_], pscs, pscs.ap[:, 0:NJ_], sbb, sbb.ap[:, 0:NJ_], ALU.add)
                S.op("dve", lambda e: e.max(out=m8a.ap[:], in_=pscs.ap[:, 0:NJ_]), reads=[pscs], writes=[m8a])
                S.op("dve", lambda e: e.match_replace(out=pscs2.ap[:, 0:NJ_], in_to_replace=m8a.ap[:], in_values=pscs.ap[:, 0:NJ_], imm_value=-3.0e38),
                     reads=[pscs, m8a], writes=[pscs2])
                S.op("dve", lambda e: e.max(out=m8b.ap[:], in_=pscs2.ap[:, 0:NJ_]), reads=[pscs2], writes=[m8b])
                ts("dve", nsbs, nsbs.ap[:, 0:NJ_], pscs, pscs.ap[:, 0:NJ_], m8b.ap[:, 7:8], NEGBIG, ALU.is_lt, ALU.mult, extra_r=[m8b])
                for ch in range(3):
                    tr(trb, trb_bf[:, ch * 128:(ch + 1) * 128], nsbs, nsbs.ap[:, ch * 128:(ch + 1) * 128], identb)
                nst = nsTs if g == 0 else nsT2
                cp("act", nst, nst.ap[:].rearrange("p a b -> p (a b)"), trb, trb_bf[:, 0:384])
            nsts = [nsTs, nsT2]

            def ktile(K_, V_, first, kt_val, templ, Tcols, diag, masked):
                stt(K_, K_.ap[64:69, :, :], augt, augt.ap[64:69, 1, :].unsqueeze(1).broadcast_to([5, 2, 128]), float(kt_val),
                    augt, augt.ap[64:69, templ, :].unsqueeze(1).broadcast_to([5, 2, 128]), ALU.mult, ALU.add)
                for g in range(2):
                    qrhs = qS.ap[:, 4 * g:4 * g + 4, :]
                    Sb = nxt("S", [banks[0], banks[1], banks[6]])
                    mm(Sb, Sb.ap[:].rearrange("p (r q) -> p r q", r=4), K_, K_.ap[:, g, :], qS, qrhs, True, not masked)
                    if masked:
                        ch, tc = Tcols
                        for r in range(4):
                            mm(Sb, Sb.ap[:, r * 128:(r + 1) * 128], Tsel, Tsel.ap[:, tc:tc + 128], nsts[g], nsts[g].ap[:, ch, :], False, r == 3)
                    P = nxt("pT", pTb)
                    act(P, P.ap[:], Sb, Sb.ap[:], AF.Exp)
                    if diag:
                        pool_mask(P, 0, -1, 1, ALU.is_ge)
                    pv_(Ob[g], first, P, V_, V_.ap[:, g, 0:65])

            for wtile in range(4):
                kb = nxt("pgb", pgb)
                cp("dve", kb, kb.ap[:, 0:128], wk32[0], wk32[0].ap[:, wtile, :])
                for g in range(2):
                    tr(trb, trb_bf[0:64, 512 + g * 128:640 + g * 128], kb, kb.ap[:, g * 64:(g + 1) * 64], identb)
                K_ = nxt("Kr", Kr)
                cp("act", K_, K_.ap[0:64, :, :], trb, trb_bf[0:64, 512:768].rearrange("p (g n) -> p g n", g=2))
                V_ = nxt("Vr", Vr)
                cp("act", V_, V_.ap[:, :, 0:64], wk32[1], wk32[1].ap[:, wtile, :].rearrange("p (g d) -> p g d", g=2))
                ktile(K_, V_, wtile == 0, (past - 512) // 128 + wtile, 2, None, False, False)
            for g in range(2):
                fin(g, Ob[g], 2, False)
            for kt in range(NPG + 1):
                K_ = nxt("Kr", Kr)
                V_ = nxt("Vr", Vr)
                kb = nxt("pgb", pgb)
                if kt < NPG:
                    p_ = nxt("pg", pg)
                    gather(p_, p_.ap[:].rearrange("p a b -> p (a b)"), 1, kt)
                    cp("dve", kb, kb.ap[:, 0:128], p_, p_.ap[:, 0, :])
                    cp("act", V_, V_.ap[:, :, 0:64], p_, p_.ap[:, 1, :].rearrange("p (g d) -> p g d", g=2))
                else:
                    S.op("dve", lambda e, kb=kb: e.memset(kb.ap[:, 0:128], 0.0), writes=[kb])
                    S.op("dve", lambda e, V_=V_: e.memset(V_.ap[:, :, 0:64], 0.0), writes=[V_])
                    nb16 = nxt("pT", pTb)
                    cp("dve", nb16, nb16.ap[:, 0:256], kvst, kvst.ap[:, 256:512])
                    S.dma("sp", kb.ap[0:1, 0:128], nb16.ap[i:i + 1, 0:128], reads=[nb16], writes=[kb])
                    S.dma("sp", V_.ap[0:1, :, 0:64], nb16.ap[i:i + 1, 128:256].rearrange("p (g d) -> p g d", g=2), reads=[nb16], writes=[V_])
                for g in range(2):
                    tr(trb, trb_bf[0:64, 512 + g * 128:640 + g * 128], kb, kb.ap[:, g * 64:(g + 1) * 64], identb)
                cp("act", K_, K_.ap[0:64, :, :], trb, trb_bf[0:64, 512:768].rearrange("p (g n) -> p g n", g=2))
                ktile(K_, V_, kt == 0, kt, 0, ((2 * kt) // 128, (kt % 64) * 128), kt == NPG, True)
            for g in range(2):
                orow = nxt("pT", pTb)
                fin(g, Ob[g], 1, True, orow)
                S.dma("sp", oabS.ap[i:i + 1, g * 256:(g + 1) * 256], orow.ap[0:1, 0:256], reads=[orow], writes=[oabS])

        import os as _os
        CUT = int(_os.environ.get("KCUT", "99"))
        KSUB = int(_os.environ.get("KSUB", "255"))

        def load_block(src, row0, nt):
            for t in range(nt):
                S.dma("sp", xres[t].ap[:], src[row0 + t * 128:row0 + (t + 1) * 128, :], writes=[xres[t]])
                transpose_tile(xres[t], t)

        for b in range(NBC if CUT >= 1 else 0):
            load_block(x_ctx, b * 512, 4)
            if CUT >= 2:
                ffn_block(0, 4, 0)
            for t in range(4):
                transpose_tile(xres[t], t)
            if CUT >= 3:
                kv_project(4, b * 4, False, 0, b == NBC - 1)
            if CUT >= 4:
                compress_block(4, b * 4)

        for b in range(NBO):
            load_block(x_own, b * 512, 4)
            if CUT >= 2:
                ffn_block(0, 4, 0)
            for t in range(4):
                transpose_tile(xres[t], t)
            if CUT >= 3:
                kv_project(4, NTC + b * 4, True, b * 512, True)
            if CUT >= 4:
                compress_block(4, NTC + b * 4)
            if stage >= 2:
                mixer_block(4, NTC + b * 4)
            if stage >= 3:
                for t in range(4):
                    transpose_tile(xres[t], t)
                ffn_block(1, 4, 4)
            for t in range(4):
                out_events.append(S.dma("act", y_own[b * 512 + t * 128:b * 512 + (t + 1) * 128, :], xres[t].ap[:], reads=[xres[t]]))

        if NPG:
            sample_phase()
        S.finish(out_events)
        S.emit()
    return nc


def sample_consts(NPG):
    import ml_dtypes
    bf = ml_dtypes.bfloat16
    past = NPG * 128
    c = {}
    c["c_piota"] = np.arange(128, dtype=np.float32).reshape(128, 1)
    npos = (NPG + 1) * 128
    pos = np.arange(npos)
    ka = np.zeros((5, npos), np.float32)
    ka[0] = pos % 128
    ka[1] = 1.0
    ka[2] = pos // 128
    ka[3] = 1.0
    c["c_skaug"] = ka.astype(bf)
    NCS = NPG * 8
    NCTS = (NCS + 127) // 128
    ci = np.arange(NCTS * 128)
    kc = np.zeros((5, NCTS * 128), np.float32)
    kc[0] = 16 * (ci % 128)
    kc[1] = 1.0
    kc[2] = 16 * (ci // 128)
    kc[3] = 1.0
    kc[4] = 31.0
    c["c_skcaug"] = kc.astype(bf)
    w = np.arange(512)
    kw = np.zeros((5, 512), np.float32)
    kw[0] = (w % 128) + 1
    kw[1] = 1.0
    kw[2] = (past - 512) // 128 + w // 128
    kw[3] = 1.0
    c["c_swaug"] = kw.astype(bf)
    qa = np.zeros((8, 5, 128), np.float32)
    qr = np.arange(128)
    for h in range(8):
        s = SLOPES[h]
        qa[h, 0] = s
        qa[h, 1] = -s * qr
        qa[h, 2] = s * 128
        qa[h, 3] = -s * 128 * NPG
        qa[h, 4] = s
    c["c_sqaug"] = qa.astype(bf)
    at = np.zeros((5, 3, 128), np.float32)
    at[0, 0] = np.arange(128)
    at[1, 0] = 1.0
    at[3, 0] = 1.0
    at[2, 1] = 1.0
    at[0, 2] = np.arange(128) + 1
    at[1, 2] = 1.0
    at[3, 2] = 1.0
    c["c_augt"] = at.astype(bf)
    W0 = np.zeros((128, 33), np.float32)
    for ir in range(128):
        for jj in range(33):
            if 4 * jj - 1 <= ir <= 4 * jj + 3:
                W0[ir, jj] = 1.0
    c["c_w0"] = W0.astype(bf)
    NJS = 2 * NPG + 1
    smb = np.zeros((128, 264), np.float32)
    sbb = np.zeros((128, 264), np.float32)
    smb[:, :NJS] = 1.0
    for j in (0, NJS - 1, NJS - 2):
        smb[:, j] = 0.0
        sbb[:, j] = 1.0e4
    c["c_smb"] = smb
    c["c_sbb"] = sbb
    return c


_CONST_CACHE = {}


def _consts(ntl, fv):
    key = (ntl, fv)
    if key not in _CONST_CACHE:
        _CONST_CACHE[key] = host_consts(ntl, fv, 8192)
    return _CONST_CACHE[key]


def kernel(**inp):
    xp = np.asarray(inp["x_prompt"], dtype=np.float32)
    xs = np.asarray(inp["x_sample"], dtype=np.float32)
    B, T, _ = xp.shape
    DB = xs.shape[0]
    half_t = T // 2
    nb = half_t // 512
    ptab = np.asarray(inp["page_table"]).astype(np.int32)
    npg = ptab.shape[1]
    pool_names = ("cache_cmp_k", "cache_cmp_v", "cache_sel_k", "cache_sel_v")
    nphys = np.asarray(inp[pool_names[0]]).shape[0]
    pr = [np.asarray(inp[n], dtype=np.float32).reshape(nphys * 128, 128) for n in pool_names]
    pools = [np.concatenate([pr[0], pr[1]], axis=1), np.concatenate([pr[2], pr[3]], axis=1)]
    cwk = np.asarray(inp["cache_win_k"], dtype=np.float32).reshape(DB, 512, 128)
    cwv = np.asarray(inp["cache_win_v"], dtype=np.float32).reshape(DB, 512, 128)
    nc = build(nb, nb, 3, NPG=npg, NPHYS=nphys)
    weights = {n: np.ascontiguousarray(np.asarray(inp[n], dtype=np.float32)) for n in W_NAMES}
    sconst = sample_consts(npg)
    in_maps = []
    for c in range(8):
        b, half = divmod(c, 2)
        m = dict(weights)
        m.update(_consts(2 * nb * 4, nb * 4 if half == 0 else 0))
        m.update(sconst)
        m["x_ctx"] = np.ascontiguousarray(xp[b, 0:half_t])
        m["x_own"] = np.ascontiguousarray(xp[b, half * half_t:(half + 1) * half_t])
        xsm = np.zeros((128, D), np.float32)
        xsm[0:4] = xs[4 * c:4 * c + 4, 0]
        m["x_smp"] = xsm
        for i in range(2):
            m["pool%d" % i] = pools[i]
        m["cwin_k"] = np.ascontiguousarray(cwk[4 * c:4 * c + 4])
        m["cwin_v"] = np.ascontiguousarray(cwv[4 * c:4 * c + 4])
        m["ptab"] = np.ascontiguousarray(ptab[4 * c:4 * c + 4])
        in_maps.append(m)
    res = run_bass_kernel_spmd(nc, in_maps, core_ids=list(range(8))).results
    y_prompt = np.stack([np.concatenate([res[2 * b]["y_own"], res[2 * b + 1]["y_own"]], axis=0) for b in range(B)])
    kv = np.stack([np.concatenate([res[2 * b]["kv_own"], res[2 * b + 1]["kv_own"]], axis=0) for b in range(B)])
    parts = [np.ascontiguousarray(kv[:, :, i * 128:(i + 1) * 128]).reshape(B, T, 2, 64) for i in range(6)]
    p_cmp_k, p_cmp_v, p_sel_k, p_sel_v, kw, vw = parts
    p_win_k = np.ascontiguousarray(kw[:, -512:])
    p_win_v = np.ascontiguousarray(vw[:, -512:])
    y_sample = np.concatenate([res[c]["y_s"][0:4] for c in range(8)], axis=0).reshape(DB, 1, D)
    kvs = np.concatenate([res[c]["kv_s"][0:4] for c in range(8)], axis=0)
    sp = [np.ascontiguousarray(kvs[:, i * 128:(i + 1) * 128]).reshape(DB, 1, 2, 64) for i in range(4)]
    s_win_k = np.concatenate([res[c]["swin_k"] for c in range(8)], axis=0).reshape(DB, 512, 2, 64)
    s_win_v = np.concatenate([res[c]["swin_v"] for c in range(8)], axis=0).reshape(DB, 512, 2, 64)
    s_chunk_v = np.concatenate([res[c]["chunkv"][0:4] for c in range(8)], axis=0).reshape(DB, 1, 512)
    f = lambda a: np.ascontiguousarray(a, dtype=np.float32)
    return (f(y_prompt), f(y_sample), p_cmp_k, p_cmp_v, p_sel_k, p_sel_v, p_win_k, p_win_v,
            sp[0], sp[1], sp[2], sp[3], f(s_win_k), f(s_win_v), f(s_chunk_v))
```
